# Optimizing a Trainium2 kernel written in Bass

```python
import math
import jax, jax.numpy as jnp
from jax import lax
import numpy as np

D_MODEL = 1024
BATCH = 8
SEQ = 2048
DEPTH = 4

CHUNK = 64
Q_BLOCK = 128
N_EVEN = (DEPTH + 1) // 2
N_ODD = DEPTH // 2

POOL_WINDOWS = (2, 4, 8, 16)
POOL_GROUPS = len(POOL_WINDOWS)
POOL_GROUP_WIDTH = D_MODEL // 8
POOL_WIDTH = POOL_GROUPS * POOL_GROUP_WIDTH
LRU_WIDTH = D_MODEL
LRU_HEADS = 8
LRU_HEAD_DIM = LRU_WIDTH // LRU_HEADS
CONV_WIDTH = 4
LRU_C = 8.0
EVEN_IN_WIDTH = POOL_WIDTH + 2 * LRU_WIDTH
EVEN_MIX_WIDTH = POOL_WIDTH + LRU_WIDTH
MLA_HEADS = 8
QK_NOPE_DIM = 128
QK_ROPE_DIM = 64
V_HEAD_DIM = 128
Q_LORA_RANK = 384
KV_LORA_RANK = 256
ODD_IN_WIDTH = Q_LORA_RANK + KV_LORA_RANK + QK_ROPE_DIM
ROPE_THETA = 10000.0
D_FF = 4 * D_MODEL
DEEPNORM_ALPHA = (2 * DEPTH) ** 0.25
DEEPNORM_BETA = (8 * DEPTH) ** -0.25
LN_EPS = 1e-5
RMS_EPS = 1e-6

kernel_name = "pool_rglru_mla_deepnorm_hybrid"


def layer_norm(x, g, b):
    xf = x.astype(jnp.float32)
    mu = jnp.mean(xf, axis=-1, keepdims=True)
    var = jnp.mean(jnp.square(xf - mu), axis=-1, keepdims=True)
    y = (xf - mu) * lax.rsqrt(var + LN_EPS) * g.astype(jnp.float32) + b.astype(jnp.float32)
    return y.astype(x.dtype)


def rms_norm(x, g):
    xf = x.astype(jnp.float32)
    y = xf * lax.rsqrt(jnp.mean(jnp.square(xf), axis=-1, keepdims=True) + RMS_EPS) * g.astype(jnp.float32)
    return y.astype(x.dtype)


def rope_tables(positions):
    inv_freq = ROPE_THETA ** (-jnp.arange(0, QK_ROPE_DIM, 2, dtype=jnp.float32) / QK_ROPE_DIM)
    ang = positions.astype(jnp.float32)[..., None] * inv_freq
    return jnp.cos(ang), jnp.sin(ang)


def apply_rope(x, cos, sin):
    xf = x.astype(jnp.float32)
    x1, x2 = jnp.split(xf, 2, axis=-1)
    return jnp.concatenate([x1 * cos - x2 * sin, x1 * sin + x2 * cos], axis=-1).astype(x.dtype)


def multiscale_pool(u, w_pool, pool_scale):
    b, s, _ = u.shape
    ug = u.astype(jnp.float32).reshape(b, s, POOL_GROUPS, POOL_GROUP_WIDTH)
    csum = jnp.cumsum(ug, axis=1)
    t = jnp.arange(s)
    diffs = []
    for g, w in enumerate(POOL_WINDOWS):
        cg = csum[:, :, g]
        lagged = jnp.pad(cg, ((0, 0), (w, 0), (0, 0)))[:, :s]
        count = jnp.minimum(t + 1, w).astype(jnp.float32)[None, :, None]
        diffs.append((cg - lagged) / count - ug[:, :, g])
    d = jnp.stack(diffs, axis=2)
    y = jnp.einsum('bsgc,gcd->bsgd', d, w_pool.astype(jnp.float32))
    y = y.reshape(b, s, POOL_WIDTH) * pool_scale.astype(jnp.float32)
    return y.astype(u.dtype)


def causal_depthwise_conv(u, w, bias):
    c = u.shape[-1]
    y = lax.conv_general_dilated(u, w[:, None, :].astype(u.dtype), window_strides=(1,),
                                 padding=[(CONV_WIDTH - 1, 0)],
                                 dimension_numbers=('NWC', 'WIO', 'NWC'),
                                 feature_group_count=c)
    return y + bias.astype(u.dtype)


def rg_lru(u, w_a, b_a, w_x, b_x, lam):
    b, s, _ = u.shape
    uf = u.astype(jnp.float32)
    uh = uf.reshape(b, s, LRU_HEADS, LRU_HEAD_DIM)
    r = jax.nn.sigmoid(jnp.einsum('bshc,hcd->bshd', uh, w_a.astype(jnp.float32)).reshape(b, s, LRU_WIDTH)
                       + b_a.astype(jnp.float32))
    i = jax.nn.sigmoid(jnp.einsum('bshc,hcd->bshd', uh, w_x.astype(jnp.float32)).reshape(b, s, LRU_WIDTH)
                       + b_x.astype(jnp.float32))
    log_a = -LRU_C * r * jax.nn.softplus(-lam.astype(jnp.float32))
    a = jnp.exp(log_a)
    mult = jnp.sqrt(-jnp.expm1(2.0 * log_a))
    xin = mult * (i * uf)

    def combine(lhs, rhs):
        a1, b1 = lhs
        a2, b2 = rhs
        return a1 * a2, a2 * b1 + b2

    _, h = lax.associative_scan(combine, (a, xin), axis=1)
    return h


def pool_lru_mixer(x, w_in, w_pool, pool_scale, conv_w, conv_b, w_a, b_a, w_x, b_x, lam, w_out):
    proj = x @ w_in
    u_pool = proj[..., :POOL_WIDTH]
    u_lru = proj[..., POOL_WIDTH:POOL_WIDTH + LRU_WIDTH]
    u_gate = proj[..., POOL_WIDTH + LRU_WIDTH:]
    y_pool = multiscale_pool(u_pool, w_pool, pool_scale)
    h = rg_lru(causal_depthwise_conv(u_lru, conv_w, conv_b), w_a, b_a, w_x, b_x, lam)
    y_lru = (h * jax.nn.gelu(u_gate.astype(jnp.float32))).astype(x.dtype)
    return jnp.concatenate([y_pool, y_lru], axis=-1) @ w_out


def mla_mixer(x, cos, sin, w_down, q_norm_g, kv_norm_g, w_qb, w_kvb, w_o):
    b, s, _ = x.shape
    down = x @ w_down
    cq = rms_norm(down[..., :Q_LORA_RANK], q_norm_g)
    ckv = rms_norm(down[..., Q_LORA_RANK:Q_LORA_RANK + KV_LORA_RANK], kv_norm_g)
    k_pe = apply_rope(down[..., Q_LORA_RANK + KV_LORA_RANK:], cos, sin)
    q = (cq @ w_qb).reshape(b, s, MLA_HEADS, QK_NOPE_DIM + QK_ROPE_DIM)
    q_nope = q[..., :QK_NOPE_DIM]
    q_pe = apply_rope(q[..., QK_NOPE_DIM:], cos[:, :, None, :], sin[:, :, None, :])
    kv = (ckv @ w_kvb).reshape(b, s, MLA_HEADS, QK_NOPE_DIM + V_HEAD_DIM)
    k_nope = kv[..., :QK_NOPE_DIM]
    v = kv[..., QK_NOPE_DIM:]
    scale = (QK_NOPE_DIM + QK_ROPE_DIM) ** -0.5
    chunk_id = jnp.arange(s) // CHUNK
    neg = jnp.finfo(jnp.float32).min
    outs = []
    for qs in range(0, s, Q_BLOCK):
        ke = qs + Q_BLOCK
        sc = (jnp.einsum('bqhd,bkhd->bhqk', q_nope[:, qs:ke], k_nope[:, :ke],
                         preferred_element_type=jnp.float32)
              + jnp.einsum('bqhr,bkr->bhqk', q_pe[:, qs:ke], k_pe[:, :ke],
                           preferred_element_type=jnp.float32)) * scale
        mask = chunk_id[:ke][None, :] <= chunk_id[qs:ke][:, None]
        p = jax.nn.softmax(jnp.where(mask, sc, neg), axis=-1).astype(v.dtype)
        outs.append(jnp.einsum('bhqk,bkhv->bqhv', p, v[:, :ke]))
    o = jnp.concatenate(outs, axis=1).reshape(b, s, MLA_HEADS * V_HEAD_DIM)
    return o @ w_o


def squared_relu_mlp(x, w1, w2):
    return jnp.square(jax.nn.relu(x @ w1)) @ w2


def setup_inputs(seed: int = 0) -> dict:
    key = jax.random.key(seed)
    ks = jax.random.split(key, 28)
    f32 = jnp.float32
    nrm = lambda k, shp, sc: jax.random.normal(k, shp, f32) * sc
    x = jax.random.normal(ks[0], (BATCH, SEQ, D_MODEL), f32)
    offset = jax.random.randint(ks[1], (BATCH, 1), 0, 4096, dtype=jnp.int32)
    positions = (offset + jnp.arange(SEQ, dtype=jnp.int32)[None, :]).astype(jnp.int32)
    ln_mix_g = 1.0 + nrm(ks[2], (DEPTH, D_MODEL), 0.1)
    ln_mix_b = nrm(ks[3], (DEPTH, D_MODEL), 0.02)
    ln_ffn_g = 1.0 + nrm(ks[4], (DEPTH, D_MODEL), 0.1)
    ln_ffn_b = nrm(ks[5], (DEPTH, D_MODEL), 0.02)
    even_w_in = nrm(ks[6], (N_EVEN, D_MODEL, EVEN_IN_WIDTH), D_MODEL ** -0.5)
    pool_w = nrm(ks[7], (N_EVEN, POOL_GROUPS, POOL_GROUP_WIDTH, POOL_GROUP_WIDTH), POOL_GROUP_WIDTH ** -0.5)
    pool_scale = 1.0 + nrm(ks[8], (N_EVEN, POOL_WIDTH), 0.1)
    lru_conv_w = nrm(ks[9], (N_EVEN, CONV_WIDTH, LRU_WIDTH), CONV_WIDTH ** -0.5)
    lru_conv_b = nrm(ks[10], (N_EVEN, LRU_WIDTH), 0.02)
    lru_w_a = nrm(ks[11], (N_EVEN, LRU_HEADS, LRU_HEAD_DIM, LRU_HEAD_DIM), LRU_HEAD_DIM ** -0.5)
    lru_b_a = nrm(ks[12], (N_EVEN, LRU_WIDTH), 0.02)
    lru_w_x = nrm(ks[13], (N_EVEN, LRU_HEADS, LRU_HEAD_DIM, LRU_HEAD_DIM), LRU_HEAD_DIM ** -0.5)
    lru_b_x = nrm(ks[14], (N_EVEN, LRU_WIDTH), 0.02)
    a_c = jax.random.uniform(ks[15], (N_EVEN, LRU_WIDTH), f32, 0.9, 0.999)
    s_a = a_c ** (1.0 / LRU_C)
    lru_lambda = jnp.log(s_a) - jnp.log1p(-s_a)
    even_w_out = nrm(ks[16], (N_EVEN, EVEN_MIX_WIDTH, D_MODEL), EVEN_MIX_WIDTH ** -0.5 * DEEPNORM_BETA)
    mla_w_down = nrm(ks[17], (N_ODD, D_MODEL, ODD_IN_WIDTH), D_MODEL ** -0.5)
    mla_q_norm_g = 1.0 + nrm(ks[18], (N_ODD, Q_LORA_RANK), 0.1)
    mla_kv_norm_g = 1.0 + nrm(ks[19], (N_ODD, KV_LORA_RANK), 0.1)
    mla_w_qb = nrm(ks[20], (N_ODD, Q_LORA_RANK, MLA_HEADS * (QK_NOPE_DIM + QK_ROPE_DIM)), Q_LORA_RANK ** -0.5)
    mla_w_kvb = nrm(ks[21], (N_ODD, KV_LORA_RANK, MLA_HEADS * (QK_NOPE_DIM + V_HEAD_DIM)), KV_LORA_RANK ** -0.5)
    mla_w_o = nrm(ks[22], (N_ODD, MLA_HEADS * V_HEAD_DIM, D_MODEL), (MLA_HEADS * V_HEAD_DIM) ** -0.5 * DEEPNORM_BETA)
    mlp_w1 = nrm(ks[23], (DEPTH, D_MODEL, D_FF), D_MODEL ** -0.5)
    mlp_w2 = nrm(ks[24], (DEPTH, D_FF, D_MODEL), D_FF ** -0.5 * DEEPNORM_BETA)
    return {"x": x, "positions": positions,
            "ln_mix_g": ln_mix_g, "ln_mix_b": ln_mix_b, "ln_ffn_g": ln_ffn_g, "ln_ffn_b": ln_ffn_b,
            "even_w_in": even_w_in, "pool_w": pool_w, "pool_scale": pool_scale,
            "lru_conv_w": lru_conv_w, "lru_conv_b": lru_conv_b,
            "lru_w_a": lru_w_a, "lru_b_a": lru_b_a, "lru_w_x": lru_w_x, "lru_b_x": lru_b_x,
            "lru_lambda": lru_lambda, "even_w_out": even_w_out,
            "mla_w_down": mla_w_down, "mla_q_norm_g": mla_q_norm_g, "mla_kv_norm_g": mla_kv_norm_g,
            "mla_w_qb": mla_w_qb, "mla_w_kvb": mla_w_kvb, "mla_w_o": mla_w_o,
            "mlp_w1": mlp_w1, "mlp_w2": mlp_w2}


def reference(x, positions, ln_mix_g, ln_mix_b, ln_ffn_g, ln_ffn_b,
              even_w_in, pool_w, pool_scale, lru_conv_w, lru_conv_b,
              lru_w_a, lru_b_a, lru_w_x, lru_b_x, lru_lambda, even_w_out,
              mla_w_down, mla_q_norm_g, mla_kv_norm_g, mla_w_qb, mla_w_kvb, mla_w_o,
              mlp_w1, mlp_w2):
    cos, sin = rope_tables(positions)
    for layer in range(DEPTH):
        j = layer // 2
        if layer % 2 == 0:
            mix = pool_lru_mixer(x, even_w_in[j], pool_w[j], pool_scale[j],
                                 lru_conv_w[j], lru_conv_b[j], lru_w_a[j], lru_b_a[j],
                                 lru_w_x[j], lru_b_x[j], lru_lambda[j], even_w_out[j])
        else:
            mix = mla_mixer(x, cos, sin, mla_w_down[j], mla_q_norm_g[j], mla_kv_norm_g[j],
                            mla_w_qb[j], mla_w_kvb[j], mla_w_o[j])
        x = layer_norm(DEEPNORM_ALPHA * x + mix, ln_mix_g[layer], ln_mix_b[layer])
        x = layer_norm(DEEPNORM_ALPHA * x + squared_relu_mlp(x, mlp_w1[layer], mlp_w2[layer]),
                       ln_ffn_g[layer], ln_ffn_b[layer])
    return x
```

```python
import math
import numpy as np
import concourse.bass as bass
import concourse.mybir as mybir
from concourse.bass_utils import run_bass_kernel_spmd

F32 = mybir.dt.float32
BF16 = mybir.dt.bfloat16
I32 = mybir.dt.int32
AF = mybir.ActivationFunctionType
ALU = mybir.AluOpType

NCORES = 8
S = 2048
D = 1024
DEPTH = 4
KC = D // 128
NB = S // 512
ALPHA = (2 * DEPTH) ** 0.25
LN_EPS = 1e-5
RMS_EPS = 1e-6


class View:
    __slots__ = ("buf", "ap", "ivs")

    def __init__(self, buf, ap, ivs):
        self.buf, self.ap, self.ivs = buf, ap, ivs


def _merge(ivs):
    ivs = sorted(ivs)
    out = [list(ivs[0])]
    for lo, hi in ivs[1:]:
        if lo <= out[-1][1]:
            out[-1][1] = max(out[-1][1], hi)
        else:
            out.append([lo, hi])
    return tuple((a, b) for a, b in out)


def _overlap(a, b):
    for lo, hi in a:
        for lo2, hi2 in b:
            if lo < hi2 and lo2 < hi:
                return True
    return False


def _covers(a, b):
    for lo2, hi2 in b:
        ok = False
        for lo, hi in a:
            if lo <= lo2 and hi2 <= hi:
                ok = True
                break
        if not ok:
            return False
    return True


class Buf:
    def __init__(self, prog, name, shape, dtype, space="sbuf"):
        self.prog, self.name, self.shape, self.dtype = prog, name, list(shape), dtype
        self.esz = 2 if dtype == BF16 else 4
        nc = prog.nc
        if space == "sbuf":
            self.t = nc.alloc_sbuf_tensor(name, self.shape, dtype)
        else:
            self.t = nc.alloc_psum_tensor(name, self.shape, dtype)
        st = []
        acc = 1
        for n in reversed(self.shape[1:]):
            st.append(acc)
            acc *= n
        self.strides = list(reversed(st))
        self.hist = []

    def __getitem__(self, idx):
        if not isinstance(idx, tuple):
            idx = (idx,)
        ap = self.t[idx]
        fidx = list(idx[1:]) + [slice(None)] * (len(self.shape) - len(idx))
        rngs = []
        for i, n in zip(fidx, self.shape[1:]):
            if isinstance(i, int):
                rngs.append((i, i + 1))
            else:
                a, b, _ = i.indices(n)
                rngs.append((a, b))
        outer = rngs[:-1]
        cnt = 1
        for a, b in outer:
            cnt *= (b - a)
        la, lb = rngs[-1]
        if cnt > 64:
            lo = sum(a * s for (a, b), s in zip(rngs, self.strides))
            hi = sum((b - 1) * s for (a, b), s in zip(rngs, self.strides)) + 1
            ivs = ((lo * self.esz, hi * self.esz),)
        else:
            starts = [0]
            for (a, b), s in zip(outer, self.strides[:-1]):
                starts = [o + j * s for o in starts for j in range(a, b)]
            ivs = _merge([((o + la) * self.esz, (o + lb) * self.esz) for o in starts])
        return View(self, ap, ivs)


class Prog:
    ENGS = ("pe", "act", "dve", "pool", "sp")

    def __init__(self, nc, n_dma_sems=24):
        self.nc = nc
        self.streams = {e: [] for e in self.ENGS}
        self.sems = {}
        self.count = {}
        self.waited = {e: {} for e in self.ENGS}
        for e in self.ENGS:
            self.sems[e] = nc.alloc_semaphore(name="s_" + e)
            self.count[e] = 0
        self.dma_sems = []
        for i in range(n_dma_sems):
            nm = "q%d" % i
            self.sems[nm] = nc.alloc_semaphore(name="s_" + nm)
            self.count[nm] = 0
            self.dma_sems.append(nm)
        self.dma_rr = 0
        self.n_ops = 0

    def _deps(self, eng, reads, writes):
        need = {}
        for v in reads:
            for rec in v.buf.hist:
                if rec[2] and _overlap(rec[3], v.ivs):
                    if not (eng == "pe" and rec[0] == "pe"):
                        if need.get(rec[0], 0) < rec[1]:
                            need[rec[0]] = rec[1]
        for v in writes:
            for rec in v.buf.hist:
                if _overlap(rec[3], v.ivs):
                    if not (eng == "pe" and rec[0] == "pe"):
                        if need.get(rec[0], 0) < rec[1]:
                            need[rec[0]] = rec[1]
        return need

    def _record(self, who, cnt, reads, writes):
        for v in reads:
            h = v.buf.hist
            for rec in h:
                if (not rec[2]) and rec[0] == who and rec[3] == v.ivs:
                    rec[1] = max(rec[1], cnt)
                    break
            else:
                h.append([who, cnt, False, v.ivs])
        for v in writes:
            h = v.buf.hist
            h[:] = [rec for rec in h if not _covers(v.ivs, rec[3])]
            h.append([who, cnt, True, v.ivs])

    def _waits(self, eng, need):
        w = []
        for f, c in need.items():
            if self.waited[eng].get(f, 0) < c:
                self.waited[eng][f] = c
                w.append((f, c))
        return w

    def op(self, eng, fn, reads=(), writes=(), inc=True):
        need = self._deps(eng, reads, writes)
        waits = self._waits(eng, need)
        cnt = self.count[eng] + 1
        if inc:
            self.count[eng] = cnt
        self._record(eng, cnt, reads, writes)
        self.streams[eng].append((waits, fn, (eng, 1) if inc else None))
        self.n_ops += 1

    def dma(self, queue, out, in_, out_view=None, in_view=None, **kw):
        reads = [in_view] if in_view is not None else []
        writes = [out_view] if out_view is not None else []
        need = self._deps(queue, reads, writes)
        q = self.dma_sems[self.dma_rr % len(self.dma_sems)]
        self.dma_rr += 1
        if self.count[q] > 0:
            need[q] = max(need.get(q, 0), self.count[q])
        waits = self._waits(queue, need)
        self.count[q] += 16
        cnt = self.count[q]
        self._record(q, cnt, reads, writes)
        self.streams[queue].append((waits, lambda e: e.dma_start(out=out, in_=in_, **kw), (q, 16)))
        self.n_ops += 1
        return (q, cnt)

    def wait_on(self, eng, who, cnt):
        w = self._waits(eng, {who: cnt})
        if w:
            self.streams[eng].append((w, None, None))

    def emit(self):
        nc = self.nc
        engobj = {"pe": "tensor", "act": "scalar", "dve": "vector", "pool": "gpsimd", "sp": "sync"}
        with nc.Block() as block:
            for e in self.ENGS:
                stream = self.streams[e]

                def body(eng, stream=stream):
                    for waits, fn, inc in stream:
                        for (f, c) in waits:
                            eng.wait_ge(self.sems[f], c)
                        if fn is None:
                            continue
                        ins = fn(eng)
                        if inc is not None:
                            ins.then_inc(self.sems[inc[0]], inc[1])

                getattr(block, engobj[e])(body)


class Arr:
    def __init__(self, buf, byte_off, dtype, shape):
        self.buf, self.off, self.dtype, self.shape = buf, byte_off, dtype, list(shape)
        self.esz = 2 if dtype == BF16 else 4
        n = 1
        for d in shape[1:]:
            n *= d
        assert byte_off % 4 == 0 and (n * self.esz) % 4 == 0
        lo, hi = byte_off // 4, (byte_off + n * self.esz) // 4
        assert hi <= buf.shape[1], (buf.name, hi, buf.shape)
        ap = buf.t[0:shape[0], lo:hi]
        if dtype != F32:
            ap = ap.bitcast(dtype)
        fd = shape[1:]
        if len(fd) == 2:
            ap = ap.rearrange("p (a b) -> p a b", a=fd[0])
        elif len(fd) == 3:
            ap = ap.rearrange("p (a b c) -> p a b c", a=fd[0], b=fd[1])
        self.full = ap
        st, acc = [], 1
        for d in reversed(fd):
            st.append(acc)
            acc *= d
        self.strides = list(reversed(st))
        self.nbytes = n * self.esz

    def __getitem__(self, idx):
        if not isinstance(idx, tuple):
            idx = (idx,)
        ap = self.full[idx]
        fidx = list(idx[1:]) + [slice(None)] * (len(self.shape) - len(idx))
        rngs = []
        for i, n in zip(fidx, self.shape[1:]):
            if isinstance(i, int):
                rngs.append((i, i + 1))
            else:
                a, b, _ = i.indices(n)
                rngs.append((a, b))
        outer = rngs[:-1]
        cnt = 1
        for a, b in outer:
            cnt *= (b - a)
        la, lb = rngs[-1]
        if cnt > 64:
            lo = sum(a * s for (a, b), s in zip(rngs, self.strides))
            hi = sum((b - 1) * s for (a, b), s in zip(rngs, self.strides)) + 1
            ivs = ((self.off + lo * self.esz, self.off + hi * self.esz),)
        else:
            starts = [0]
            for (a, b), s in zip(outer, self.strides[:-1]):
                starts = [o + j * s for o in starts for j in range(a, b)]
            ivs = _merge([(self.off + (o + la) * self.esz, self.off + (o + lb) * self.esz) for o in starts])
        return View(self.buf, ap, ivs)


class Carver:
    def __init__(self, buf):
        self.buf, self.pos = buf, 0

    def arr(self, dtype, shape):
        a = Arr(self.buf, self.pos, dtype, shape)
        self.pos += (a.nbytes + 31) // 32 * 32
        assert self.pos <= self.buf.shape[1] * 4, (self.buf.name, self.pos)
        return a


class PsumPool:
    def __init__(self, prog):
        self.banks = [Buf(prog, "psb%d" % i, [128, 512], F32, space="psum") for i in range(8)]
        self.free = list(range(8))

    def get(self):
        assert self.free, "out of PSUM banks"
        return self.banks[self.free.pop(0)]

    def put(self, b):
        self.free.append(self.banks.index(b))


def MM(P, out, lhsT, rhs, start, stop, inc=None):
    if inc is None:
        inc = stop
    P.op("pe", lambda e: e.matmul(out.ap, lhsT=lhsT.ap, rhs=rhs.ap, start=start, stop=stop),
         reads=[lhsT, rhs], writes=[out], inc=inc)


def ACT(P, out, in_, func, scale=1.0, bias=None):
    reads = [in_]
    kw = {}
    if isinstance(scale, View):
        reads.append(scale)
        kw["scale"] = scale.ap
    else:
        kw["scale"] = float(scale)
    if isinstance(bias, View):
        reads.append(bias)
        kw["bias"] = bias.ap
    elif bias is not None:
        kw["bias"] = float(bias)
    P.op("act", lambda e: e.activation(out=out.ap, in_=in_.ap, func=func, **kw), reads=reads, writes=[out])


def TS(P, eng, out, in0, s1, op0, s2=None, op1=None):
    reads = [in0]
    a1 = s1
    if isinstance(s1, View):
        reads.append(s1)
        a1 = s1.ap
    a2 = s2
    if isinstance(s2, View):
        reads.append(s2)
        a2 = s2.ap
    if op1 is None:
        P.op(eng, lambda e: e.tensor_scalar(out=out.ap, in0=in0.ap, scalar1=a1, scalar2=None, op0=op0),
             reads=reads, writes=[out])
    else:
        P.op(eng, lambda e: e.tensor_scalar(out=out.ap, in0=in0.ap, scalar1=a1, scalar2=a2, op0=op0, op1=op1),
             reads=reads, writes=[out])


def TT(P, eng, out, in0, in1, op):
    P.op(eng, lambda e: e.tensor_tensor(out=out.ap, in0=in0.ap, in1=in1.ap, op=op), reads=[in0, in1], writes=[out])


def STT(P, out, in0, sc, in1, op0, op1):
    reads = [in0, in1]
    a = sc
    if isinstance(sc, View):
        reads.append(sc)
        a = sc.ap
    P.op("dve", lambda e: e.scalar_tensor_tensor(out=out.ap, in0=in0.ap, scalar=a, in1=in1.ap, op0=op0, op1=op1),
         reads=reads, writes=[out])


def CP(P, eng, out, in_):
    P.op(eng, lambda e: e.tensor_copy(out=out.ap, in_=in_.ap), reads=[in_], writes=[out])


def MSET(P, eng, out, val):
    P.op(eng, lambda e: e.memset(out.ap, val), writes=[out])


def _pvec_layout():
    cols = {}
    n = 0
    for l in range(DEPTH):
        for nm in ("lmg", "lmb", "lfg", "lfb"):
            cols[(nm, l)] = n
            n += KC
    for j in range(2):
        cols[("pscale", j)] = n; n += 4
        cols[("convw", j)] = n; n += 4 * KC
        for nm in ("convb", "ba", "bx", "lam"):
            cols[(nm, j)] = n; n += KC
        cols[("qg", j)] = n; n += 3
        cols[("kvg", j)] = n; n += 2
    return cols, n


PV_COLS, PV_N = _pvec_layout()
DV_N = 2 * 4 * KC


class Ctx:
    pass


def build(layers=(0, 1, 2, 3)):
    nc = bass.Bass("TRN2", target_bir_lowering=False)
    P = Prog(nc)
    C = Ctx()
    C.P, C.nc = P, nc
    dr = {}

    def din(name, shape, dt=F32):
        dr[name] = nc.dram_tensor(name, list(shape), dt, kind="ExternalInput").ap()
        return dr[name]

    din("xT", [D, S]); din("pos", [1, S], I32); din("pvec", [128, PV_N]); din("invf2", [64, 1]); din("invc", [128, 64])
    din("even_w_in", [2, D, 2560]); din("pool_w", [2, 4, 128, 128]); din("lru_w_a", [2, 8, 128, 128])
    din("lru_w_x", [2, 8, 128, 128]); din("even_w_out", [2, 1536, D])
    din("mla_w_down", [2, D, 704]); din("w_down_sw", [2, D, 64]); din("mla_w_qb", [2, 384, 1536])
    din("w_qb_sw", [2, 384, 512]); din("mla_w_kvb", [2, 256, 2048]); din("mla_w_o", [2, D, D])
    din("mlp_w1", [DEPTH, D, 4096]); din("mlp_w2", [DEPTH, 4096, D])
    yT = nc.dram_tensor("yT", [D, S], F32, kind="ExternalOutput").ap()
    C.dr = dr

    xres_b = Buf(P, "xres", [128, KC * S], F32)
    xt_b = Buf(P, "xtb", [128, KC * S // 2], F32)
    wr_b = [Buf(P, "wring%d" % i, [128, 2048], F32) for i in range(4)]
    rope_b = Buf(P, "rope", [128, 2 * S], F32)
    cst_b = Buf(P, "cst", [128, PV_N + DV_N + 64 + 8 + 64 + 512 + 512 + 16], F32)
    scr_b = Buf(P, "scr", [128, 14080], F32)
    C.xres = Arr(xres_b, 0, F32, [128, KC, S])
    C.xT = Arr(xt_b, 0, BF16, [128, KC, S])
    cc = Carver(cst_b)
    C.pv = cc.arr(F32, [128, PV_N + DV_N])
    C.invc = cc.arr(F32, [128, 64])
    C.invf2 = cc.arr(F32, [64, 1])
    C.ones = cc.arr(BF16, [128, 128])
    C.halfc = cc.arr(F32, [128, 512])
    C.nhalfc = cc.arr(F32, [128, 512])
    C.cos2 = Arr(rope_b, 0, F32, [64, S])
    C.sins = Arr(rope_b, S * 4, F32, [64, S])
    C.scr = scr_b
    C.PS = PsumPool(P)
    C.wr_b = wr_b
    C.wi = 0

    def pv(nm, idx, c0=0, n=1, p=128):
        o = PV_COLS[(nm, idx)] + c0
        return C.pv[0:p, o:o + n]

    def dv(j, which, c0=0, n=1):
        o = PV_N + j * 4 * KC + which * KC + c0
        return C.pv[:, o:o + n]

    C.pvf, C.dvf = pv, dv

    def wslot(loads):
        buf = wr_b[C.wi % len(wr_b)]
        C.wi += 1
        arrs = []
        for off, shp, src in loads:
            a = Arr(buf, off * 2, BF16, [128] + list(shp))
            v = a[:]
            P.dma("pool", v.ap, src, out_view=v)
            arrs.append(a)
        return arrs

    C.wslot = wslot

    v = C.pv[:, 0:PV_N]
    P.dma("sp", v.ap, dr["pvec"], out_view=v)
    v = C.invc[:]
    P.dma("sp", v.ap, dr["invc"], out_view=v)
    v = C.invf2[:]
    P.dma("sp", v.ap, dr["invf2"], out_view=v)
    xsrc = dr["xT"].rearrange("(c p) s -> p c s", p=128)
    for c in range(KC):
        v = C.xres[:, c, :]
        P.dma("sp", v.ap, xsrc[:, c, :], out_view=v)
    for c in range(KC):
        for hf in range(2):
            v = C.xT[:, c, hf * 1024:(hf + 1) * 1024]
            P.dma("pool", v.ap, xsrc[:, c, hf * 1024:(hf + 1) * 1024], out_view=v)
    MSET(P, "dve", C.ones[:], 1.0)
    MSET(P, "dve", C.halfc[:], 0.5)
    MSET(P, "dve", C.nhalfc[:], -0.5)

    if any(l % 2 == 1 for l in layers):
        emit_rope(C)
    for l in layers:
        if l % 2 == 0:
            emit_even(C, l)
        else:
            emit_odd(C, l)
        emit_ln(C, lambda k, l=l: pv("lmg", l, k, 1), lambda k, l=l: pv("lmb", l, k, 1))
        emit_mlp(C, l)
        emit_ln(C, lambda k, l=l: pv("lfg", l, k, 1), lambda k, l=l: pv("lfb", l, k, 1))

    ysrc = yT.rearrange("(c p) s -> p c s", p=128)
    for c in range(KC):
        v = C.xres[:, c, :]
        P.dma("sp", ysrc[:, c, :], v.ap, in_view=v)
    for q in P.dma_sems:
        if P.count[q] > 0:
            P.wait_on("sp", q, P.count[q])
    P.emit()
    return nc


def emit_rope(C):
    P = C.P
    cv = Carver(C.scr)
    pi_ = cv.arr(I32, [64, S])
    ang = cv.arr(F32, [64, S])
    kk = cv.arr(F32, [64, S])
    y = cv.arr(F32, [64, S])
    acc = cv.arr(F32, [64, S])
    sn = cv.arr(F32, [64, S])
    v = pi_[:]
    P.dma("sp", v.ap, C.dr["pos"].partition_broadcast(64), out_view=v)
    CP(P, "dve", ang[:], pi_[:])
    TS(P, "dve", ang[:], ang[:], C.invf2[:, 0:1], ALU.mult)
    MAGIC = 12582912.0
    TS(P, "dve", kk[:], ang[:], 1.0 / (2 * math.pi), ALU.mult, MAGIC, ALU.add)
    TS(P, "dve", kk[:], kk[:], MAGIC, ALU.subtract)
    c1 = 6.28125
    c2 = float(np.float32(2 * math.pi - c1))
    c3 = float(2 * math.pi - c1 - c2)
    for c in (c1, c2, c3):
        STT(P, ang[:], kk[:], -c, ang[:], ALU.mult, ALU.add)
    TS(P, "dve", ang[:], ang[:], 0.5, ALU.mult)
    TT(P, "dve", y[:], ang[:], ang[:], ALU.mult)
    sc_ = [-1.0 / 6, 1.0 / 120, -1.0 / 5040, 1.0 / 362880, -1.0 / 39916800, 1.0 / 6227020800]
    TS(P, "dve", acc[:], y[:], sc_[5], ALU.mult)
    for k in (4, 3, 2, 1, 0):
        STT(P, acc[:], acc[:], sc_[k], y[:], ALU.add, ALU.mult)
    STT(P, sn[:], acc[:], 1.0, ang[:], ALU.add, ALU.mult)
    cc_ = [-0.5, 1.0 / 24, -1.0 / 720, 1.0 / 40320, -1.0 / 3628800, 1.0 / 479001600, -1.0 / 87178291200]
    TS(P, "dve", acc[:], y[:], cc_[6], ALU.mult)
    for k in (5, 4, 3, 2, 1, 0):
        STT(P, acc[:], acc[:], cc_[k], y[:], ALU.add, ALU.mult)
    TS(P, "dve", acc[:], acc[:], 1.0, ALU.add)
    STT(P, C.sins[:], sn[:], 2.0, acc[:], ALU.mult, ALU.mult)
    TT(P, "dve", y[:], sn[:], sn[:], ALU.mult)
    TS(P, "dve", C.cos2[:], y[:], -2.0, ALU.mult, 1.0, ALU.add)
    TS(P, "dve", C.sins[0:32, :], C.sins[0:32, :], -1.0, ALU.mult)


def emit_ln(C, g, b):
    P, PS = C.P, C.PS
    cv = Carver(C.scr)
    sq = cv.arr(BF16, [128, KC, 512])
    mean = cv.arr(F32, [128, 512])
    ve = cv.arr(F32, [128, 512])
    m2 = cv.arr(F32, [128, 512])
    tt_ = [cv.arr(F32, [128, 512]) for _ in range(2)]
    for nb in range(NB):
        blk = slice(nb * 512, (nb + 1) * 512)
        ACT(P, C.xT[:, :, blk], C.xres[:, :, blk], AF.Copy)
        ACT(P, sq[:], C.xres[:, :, blk], AF.Square)
        s1, s2 = PS.get(), PS.get()
        for k in range(KC):
            MM(P, s1[:], C.ones[:], C.xT[:, k, blk], k == 0, k == KC - 1)
        for k in range(KC):
            MM(P, s2[:], C.ones[:], sq[:, k, :], k == 0, k == KC - 1)
        TS(P, "dve", mean[:], s1[:], 1.0 / D, ALU.mult)
        TS(P, "dve", ve[:], s2[:], 1.0 / D, ALU.mult, LN_EPS, ALU.add)
        PS.put(s1); PS.put(s2)
        TT(P, "dve", m2[:], mean[:], mean[:], ALU.mult)
        TT(P, "dve", ve[:], ve[:], m2[:], ALU.subtract)
        TT(P, "pool", ve[:], ve[:], C.nhalfc[:], ALU.pow)
        for k in range(KC):
            t = tt_[k % 2]
            TT(P, "dve", t[:], C.xres[:, k, blk], mean[:], ALU.subtract)
            TT(P, "dve", t[:], t[:], ve[:], ALU.mult)
            ACT(P, C.xres[:, k, blk], t[:], AF.Identity, scale=g(k), bias=b(k))
            CP(P, "pool", C.xT[:, k, blk], C.xres[:, k, blk])


def emit_mlp(C, l):
    P, PS = C.P, C.PS
    cv = Carver(C.scr)
    hb = [cv.arr(BF16, [128, 4, S]) for _ in range(2)]
    tmp = [cv.arr(F32, [128, 512]) for _ in range(2)]
    w1 = C.dr["mlp_w1"][l].rearrange("(kc p) n -> p kc n", p=128)
    w2 = C.dr["mlp_w2"][l].rearrange("(kc p) n -> p kc n", p=128)
    ti = 0
    for c in range(8):
        (W1,) = C.wslot([(0, [KC, 512], w1[:, :, c * 512:(c + 1) * 512])])
        (W2,) = C.wslot([(0, [4, D], w2[:, c * 4:(c + 1) * 4, :])])
        h = hb[c % 2]
        for m in range(4):
            for nb in range(NB):
                blk = slice(nb * 512, (nb + 1) * 512)
                ps = PS.get()
                for k in range(KC):
                    MM(P, ps[:], W1[:, k, m * 128:(m + 1) * 128], C.xT[:, k, blk], k == 0, k == KC - 1)
                t = tmp[ti % 2]; ti += 1
                ACT(P, t[:], ps[:], AF.Relu)
                PS.put(ps)
                TT(P, "pool", h[:, m, blk], t[:], t[:], ALU.mult)
        for mo in range(KC):
            for nb in range(NB):
                blk = slice(nb * 512, (nb + 1) * 512)
                ps = PS.get()
                for k in range(4):
                    MM(P, ps[:], W2[:, k, mo * 128:(mo + 1) * 128], h[:, k, blk], k == 0, k == 3)
                if c == 0:
                    STT(P, C.xres[:, mo, blk], C.xres[:, mo, blk], ALPHA, ps[:], ALU.mult, ALU.add)
                else:
                    TT(P, "dve", C.xres[:, mo, blk], C.xres[:, mo, blk], ps[:], ALU.add)
                PS.put(ps)


def emit_even(C, l):
    P, PS = C.P, C.PS
    j = l // 2
    pv, dv = C.pvf, C.dvf
    cv = Carver(C.scr)
    mixT = cv.arr(BF16, [128, 4, S])
    HW = 1024
    A = cv.arr(F32, [128, 16 + HW])
    Bg = cv.arr(F32, [128, HW])
    Cg = cv.arr(F32, [128, HW])
    C2 = cv.arr(F32, [128, HW])
    R = cv.arr(F32, [128, HW])
    I_ = cv.arr(F32, [128, HW])
    T = cv.arr(F32, [128, HW])
    Hb = cv.arr(F32, [128, HW])
    cb = cv.arr(BF16, [128, HW])
    carry = cv.arr(F32, [128, 8])
    sm = cv.arr(F32, [128, 16])
    w_in = C.dr["even_w_in"][j].rearrange("(kc p) n -> p kc n", p=128)
    w_out = C.dr["even_w_out"][j].rearrange("(kc p) n -> p kc n", p=128)

    cf, hcf, hba, hbx = dv(j, 0, 0, KC), dv(j, 1, 0, KC), dv(j, 2, 0, KC), dv(j, 3, 0, KC)
    ACT(P, cf, pv("lam", j, 0, KC), AF.Exp, scale=-1.0)
    TS(P, "dve", cf, cf, 1.0, ALU.add)
    ACT(P, cf, cf, AF.Ln)
    TS(P, "dve", cf, cf, -8.0, ALU.mult)
    TS(P, "dve", hcf, cf, 0.5, ALU.mult)
    TS(P, "dve", hba, pv("ba", j, 0, KC), 0.5, ALU.mult)
    TS(P, "dve", hbx, pv("bx", j, 0, KC), 0.5, ALU.mult)

    GK = math.sqrt(2.0 / math.pi)
    for grp in range(3):
        for ui in range(4):
            u = grp * 4 + ui
            if u < 4:
                g = u
                Win, Wp = C.wslot([(0, [KC, 128], w_in[:, :, g * 128:(g + 1) * 128]),
                                   (1024, [128], C.dr["pool_w"][j, g])])
                w = 2 << g
                for hf in range(2):
                    if hf == 0:
                        MSET(P, "dve", A[:, 0:16], 0.0)
                    else:
                        CP(P, "dve", A[:, 0:16], A[:, HW:HW + 16])
                    for q in range(2):
                        blk = slice(hf * HW + q * 512, hf * HW + (q + 1) * 512)
                        ps = PS.get()
                        for k in range(KC):
                            MM(P, ps[:], Win[:, k, :], C.xT[:, k, blk], k == 0, k == KC - 1)
                        ACT(P, A[:, 16 + q * 512:16 + (q + 1) * 512], ps[:], AF.Copy)
                        PS.put(ps)
                    src = A
                    sh = 1
                    lo = 0
                    bufs = [Cg, C2]
                    ext0 = Arr(C.scr, Cg.off, F32, [128, 16 + HW])
                    ext1 = Arr(C.scr, R.off, F32, [128, 16 + HW])
                    exts = [ext0, ext1]
                    ei = 0
                    while sh < w:
                        dst = exts[ei % 2]; ei += 1
                        lo2 = lo + sh
                        TT(P, "dve", dst[:, lo2:16 + HW], src[:, lo2:16 + HW], src[:, lo2 - sh:16 + HW - sh], ALU.add)
                        src, lo, sh = dst, lo2, sh * 2
                    STT(P, T[:], src[:, 16:16 + HW], 1.0 / w, A[:, 16:16 + HW], ALU.mult, ALU.subtract)
                    if hf == 0:
                        TT(P, "dve", sm[:], src[:, 16:32], C.invc[:, g * 16:(g + 1) * 16], ALU.mult)
                        TT(P, "dve", T[:, 0:16], sm[:], A[:, 16:32], ALU.subtract)
                    CP(P, "act", cb[:], T[:]) if False else ACT(P, cb[:], T[:], AF.Copy)
                    for q in range(2):
                        blk = slice(hf * HW + q * 512, hf * HW + (q + 1) * 512)
                        ps = PS.get()
                        MM(P, ps[:], Wp[:], cb[:, q * 512:(q + 1) * 512], True, True)
                        ACT(P, mixT[:, ui, blk], ps[:], AF.Copy, scale=pv("pscale", j, g, 1))
                        PS.put(ps)
            else:
                h = u - 4
                Wu, Wg, Wa, Wx = C.wslot([
                    (0, [KC, 128], w_in[:, :, 512 + h * 128:512 + (h + 1) * 128]),
                    (1024, [KC, 128], w_in[:, :, 1536 + h * 128:1536 + (h + 1) * 128]),
                    (2048, [128], C.dr["lru_w_a"][j, h]),
                    (2176, [128], C.dr["lru_w_x"][j, h])])
                for hf in range(2):
                    if hf == 0:
                        MSET(P, "dve", A[:, 0:16], 0.0)
                    else:
                        CP(P, "dve", A[:, 0:16], A[:, HW:HW + 16])
                    for q in range(2):
                        blk = slice(hf * HW + q * 512, hf * HW + (q + 1) * 512)
                        ps = PS.get()
                        for k in range(KC):
                            MM(P, ps[:], Wu[:, k, :], C.xT[:, k, blk], k == 0, k == KC - 1)
                        ACT(P, A[:, 16 + q * 512:16 + (q + 1) * 512], ps[:], AF.Copy)
                        PS.put(ps)
                    for q in range(2):
                        blk = slice(hf * HW + q * 512, hf * HW + (q + 1) * 512)
                        ps = PS.get()
                        for k in range(KC):
                            MM(P, ps[:], Wg[:, k, :], C.xT[:, k, blk], k == 0, k == KC - 1)
                        ACT(P, Bg[:, q * 512:(q + 1) * 512], ps[:], AF.Copy)
                        PS.put(ps)
                    cw = lambda k: pv("convw", j, k * KC + h, 1)
                    TS(P, "dve", C2[:], A[:, 16:16 + HW], cw(3), ALU.mult, pv("convb", j, h, 1), ALU.add)
                    for k in (2, 1, 0):
                        STT(P, C2[:], A[:, 13 + k:13 + k + HW], cw(k), C2[:], ALU.mult, ALU.add)
                    ACT(P, cb[:], C2[:], AF.Copy)
                    for q in range(2):
                        ps = PS.get()
                        MM(P, ps[:], Wa[:], cb[:, q * 512:(q + 1) * 512], True, True)
                        ACT(P, R[:, q * 512:(q + 1) * 512], ps[:], AF.Tanh, scale=0.5, bias=dv(j, 2, h, 1))
                        PS.put(ps)
                        ps = PS.get()
                        MM(P, ps[:], Wx[:], cb[:, q * 512:(q + 1) * 512], True, True)
                        ACT(P, I_[:, q * 512:(q + 1) * 512], ps[:], AF.Tanh, scale=0.5, bias=dv(j, 3, h, 1))
                        PS.put(ps)
                    ACT(P, T[:], R[:], AF.Exp, scale=dv(j, 0, h, 1), bias=dv(j, 0, h, 1))
                    ACT(P, R[:], R[:], AF.Exp, scale=dv(j, 1, h, 1), bias=dv(j, 1, h, 1))
                    TS(P, "dve", T[:], T[:], -1.0, ALU.mult, 1.0, ALU.add)
                    TS(P, "dve", T[:], T[:], 0.0, ALU.max)
                    for q in range(2):
                        TT(P, "pool", T[:, q * 512:(q + 1) * 512], T[:, q * 512:(q + 1) * 512], C.halfc[:], ALU.pow)
                    STT(P, I_[:], I_[:], 1.0, T[:], ALU.add, ALU.mult)
                    STT(P, I_[:], I_[:], 0.5, C2[:], ALU.mult, ALU.mult)
                    init = 0.0 if hf == 0 else carry[:, h:h + 1]
                    if hf == 0:
                        P.op("dve", lambda e, o=Hb[:], a=R[:], b_=I_[:]: e.tensor_tensor_scan(
                            out=o.ap, data0=a.ap, data1=b_.ap, initial=0.0, op0=ALU.mult, op1=ALU.add),
                            reads=[R[:], I_[:]], writes=[Hb[:]])
                    else:
                        P.op("dve", lambda e, o=Hb[:], a=R[:], b_=I_[:], i0=init: e.tensor_tensor_scan(
                            out=o.ap, data0=a.ap, data1=b_.ap, initial=i0.ap, op0=ALU.mult, op1=ALU.add),
                            reads=[R[:], I_[:], init], writes=[Hb[:]])
                    CP(P, "dve", carry[:, h:h + 1], Hb[:, HW - 1:HW])
                    ACT(P, Cg[:], Bg[:], AF.Square)
                    TS(P, "dve", Cg[:], Cg[:], 0.044715, ALU.mult, 1.0, ALU.add)
                    TT(P, "dve", Cg[:], Cg[:], Bg[:], ALU.mult)
                    ACT(P, Cg[:], Cg[:], AF.Tanh, scale=GK)
                    STT(P, Bg[:], Cg[:], 1.0, Bg[:], ALU.add, ALU.mult)
                    STT(P, mixT[:, ui, hf * HW:(hf + 1) * HW], Bg[:], 0.5, Hb[:], ALU.mult, ALU.mult)
        (Wo,) = C.wslot([(0, [4, D], w_out[:, grp * 4:(grp + 1) * 4, :])])
        for mo in range(KC):
            for nb in range(NB):
                blk = slice(nb * 512, (nb + 1) * 512)
                ps = PS.get()
                for k in range(4):
                    MM(P, ps[:], Wo[:, k, mo * 128:(mo + 1) * 128], mixT[:, k, blk], k == 0, k == 3)
                if grp == 0:
                    STT(P, C.xres[:, mo, blk], C.xres[:, mo, blk], ALPHA, ps[:], ALU.mult, ALU.add)
                else:
                    TT(P, "dve", C.xres[:, mo, blk], C.xres[:, mo, blk], ps[:], ALU.add)
                PS.put(ps)


def emit_odd(C, l):
    P, PS = C.P, C.PS
    j = l // 2
    pv = C.pvf
    cv = Carver(C.scr)
    cqn = cv.arr(BF16, [128, 3, S])
    ckvn = cv.arr(BF16, [128, 2, S])
    kpe = cv.arr(BF16, [64, S])
    qn = cv.arr(BF16, [128, S])
    qpe = cv.arr(BF16, [64, S])
    kn = cv.arr(BF16, [128, S])
    V = cv.arr(BF16, [128, 16, 128])
    E = [cv.arr(BF16, [128, 512]) for _ in range(3)]
    sq = cv.arr(BF16, [128, 3, 512])
    rs = [cv.arr(F32, [128, 512]) for _ in range(2)]
    r1 = [cv.arr(F32, [64, 512]) for _ in range(1)]
    r2 = [cv.arr(F32, [64, 512]) for _ in range(1)]
    oT = Arr(C.xT.buf, 0, BF16, [128, KC, S])
    wd = C.dr["mla_w_down"][j].rearrange("(kc p) n -> p kc n", p=128)
    wds = C.dr["w_down_sw"][j].rearrange("(kc p) n -> p kc n", p=128)
    wqb = C.dr["mla_w_qb"][j].rearrange("(kc p) n -> p kc n", p=128)
    wqs = C.dr["w_qb_sw"][j].rearrange("(kc p) n -> p kc n", p=128)
    wkv = C.dr["mla_w_kvb"][j].rearrange("(kc p) n -> p kc n", p=128)
    wo = C.dr["mla_w_o"][j].rearrange("(kc p) n -> p kc n", p=128)
    SCALE = 192.0 ** -0.5

    (Wd1,) = C.wslot([(0, [KC, 384], wd[:, :, 0:384])])
    Wd2, Wd3 = C.wslot([(0, [KC, 320], wd[:, :, 384:704]), (2560, [KC, 64], wds)])

    def rope(dst, pa, pb, blk, ri):
        TT(P, "dve", r1[ri][:], pa, C.cos2[:, blk], ALU.mult)
        TT(P, "dve", r2[ri][:], pb, C.sins[:, blk], ALU.mult)
        TT(P, "dve", dst, r1[ri][:], r2[ri][:], ALU.add)

    def rmsn(dst, W, ncol, nch, gname, blk, ri):
        banks = []
        for m in range(nch):
            ps = PS.get()
            for k in range(KC):
                MM(P, ps[:], W[:, k, m * 128:(m + 1) * 128], C.xT[:, k, blk], k == 0, k == KC - 1)
            ACT(P, sq[:, m, :], ps[:], AF.Square)
            banks.append(ps)
        s2 = PS.get()
        for m in range(nch):
            MM(P, s2[:], C.ones[:], sq[:, m, :], m == 0, m == nch - 1)
        ve = rs[ri]
        TS(P, "dve", ve[:], s2[:], 1.0 / ncol, ALU.mult, RMS_EPS, ALU.add)
        PS.put(s2)
        TT(P, "pool", ve[:], ve[:], C.nhalfc[:], ALU.pow)
        for m in range(nch):
            STT(P, dst[:, m, blk], banks[m][:], pv(gname, j, m, 1), ve[:], ALU.mult, ALU.mult)
            PS.put(banks[m])

    for nb in range(NB):
        blk = slice(nb * 512, (nb + 1) * 512)
        rmsn(cqn, Wd1, 384, 3, "qg", blk, 0)
        rmsn(ckvn, Wd2, 256, 2, "kvg", blk, 1)
        pa, pb = PS.get(), PS.get()
        for k in range(KC):
            MM(P, pa[0:64, :], Wd2[:, k, 256:320], C.xT[:, k, blk], k == 0, k == KC - 1)
        for k in range(KC):
            MM(P, pb[0:64, :], Wd3[:, k, :], C.xT[:, k, blk], k == 0, k == KC - 1)
        rope(kpe[:, blk], pa[0:64, :], pb[0:64, :], blk, 0)
        PS.put(pa); PS.put(pb)

    ei = 0
    for h in range(8):
        Wq, Wqs, Wkv = C.wslot([(0, [3, 192], wqb[:, :, h * 192:(h + 1) * 192]),
                                (576, [3, 64], wqs[:, :, h * 64:(h + 1) * 64]),
                                (768, [2, 256], wkv[0:128, :, h * 256:(h + 1) * 256])])
        for nb in range(NB):
            blk = slice(nb * 512, (nb + 1) * 512)
            ps = PS.get()
            for k in range(3):
                MM(P, ps[:], Wq[:, k, 0:128], cqn[:, k, blk], k == 0, k == 2)
            ACT(P, qn[:, blk], ps[:], AF.Copy)
            PS.put(ps)
            pa, pb = PS.get(), PS.get()
            for k in range(3):
                MM(P, pa[0:64, :], Wq[:, k, 128:192], cqn[:, k, blk], k == 0, k == 2)
            for k in range(3):
                MM(P, pb[0:64, :], Wqs[:, k, :], cqn[:, k, blk], k == 0, k == 2)
            rope(qpe[:, blk], pa[0:64, :], pb[0:64, :], blk, 0)
            PS.put(pa); PS.put(pb)
            ps = PS.get()
            for k in range(2):
                MM(P, ps[:], Wkv[:, k, 0:128], ckvn[:, k, blk], k == 0, k == 1)
            ACT(P, kn[:, blk], ps[:], AF.Copy)
            PS.put(ps)
        for t4 in range(4):
            ps = PS.get()
            for tt in range(4):
                t = t4 * 4 + tt
                for k in range(2):
                    MM(P, ps[:, tt * 128:(tt + 1) * 128], ckvn[:, k, t * 128:(t + 1) * 128], Wkv[:, k, 128:256],
                       k == 0, k == 1, inc=(k == 1 and tt == 3))
            ACT(P, V[:, t4 * 4:(t4 + 1) * 4, :], ps[:], AF.Copy)
            PS.put(ps)
        for qb in range(NB):
            num, den = PS.get(), PS.get()
            nkt = 4 * qb + 4
            for kt in range(nkt):
                q0 = max(512 * qb, 128 * kt)
                N = 512 * qb + 512 - q0
                c0 = q0 - 512 * qb
                kts = slice(kt * 128, (kt + 1) * 128)
                sc = PS.get()
                MM(P, sc[:, 0:N], kn[:, kts], qn[:, q0:q0 + N], True, False)
                MM(P, sc[:, 0:N], kpe[:, kts], qpe[:, q0:q0 + N], False, True)
                Eb = E[ei % 3]; ei += 1
                ACT(P, Eb[:, 0:N], sc[:, 0:N], AF.Exp, scale=SCALE)
                PS.put(sc)
                if kt >= 4 * qb:
                    MSET(P, "dve", Eb[64:128, 0:64], 0.0)
                MM(P, num[:, c0:c0 + N], V[:, kt, :], Eb[:, 0:N], kt == 0, kt == nkt - 1, inc=False)
                MM(P, den[:, c0:c0 + N], C.ones[:], Eb[:, 0:N], kt == 0, kt == nkt - 1, inc=True)
            rc = rs[qb % 2]
            P.op("dve", lambda e, o=rc[:], i=den[:]: e.reciprocal(out=o.ap, in_=i.ap), reads=[den[:]], writes=[rc[:]])
            TT(P, "dve", oT[:, h, qb * 512:(qb + 1) * 512], num[:], rc[:], ALU.mult)
            PS.put(num); PS.put(den)

    Wo1 = C.wslot([(0, [4, D], wo[:, 0:4, :])])[0]
    Wo2 = C.wslot([(0, [4, D], wo[:, 4:8, :])])[0]
    for mo in range(KC):
        for nb in range(NB):
            blk = slice(nb * 512, (nb + 1) * 512)
            ps = PS.get()
            for k in range(KC):
                W = Wo1 if k < 4 else Wo2
                MM(P, ps[:], W[:, k % 4, mo * 128:(mo + 1) * 128], oT[:, k, blk], k == 0, k == KC - 1)
            STT(P, C.xres[:, mo, blk], C.xres[:, mo, blk], ALPHA, ps[:], ALU.mult, ALU.add)
            PS.put(ps)


_NC_CACHE = {}


def _host_prep(inp):
    f = np.float32
    pvec = np.zeros((128, PV_N), f)

    def put(key, vec):
        v = np.asarray(vec, f)
        n = v.shape[0] // 128
        o = PV_COLS[key]
        pvec[:, o:o + n] = v.reshape(n, 128).T

    for l in range(DEPTH):
        put(("lmg", l), inp["ln_mix_g"][l]); put(("lmb", l), inp["ln_mix_b"][l])
        put(("lfg", l), inp["ln_ffn_g"][l]); put(("lfb", l), inp["ln_ffn_b"][l])
    for j in range(2):
        put(("pscale", j), inp["pool_scale"][j])
        cw = np.asarray(inp["lru_conv_w"][j], f)
        o = PV_COLS[("convw", j)]
        for k in range(4):
            pvec[:, o + k * KC:o + (k + 1) * KC] = cw[k].reshape(KC, 128).T
        put(("convb", j), inp["lru_conv_b"][j]); put(("ba", j), inp["lru_b_a"][j])
        put(("bx", j), inp["lru_b_x"][j]); put(("lam", j), inp["lru_lambda"][j])
        put(("qg", j), inp["mla_q_norm_g"][j]); put(("kvg", j), inp["mla_kv_norm_g"][j])
    inv_freq = (10000.0 ** (-np.arange(0, 64, 2, dtype=f) / f(64))).astype(f)
    invf2 = np.concatenate([inv_freq, inv_freq]).reshape(64, 1).astype(f)
    invc = np.zeros((128, 64), f)
    for g, w in enumerate((2, 4, 8, 16)):
        invc[:, g * 16:(g + 1) * 16] = (1.0 / np.minimum(np.arange(16) + 1, w)).astype(f)[None, :]
    wd = np.asarray(inp["mla_w_down"], f)
    w_down_sw = np.ascontiguousarray(np.concatenate([wd[:, :, 672:704], wd[:, :, 640:672]], axis=2))
    wq = np.asarray(inp["mla_w_qb"], f).reshape(2, 384, 8, 192)
    w_qb_sw = np.ascontiguousarray(np.concatenate([wq[..., 160:192], wq[..., 128:160]], axis=3).reshape(2, 384, 512))
    shared = {"pvec": pvec, "invf2": invf2, "invc": invc, "w_down_sw": w_down_sw, "w_qb_sw": w_qb_sw}
    for nm in ("even_w_in", "pool_w", "lru_w_a", "lru_w_x", "even_w_out", "mla_w_down", "mla_w_qb",
               "mla_w_kvb", "mla_w_o", "mlp_w1", "mlp_w2"):
        shared[nm] = np.ascontiguousarray(np.asarray(inp[nm], f))
    return shared


LAUNCH_GROUPS = [(0, 1, 2, 3)]


def kernel(**inp):
    shared = _host_prep(inp)
    x = np.asarray(inp["x"], np.float32)
    pos = np.asarray(inp["positions"], np.int32)
    cur = [np.ascontiguousarray(x[b].T) for b in range(NCORES)]
    for grp in LAUNCH_GROUPS:
        if grp not in _NC_CACHE:
            _NC_CACHE[grp] = build(grp)
        nc = _NC_CACHE[grp]
        in_maps = []
        for b in range(NCORES):
            m = dict(shared)
            m["xT"] = cur[b]
            m["pos"] = np.ascontiguousarray(pos[b][None, :])
            in_maps.append(m)
        res = run_bass_kernel_spmd(nc, in_maps, core_ids=list(range(NCORES)))
        cur = [np.ascontiguousarray(res.results[b]["yT"]) for b in range(NCORES)]
    return np.stack([c.T for c in cur], axis=0).astype(np.float32)
```

```python
import math
import numpy as np
import concourse.bass as bass
import concourse.mybir as mybir
from concourse.bass_utils import run_bass_kernel_spmd

F32 = mybir.dt.float32
BF16 = mybir.dt.bfloat16
I32 = mybir.dt.int32
AF = mybir.ActivationFunctionType
ALU = mybir.AluOpType

NCORES = 8
S = 2048
D = 1024
DEPTH = 4
KC = D // 128
NB = S // 512
ALPHA = (2 * DEPTH) ** 0.25
LN_EPS = 1e-5
RMS_EPS = 1e-6


class View:
    __slots__ = ("buf", "ap", "ivs")

    def __init__(self, buf, ap, ivs):
        self.buf, self.ap, self.ivs = buf, ap, ivs


def _merge(ivs):
    ivs = sorted(ivs)
    out = [list(ivs[0])]
    for lo, hi in ivs[1:]:
        if lo <= out[-1][1]:
            out[-1][1] = max(out[-1][1], hi)
        else:
            out.append([lo, hi])
    return tuple((a, b) for a, b in out)


def _overlap(a, b):
    for lo, hi in a:
        for lo2, hi2 in b:
            if lo < hi2 and lo2 < hi:
                return True
    return False


def _covers(a, b):
    for lo2, hi2 in b:
        ok = False
        for lo, hi in a:
            if lo <= lo2 and hi2 <= hi:
                ok = True
                break
        if not ok:
            return False
    return True


class Buf:
    def __init__(self, prog, name, shape, dtype, space="sbuf"):
        self.prog, self.name, self.shape, self.dtype = prog, name, list(shape), dtype
        self.esz = 2 if dtype == BF16 else 4
        nc = prog.nc
        if space == "sbuf":
            self.t = nc.alloc_sbuf_tensor(name, self.shape, dtype)
        else:
            self.t = nc.alloc_psum_tensor(name, self.shape, dtype)
        st = []
        acc = 1
        for n in reversed(self.shape[1:]):
            st.append(acc)
            acc *= n
        self.strides = list(reversed(st))
        self.hist = []

    def __getitem__(self, idx):
        if not isinstance(idx, tuple):
            idx = (idx,)
        ap = self.t[idx]
        fidx = list(idx[1:]) + [slice(None)] * (len(self.shape) - len(idx))
        rngs = []
        for i, n in zip(fidx, self.shape[1:]):
            if isinstance(i, int):
                rngs.append((i, i + 1))
            else:
                a, b, _ = i.indices(n)
                rngs.append((a, b))
        outer = rngs[:-1]
        cnt = 1
        for a, b in outer:
            cnt *= (b - a)
        la, lb = rngs[-1]
        if cnt > 64:
            lo = sum(a * s for (a, b), s in zip(rngs, self.strides))
            hi = sum((b - 1) * s for (a, b), s in zip(rngs, self.strides)) + 1
            ivs = ((lo * self.esz, hi * self.esz),)
        else:
            starts = [0]
            for (a, b), s in zip(outer, self.strides[:-1]):
                starts = [o + j * s for o in starts for j in range(a, b)]
            ivs = _merge([((o + la) * self.esz, (o + lb) * self.esz) for o in starts])
        return View(self, ap, ivs)


class Prog:
    ENGS = ("pe", "act", "dve", "pool", "sp")

    def __init__(self, nc, n_dma_sems=24):
        self.nc = nc
        self.streams = {e: [] for e in self.ENGS}
        self.sems = {}
        self.count = {}
        self.waited = {e: {} for e in self.ENGS}
        for e in self.ENGS:
            self.sems[e] = nc.alloc_semaphore(name="s_" + e)
            self.count[e] = 0
        self.dma_sems = []
        for i in range(n_dma_sems):
            nm = "q%d" % i
            self.sems[nm] = nc.alloc_semaphore(name="s_" + nm)
            self.count[nm] = 0
            self.dma_sems.append(nm)
        self.dma_rr = 0
        self.n_ops = 0

    def _deps(self, eng, reads, writes):
        need = {}
        for v in reads:
            for rec in v.buf.hist:
                if rec[2] and _overlap(rec[3], v.ivs):
                    if not (eng == "pe" and rec[0] == "pe"):
                        if need.get(rec[0], 0) < rec[1]:
                            need[rec[0]] = rec[1]
        for v in writes:
            for rec in v.buf.hist:
                if _overlap(rec[3], v.ivs):
                    if not (eng == "pe" and rec[0] == "pe"):
                        if need.get(rec[0], 0) < rec[1]:
                            need[rec[0]] = rec[1]
        return need

    def _record(self, who, cnt, reads, writes):
        for v in reads:
            h = v.buf.hist
            for rec in h:
                if (not rec[2]) and rec[0] == who and rec[3] == v.ivs:
                    rec[1] = max(rec[1], cnt)
                    break
            else:
                h.append([who, cnt, False, v.ivs])
        for v in writes:
            h = v.buf.hist
            h[:] = [rec for rec in h if not _covers(v.ivs, rec[3])]
            h.append([who, cnt, True, v.ivs])

    def _waits(self, eng, need):
        w = []
        for f, c in need.items():
            if self.waited[eng].get(f, 0) < c:
                self.waited[eng][f] = c
                w.append((f, c))
        return w

    def op(self, eng, fn, reads=(), writes=(), inc=True):
        need = self._deps(eng, reads, writes)
        waits = self._waits(eng, need)
        cnt = self.count[eng] + 1
        if inc:
            self.count[eng] = cnt
        self._record(eng, cnt, reads, writes)
        self.streams[eng].append((waits, fn, (eng, 1) if inc else None))
        self.n_ops += 1

    def dma(self, queue, out, in_, out_view=None, in_view=None, **kw):
        reads = [in_view] if in_view is not None else []
        writes = [out_view] if out_view is not None else []
        need = self._deps(queue, reads, writes)
        q = self.dma_sems[self.dma_rr % len(self.dma_sems)]
        self.dma_rr += 1
        if self.count[q] > 0:
            need[q] = max(need.get(q, 0), self.count[q])
        waits = self._waits(queue, need)
        self.count[q] += 16
        cnt = self.count[q]
        self._record(q, cnt, reads, writes)
        self.streams[queue].append((waits, lambda e: e.dma_start(out=out, in_=in_, **kw), (q, 16)))
        self.n_ops += 1
        return (q, cnt)

    def wait_on(self, eng, who, cnt):
        w = self._waits(eng, {who: cnt})
        if w:
            self.streams[eng].append((w, None, None))

    def emit(self):
        nc = self.nc
        engobj = {"pe": "tensor", "act": "scalar", "dve": "vector", "pool": "gpsimd", "sp": "sync"}
        with nc.Block() as block:
            for e in self.ENGS:
                stream = self.streams[e]

                def body(eng, stream=stream):
                    for waits, fn, inc in stream:
                        for (f, c) in waits:
                            eng.wait_ge(self.sems[f], c)
                        if fn is None:
                            continue
                        ins = fn(eng)
                        if inc is not None:
                            ins.then_inc(self.sems[inc[0]], inc[1])

                getattr(block, engobj[e])(body)


class Arr:
    def __init__(self, buf, byte_off, dtype, shape):
        self.buf, self.off, self.dtype, self.shape = buf, byte_off, dtype, list(shape)
        self.esz = 2 if dtype == BF16 else 4
        n = 1
        for d in shape[1:]:
            n *= d
        assert byte_off % 4 == 0 and (n * self.esz) % 4 == 0
        lo, hi = byte_off // 4, (byte_off + n * self.esz) // 4
        assert hi <= buf.shape[1], (buf.name, hi, buf.shape)
        ap = buf.t[0:shape[0], lo:hi]
        if dtype != F32:
            ap = ap.bitcast(dtype)
        fd = shape[1:]
        if len(fd) == 2:
            ap = ap.rearrange("p (a b) -> p a b", a=fd[0])
        elif len(fd) == 3:
            ap = ap.rearrange("p (a b c) -> p a b c", a=fd[0], b=fd[1])
        self.full = ap
        st, acc = [], 1
        for d in reversed(fd):
            st.append(acc)
            acc *= d
        self.strides = list(reversed(st))
        self.nbytes = n * self.esz

    def __getitem__(self, idx):
        if not isinstance(idx, tuple):
            idx = (idx,)
        ap = self.full[idx]
        fidx = list(idx[1:]) + [slice(None)] * (len(self.shape) - len(idx))
        rngs = []
        for i, n in zip(fidx, self.shape[1:]):
            if isinstance(i, int):
                rngs.append((i, i + 1))
            else:
                a, b, _ = i.indices(n)
                rngs.append((a, b))
        outer = rngs[:-1]
        cnt = 1
        for a, b in outer:
            cnt *= (b - a)
        la, lb = rngs[-1]
        if cnt > 64:
            lo = sum(a * s for (a, b), s in zip(rngs, self.strides))
            hi = sum((b - 1) * s for (a, b), s in zip(rngs, self.strides)) + 1
            ivs = ((self.off + lo * self.esz, self.off + hi * self.esz),)
        else:
            starts = [0]
            for (a, b), s in zip(outer, self.strides[:-1]):
                starts = [o + j * s for o in starts for j in range(a, b)]
            ivs = _merge([(self.off + (o + la) * self.esz, self.off + (o + lb) * self.esz) for o in starts])
        return View(self.buf, ap, ivs)


class Carver:
    def __init__(self, buf):
        self.buf, self.pos = buf, 0

    def arr(self, dtype, shape):
        a = Arr(self.buf, self.pos, dtype, shape)
        self.pos += (a.nbytes + 31) // 32 * 32
        assert self.pos <= self.buf.shape[1] * 4, (self.buf.name, self.pos)
        return a


class PsumPool:
    def __init__(self, prog):
        self.banks = [Buf(prog, "psb%d" % i, [128, 512], F32, space="psum") for i in range(8)]
        self.free = list(range(8))

    def get(self):
        assert self.free, "out of PSUM banks"
        return self.banks[self.free.pop(0)]

    def put(self, b):
        self.free.append(self.banks.index(b))


def MM(P, out, lhsT, rhs, start, stop, inc=None):
    if inc is None:
        inc = stop
    P.op("pe", lambda e: e.matmul(out.ap, lhsT=lhsT.ap, rhs=rhs.ap, start=start, stop=stop),
         reads=[lhsT, rhs], writes=[out], inc=inc)


def ACT(P, out, in_, func, scale=1.0, bias=None):
    reads = [in_]
    kw = {}
    if isinstance(scale, View):
        reads.append(scale)
        kw["scale"] = scale.ap
    else:
        kw["scale"] = float(scale)
    if isinstance(bias, View):
        reads.append(bias)
        kw["bias"] = bias.ap
    elif bias is not None:
        kw["bias"] = float(bias)
    P.op("act", lambda e: e.activation(out=out.ap, in_=in_.ap, func=func, **kw), reads=reads, writes=[out])


def TS(P, eng, out, in0, s1, op0, s2=None, op1=None):
    reads = [in0]
    a1 = s1
    if isinstance(s1, View):
        reads.append(s1)
        a1 = s1.ap
    a2 = s2
    if isinstance(s2, View):
        reads.append(s2)
        a2 = s2.ap
    if op1 is None:
        P.op(eng, lambda e: e.tensor_scalar(out=out.ap, in0=in0.ap, scalar1=a1, scalar2=None, op0=op0),
             reads=reads, writes=[out])
    else:
        P.op(eng, lambda e: e.tensor_scalar(out=out.ap, in0=in0.ap, scalar1=a1, scalar2=a2, op0=op0, op1=op1),
             reads=reads, writes=[out])


def TT(P, eng, out, in0, in1, op):
    P.op(eng, lambda e: e.tensor_tensor(out=out.ap, in0=in0.ap, in1=in1.ap, op=op), reads=[in0, in1], writes=[out])


def STT(P, out, in0, sc, in1, op0, op1):
    reads = [in0, in1]
    a = sc
    if isinstance(sc, View):
        reads.append(sc)
        a = sc.ap
    P.op("dve", lambda e: e.scalar_tensor_tensor(out=out.ap, in0=in0.ap, scalar=a, in1=in1.ap, op0=op0, op1=op1),
         reads=reads, writes=[out])


def CP(P, eng, out, in_):
    P.op(eng, lambda e: e.tensor_copy(out=out.ap, in_=in_.ap), reads=[in_], writes=[out])


def MSET(P, eng, out, val):
    P.op(eng, lambda e: e.memset(out.ap, val), writes=[out])


def _pvec_layout():
    cols = {}
    n = 0
    for l in range(DEPTH):
        for nm in ("lmg", "lmb", "lfg", "lfb"):
            cols[(nm, l)] = n
            n += KC
    for j in range(2):
        cols[("pscale", j)] = n; n += 4
        cols[("convw", j)] = n; n += 4 * KC
        for nm in ("convb", "ba", "bx", "lam"):
            cols[(nm, j)] = n; n += KC
        cols[("qg", j)] = n; n += 3
        cols[("kvg", j)] = n; n += 2
    return cols, n


PV_COLS, PV_N = _pvec_layout()
DV_N = 2 * 4 * KC


class Ctx:
    pass


def build(layers=(0, 1, 2, 3)):
    nc = bass.Bass("TRN2", target_bir_lowering=False)
    P = Prog(nc)
    C = Ctx()
    C.P, C.nc = P, nc
    dr = {}

    def din(name, shape, dt=F32):
        dr[name] = nc.dram_tensor(name, list(shape), dt, kind="ExternalInput").ap()
        return dr[name]

    din("xT", [D, S]); din("pos", [1, S], I32); din("pvec", [128, PV_N]); din("invf2", [64, 1]); din("invc", [128, 64])
    din("even_w_in", [2, D, 2560]); din("pool_w", [2, 4, 128, 128]); din("lru_w_a", [2, 8, 128, 128])
    din("lru_w_x", [2, 8, 128, 128]); din("even_w_out", [2, 1536, D])
    din("mla_w_down", [2, D, 704]); din("w_down_sw", [2, D, 64]); din("mla_w_qb", [2, 384, 1536])
    din("w_qb_sw", [2, 384, 512]); din("mla_w_kvb", [2, 256, 2048]); din("mla_w_o", [2, D, D])
    din("mlp_w1", [DEPTH, D, 4096]); din("mlp_w2", [DEPTH, 4096, D])
    yT = nc.dram_tensor("yT", [D, S], F32, kind="ExternalOutput").ap()
    C.dr = dr

    xres_b = Buf(P, "xres", [128, KC * S], F32)
    xt_b = Buf(P, "xtb", [128, KC * S // 2], F32)
    wr_b = [Buf(P, "wring%d" % i, [128, 2048], F32) for i in range(4)]
    rope_b = Buf(P, "rope", [128, 2 * S], F32)
    cst_b = Buf(P, "cst", [128, PV_N + DV_N + 64 + 8 + 64 + 512 + 512 + 16], F32)
    scr_b = Buf(P, "scr", [128, 14080], F32)
    C.xres = Arr(xres_b, 0, F32, [128, KC, S])
    C.xT = Arr(xt_b, 0, BF16, [128, KC, S])
    cc = Carver(cst_b)
    C.pv = cc.arr(F32, [128, PV_N + DV_N])
    C.invc = cc.arr(F32, [128, 64])
    C.invf2 = cc.arr(F32, [64, 1])
    C.ones = cc.arr(BF16, [128, 128])
    C.halfc = cc.arr(F32, [128, 512])
    C.nhalfc = cc.arr(F32, [128, 512])
    C.cos2 = Arr(rope_b, 0, F32, [64, S])
    C.sins = Arr(rope_b, S * 4, F32, [64, S])
    C.scr = scr_b
    C.PS = PsumPool(P)
    C.wr_b = wr_b
    C.wi = 0

    def pv(nm, idx, c0=0, n=1, p=128):
        o = PV_COLS[(nm, idx)] + c0
        return C.pv[0:p, o:o + n]

    def dv(j, which, c0=0, n=1):
        o = PV_N + j * 4 * KC + which * KC + c0
        return C.pv[:, o:o + n]

    C.pvf, C.dvf = pv, dv

    def wslot(loads):
        buf = wr_b[C.wi % len(wr_b)]
        C.wi += 1
        arrs = []
        for off, shp, src in loads:
            a = Arr(buf, off * 2, BF16, [128] + list(shp))
            v = a[:]
            P.dma("pool", v.ap, src, out_view=v)
            arrs.append(a)
        return arrs

    C.wslot = wslot

    v = C.pv[:, 0:PV_N]
    P.dma("sp", v.ap, dr["pvec"], out_view=v)
    v = C.invc[:]
    P.dma("sp", v.ap, dr["invc"], out_view=v)
    v = C.invf2[:]
    P.dma("sp", v.ap, dr["invf2"], out_view=v)
    xsrc = dr["xT"].rearrange("(c p) s -> p c s", p=128)
    for c in range(KC):
        v = C.xres[:, c, :]
        P.dma("sp", v.ap, xsrc[:, c, :], out_view=v)
    for c in range(KC):
        for hf in range(2):
            v = C.xT[:, c, hf * 1024:(hf + 1) * 1024]
            P.dma("pool", v.ap, xsrc[:, c, hf * 1024:(hf + 1) * 1024], out_view=v)
    MSET(P, "dve", C.ones[:], 1.0)
    MSET(P, "dve", C.halfc[:], 0.5)
    MSET(P, "dve", C.nhalfc[:], -0.5)

    if any(l % 2 == 1 for l in layers):
        emit_rope(C)
    for l in layers:
        if l % 2 == 0:
            emit_even(C, l)
        else:
            emit_odd(C, l)
        emit_ln(C, lambda k, l=l: pv("lmg", l, k, 1), lambda k, l=l: pv("lmb", l, k, 1))
        emit_mlp(C, l)
        emit_ln(C, lambda k, l=l: pv("lfg", l, k, 1), lambda k, l=l: pv("lfb", l, k, 1))

    ysrc = yT.rearrange("(c p) s -> p c s", p=128)
    for c in range(KC):
        v = C.xres[:, c, :]
        P.dma("sp", ysrc[:, c, :], v.ap, in_view=v)
    for q in P.dma_sems:
        if P.count[q] > 0:
            P.wait_on("sp", q, P.count[q])
    P.emit()
    return nc


def emit_rope(C):
    P = C.P
    cv = Carver(C.scr)
    pi_ = cv.arr(I32, [64, S])
    ang = cv.arr(F32, [64, S])
    kk = cv.arr(F32, [64, S])
    y = cv.arr(F32, [64, S])
    acc = cv.arr(F32, [64, S])
    sn = cv.arr(F32, [64, S])
    v = pi_[:]
    P.dma("sp", v.ap, C.dr["pos"].partition_broadcast(64), out_view=v)
    CP(P, "dve", ang[:], pi_[:])
    TS(P, "dve", ang[:], ang[:], C.invf2[:, 0:1], ALU.mult)
    MAGIC = 12582912.0
    TS(P, "dve", kk[:], ang[:], 1.0 / (2 * math.pi), ALU.mult, MAGIC, ALU.add)
    TS(P, "dve", kk[:], kk[:], MAGIC, ALU.subtract)
    c1 = 6.28125
    c2 = float(np.float32(2 * math.pi - c1))
    c3 = float(2 * math.pi - c1 - c2)
    for c in (c1, c2, c3):
        STT(P, ang[:], kk[:], -c, ang[:], ALU.mult, ALU.add)
    TS(P, "dve", ang[:], ang[:], 0.5, ALU.mult)
    TT(P, "dve", y[:], ang[:], ang[:], ALU.mult)
    sc_ = [-1.0 / 6, 1.0 / 120, -1.0 / 5040, 1.0 / 362880, -1.0 / 39916800, 1.0 / 6227020800]
    TS(P, "dve", acc[:], y[:], sc_[5], ALU.mult)
    for k in (4, 3, 2, 1, 0):
        STT(P, acc[:], acc[:], sc_[k], y[:], ALU.add, ALU.mult)
    STT(P, sn[:], acc[:], 1.0, ang[:], ALU.add, ALU.mult)
    cc_ = [-0.5, 1.0 / 24, -1.0 / 720, 1.0 / 40320, -1.0 / 3628800, 1.0 / 479001600, -1.0 / 87178291200]
    TS(P, "dve", acc[:], y[:], cc_[6], ALU.mult)
    for k in (5, 4, 3, 2, 1, 0):
        STT(P, acc[:], acc[:], cc_[k], y[:], ALU.add, ALU.mult)
    TS(P, "dve", acc[:], acc[:], 1.0, ALU.add)
    STT(P, C.sins[:], sn[:], 2.0, acc[:], ALU.mult, ALU.mult)
    TT(P, "dve", y[:], sn[:], sn[:], ALU.mult)
    TS(P, "dve", C.cos2[:], y[:], -2.0, ALU.mult, 1.0, ALU.add)
    TS(P, "dve", C.sins[0:32, :], C.sins[0:32, :], -1.0, ALU.mult)


def emit_ln(C, g, b):
    P, PS = C.P, C.PS
    cv = Carver(C.scr)
    sq = cv.arr(BF16, [128, KC, 512])
    mean = cv.arr(F32, [128, 512])
    ve = cv.arr(F32, [128, 512])
    m2 = cv.arr(F32, [128, 512])
    tt_ = [cv.arr(F32, [128, 512]) for _ in range(3)]
    for nb in range(NB):
        blk = slice(nb * 512, (nb + 1) * 512)
        ACT(P, C.xT[:, :, blk], C.xres[:, :, blk], AF.Copy)
        ACT(P, sq[:], C.xres[:, :, blk], AF.Square)
        s1, s2 = PS.get(), PS.get()
        for k in range(KC):
            MM(P, s1[:], C.ones[:], C.xT[:, k, blk], k == 0, k == KC - 1)
        for k in range(KC):
            MM(P, s2[:], C.ones[:], sq[:, k, :], k == 0, k == KC - 1)
        TS(P, "dve", mean[:], s1[:], 1.0 / D, ALU.mult)
        TS(P, "dve", ve[:], s2[:], 1.0 / D, ALU.mult, LN_EPS, ALU.add)
        PS.put(s1); PS.put(s2)
        TT(P, "dve", m2[:], mean[:], mean[:], ALU.mult)
        TT(P, "dve", ve[:], ve[:], m2[:], ALU.subtract)
        ACT(P, ve[:], ve[:], AF.Ln)
        ACT(P, ve[:], ve[:], AF.Exp, scale=-0.5)
        for k in range(KC):
            t = tt_[k % 3]
            TT(P, "dve", t[:], C.xres[:, k, blk], mean[:], ALU.subtract)
            TT(P, "pool", t[:], t[:], ve[:], ALU.mult)
            ACT(P, C.xres[:, k, blk], t[:], AF.Identity, scale=g(k), bias=b(k))
            ACT(P, C.xT[:, k, blk], t[:], AF.Identity, scale=g(k), bias=b(k))


def emit_mlp(C, l):
    P, PS = C.P, C.PS
    cv = Carver(C.scr)
    hb = [cv.arr(BF16, [128, 4, S]) for _ in range(2)]
    tmp = [cv.arr(F32, [128, 512]) for _ in range(3)]
    w1 = C.dr["mlp_w1"][l].rearrange("(kc p) n -> p kc n", p=128)
    w2 = C.dr["mlp_w2"][l].rearrange("(kc p) n -> p kc n", p=128)
    st = {"ti": 0}

    def mlp1(c):
        (W1,) = C.wslot([(0, [KC, 512], w1[:, :, c * 512:(c + 1) * 512])])
        h = hb[c % 2]
        for m in range(4):
            for nb in range(NB):
                blk = slice(nb * 512, (nb + 1) * 512)
                ps = PS.get()
                for k in range(KC):
                    MM(P, ps[:], W1[:, k, m * 128:(m + 1) * 128], C.xT[:, k, blk], k == 0, k == KC - 1)
                t = tmp[st["ti"] % 3]; st["ti"] += 1
                ACT(P, t[:], ps[:], AF.Relu)
                PS.put(ps)
                TT(P, "pool", h[:, m, blk], t[:], t[:], ALU.mult)

    def mlp2(c):
        (W2,) = C.wslot([(0, [4, D], w2[:, c * 4:(c + 1) * 4, :])])
        h = hb[c % 2]
        for mo in range(KC):
            for nb in range(NB):
                blk = slice(nb * 512, (nb + 1) * 512)
                ps = PS.get()
                for k in range(4):
                    MM(P, ps[:], W2[:, k, mo * 128:(mo + 1) * 128], h[:, k, blk], k == 0, k == 3)
                if c == 0:
                    STT(P, C.xres[:, mo, blk], C.xres[:, mo, blk], ALPHA, ps[:], ALU.mult, ALU.add)
                else:
                    TT(P, "dve", C.xres[:, mo, blk], C.xres[:, mo, blk], ps[:], ALU.add)
                PS.put(ps)

    mlp1(0)
    for c in range(8):
        if c + 1 < 8:
            mlp1(c + 1)
        mlp2(c)


def emit_even(C, l):
    P, PS = C.P, C.PS
    j = l // 2
    pv, dv = C.pvf, C.dvf
    cv = Carver(C.scr)
    mixT = cv.arr(BF16, [128, 4, S])
    HW = 1024
    A = [cv.arr(F32, [128, 16 + HW]) for _ in range(2)]
    Bg = [cv.arr(F32, [128, HW]) for _ in range(2)]
    E0 = cv.arr(F32, [128, 16 + HW])
    E1 = cv.arr(F32, [128, 16 + HW])
    C2 = Arr(C.scr, E0.off, F32, [128, HW])
    R = Arr(C.scr, E1.off, F32, [128, HW])
    I_ = cv.arr(F32, [128, HW])
    T = cv.arr(F32, [128, HW])
    cb = [cv.arr(BF16, [128, HW]) for _ in range(2)]
    carry = cv.arr(F32, [128, 8])
    sm = cv.arr(F32, [128, 16])
    w_in = C.dr["even_w_in"][j].rearrange("(kc p) n -> p kc n", p=128)
    w_out = C.dr["even_w_out"][j].rearrange("(kc p) n -> p kc n", p=128)

    cfA, hcfA, hbaA, hbxA = dv(j, 0, 0, KC), dv(j, 1, 0, KC), dv(j, 2, 0, KC), dv(j, 3, 0, KC)
    ACT(P, cfA, pv("lam", j, 0, KC), AF.Exp, scale=-1.0)
    TS(P, "dve", cfA, cfA, 1.0, ALU.add)
    ACT(P, cfA, cfA, AF.Ln)
    TS(P, "dve", cfA, cfA, -8.0, ALU.mult)
    TS(P, "dve", hcfA, cfA, 0.5, ALU.mult)
    TS(P, "dve", hbaA, pv("ba", j, 0, KC), 0.5, ALU.mult)
    TS(P, "dve", hbxA, pv("bx", j, 0, KC), 0.5, ALU.mult)

    GK = math.sqrt(2.0 / math.pi)
    items = [(u, hf) for u in range(12) for hf in range(2)]
    W = {}

    def stageP(i):
        u, hf = items[i]
        a = A[i % 2]
        if hf == 0:
            if u < 4:
                W[u] = C.wslot([(0, [KC, 128], w_in[:, :, u * 128:(u + 1) * 128]),
                                (1024, [128], C.dr["pool_w"][j, u])])
            else:
                h = u - 4
                W[u] = C.wslot([
                    (0, [KC, 128], w_in[:, :, 512 + h * 128:512 + (h + 1) * 128]),
                    (1024, [KC, 128], w_in[:, :, 1536 + h * 128:1536 + (h + 1) * 128]),
                    (2048, [128], C.dr["lru_w_a"][j, h]),
                    (2176, [128], C.dr["lru_w_x"][j, h])])
            MSET(P, "dve", a[:, 0:16], 0.0)
        else:
            CP(P, "dve", a[:, 0:16], A[(i - 1) % 2][:, HW:HW + 16])
        Wu = W[u][0]
        for q in range(2):
            blk = slice(hf * HW + q * 512, hf * HW + (q + 1) * 512)
            ps = PS.get()
            for k in range(KC):
                MM(P, ps[:], Wu[:, k, :], C.xT[:, k, blk], k == 0, k == KC - 1)
            ACT(P, a[:, 16 + q * 512:16 + (q + 1) * 512], ps[:], AF.Copy)
            PS.put(ps)
        if u >= 4:
            Wg = W[u][1]
            for q in range(2):
                blk = slice(hf * HW + q * 512, hf * HW + (q + 1) * 512)
                ps = PS.get()
                for k in range(KC):
                    MM(P, ps[:], Wg[:, k, :], C.xT[:, k, blk], k == 0, k == KC - 1)
                ACT(P, Bg[i % 2][:, q * 512:(q + 1) * 512], ps[:], AF.Copy)
                PS.put(ps)

    def stageE(i):
        u, hf = items[i]
        a = A[i % 2]
        ui = u % 4
        cbi = cb[i % 2]
        if u < 4:
            g = u
            Wp = W[u][1]
            w = 2 << g
            src, sh, lo, ei = a, 1, 0, 0
            exts = [E0, E1]
            while sh < w:
                dst = exts[ei % 2]; ei += 1
                lo2 = lo + sh
                TT(P, "dve", dst[:, lo2:16 + HW], src[:, lo2:16 + HW], src[:, lo2 - sh:16 + HW - sh], ALU.add)
                src, lo, sh = dst, lo2, sh * 2
            STT(P, T[:], src[:, 16:16 + HW], 1.0 / w, a[:, 16:16 + HW], ALU.mult, ALU.subtract)
            if hf == 0:
                TT(P, "dve", sm[:], src[:, 16:32], C.invc[:, g * 16:(g + 1) * 16], ALU.mult)
                TT(P, "dve", T[:, 0:16], sm[:], a[:, 16:32], ALU.subtract)
            ACT(P, cbi[:], T[:], AF.Copy)
            for q in range(2):
                blk = slice(hf * HW + q * 512, hf * HW + (q + 1) * 512)
                ps = PS.get()
                MM(P, ps[:], Wp[:], cbi[:, q * 512:(q + 1) * 512], True, True)
                ACT(P, mixT[:, ui, blk], ps[:], AF.Copy, scale=pv("pscale", j, g, 1))
                PS.put(ps)
        else:
            h = u - 4
            Wa, Wx = W[u][2], W[u][3]
            bg = Bg[i % 2]
            cw = lambda k: pv("convw", j, k * KC + h, 1)
            TS(P, "dve", C2[:], a[:, 16:16 + HW], cw(3), ALU.mult, pv("convb", j, h, 1), ALU.add)
            for k in (2, 1, 0):
                STT(P, C2[:], a[:, 13 + k:13 + k + HW], cw(k), C2[:], ALU.mult, ALU.add)
            ACT(P, cbi[:], C2[:], AF.Copy)
            for q in range(2):
                ps = PS.get()
                MM(P, ps[:], Wa[:], cbi[:, q * 512:(q + 1) * 512], True, True)
                ACT(P, R[:, q * 512:(q + 1) * 512], ps[:], AF.Tanh, scale=0.5, bias=dv(j, 2, h, 1))
                PS.put(ps)
                ps = PS.get()
                MM(P, ps[:], Wx[:], cbi[:, q * 512:(q + 1) * 512], True, True)
                ACT(P, I_[:, q * 512:(q + 1) * 512], ps[:], AF.Tanh, scale=0.5, bias=dv(j, 3, h, 1))
                PS.put(ps)
            ACT(P, T[:], R[:], AF.Exp, scale=dv(j, 0, h, 1), bias=dv(j, 0, h, 1))
            ACT(P, R[:], R[:], AF.Exp, scale=dv(j, 1, h, 1), bias=dv(j, 1, h, 1))
            TS(P, "dve", T[:], T[:], -1.0, ALU.mult, 1.0, ALU.add)
            TS(P, "dve", T[:], T[:], 1e-18, ALU.max)
            ACT(P, T[:], T[:], AF.Ln)
            ACT(P, T[:], T[:], AF.Exp, scale=0.5)
            STT(P, I_[:], I_[:], 1.0, T[:], ALU.add, ALU.mult)
            STT(P, I_[:], I_[:], 0.5, C2[:], ALU.mult, ALU.mult)
            Hb = T
            if hf == 0:
                P.op("dve", lambda e, o=Hb[:], a_=R[:], b_=I_[:]: e.tensor_tensor_scan(
                    out=o.ap, data0=a_.ap, data1=b_.ap, initial=0.0, op0=ALU.mult, op1=ALU.add),
                    reads=[R[:], I_[:]], writes=[Hb[:]])
            else:
                init = carry[:, h:h + 1]
                P.op("dve", lambda e, o=Hb[:], a_=R[:], b_=I_[:], i0=init: e.tensor_tensor_scan(
                    out=o.ap, data0=a_.ap, data1=b_.ap, initial=i0.ap, op0=ALU.mult, op1=ALU.add),
                    reads=[R[:], I_[:], init], writes=[Hb[:]])
            CP(P, "dve", carry[:, h:h + 1], Hb[:, HW - 1:HW])
            Cg = C2
            ACT(P, Cg[:], bg[:], AF.Square)
            TS(P, "dve", Cg[:], Cg[:], 0.044715, ALU.mult, 1.0, ALU.add)
            TT(P, "pool", Cg[:], Cg[:], bg[:], ALU.mult)
            ACT(P, Cg[:], Cg[:], AF.Tanh, scale=GK)
            STT(P, bg[:], Cg[:], 1.0, bg[:], ALU.add, ALU.mult)
            STT(P, mixT[:, ui, hf * HW:(hf + 1) * HW], bg[:], 0.5, Hb[:], ALU.mult, ALU.mult)
        if ui == 3 and hf == 1:
            grp = u // 4
            (Wo,) = C.wslot([(0, [4, D], w_out[:, grp * 4:(grp + 1) * 4, :])])
            for mo in range(KC):
                for nb in range(NB):
                    blk = slice(nb * 512, (nb + 1) * 512)
                    ps = PS.get()
                    for k in range(4):
                        MM(P, ps[:], Wo[:, k, mo * 128:(mo + 1) * 128], mixT[:, k, blk], k == 0, k == 3)
                    if grp == 0:
                        STT(P, C.xres[:, mo, blk], C.xres[:, mo, blk], ALPHA, ps[:], ALU.mult, ALU.add)
                    else:
                        TT(P, "dve", C.xres[:, mo, blk], C.xres[:, mo, blk], ps[:], ALU.add)
                    PS.put(ps)

    stageP(0)
    for i in range(len(items)):
        if i + 1 < len(items):
            stageP(i + 1)
        stageE(i)


def emit_odd(C, l):
    P, PS = C.P, C.PS
    j = l // 2
    pv = C.pvf
    cv = Carver(C.scr)
    cqn = cv.arr(BF16, [128, 3, S])
    ckvn = cv.arr(BF16, [128, 2, S])
    kpe = cv.arr(BF16, [64, S])
    qn = cv.arr(BF16, [128, S])
    qpe = cv.arr(BF16, [64, S])
    kn = cv.arr(BF16, [128, S])
    V = cv.arr(BF16, [128, 16, 128])
    E = [cv.arr(BF16, [128, 512]) for _ in range(3)]
    sq = cv.arr(BF16, [128, 3, 512])
    rs = [cv.arr(F32, [128, 512]) for _ in range(2)]
    r1 = [cv.arr(F32, [64, 512]) for _ in range(1)]
    r2 = [cv.arr(F32, [64, 512]) for _ in range(1)]
    oT = Arr(C.xT.buf, 0, BF16, [128, KC, S])
    wd = C.dr["mla_w_down"][j].rearrange("(kc p) n -> p kc n", p=128)
    wds = C.dr["w_down_sw"][j].rearrange("(kc p) n -> p kc n", p=128)
    wqb = C.dr["mla_w_qb"][j].rearrange("(kc p) n -> p kc n", p=128)
    wqs = C.dr["w_qb_sw"][j].rearrange("(kc p) n -> p kc n", p=128)
    wkv = C.dr["mla_w_kvb"][j].rearrange("(kc p) n -> p kc n", p=128)
    wo = C.dr["mla_w_o"][j].rearrange("(kc p) n -> p kc n", p=128)
    SCALE = 192.0 ** -0.5

    (Wd1,) = C.wslot([(0, [KC, 384], wd[:, :, 0:384])])
    Wd2, Wd3 = C.wslot([(0, [KC, 320], wd[:, :, 384:704]), (2560, [KC, 64], wds)])

    def rope(dst, pa, pb, blk, ri):
        TT(P, "dve", r1[ri][:], pa, C.cos2[:, blk], ALU.mult)
        TT(P, "dve", r2[ri][:], pb, C.sins[:, blk], ALU.mult)
        TT(P, "dve", dst, r1[ri][:], r2[ri][:], ALU.add)

    def rmsn(dst, W, ncol, nch, gname, blk, ri):
        banks = []
        for m in range(nch):
            ps = PS.get()
            for k in range(KC):
                MM(P, ps[:], W[:, k, m * 128:(m + 1) * 128], C.xT[:, k, blk], k == 0, k == KC - 1)
            ACT(P, sq[:, m, :], ps[:], AF.Square)
            banks.append(ps)
        s2 = PS.get()
        for m in range(nch):
            MM(P, s2[:], C.ones[:], sq[:, m, :], m == 0, m == nch - 1)
        ve = rs[ri]
        TS(P, "dve", ve[:], s2[:], 1.0 / ncol, ALU.mult, RMS_EPS, ALU.add)
        PS.put(s2)
        ACT(P, ve[:], ve[:], AF.Ln)
        ACT(P, ve[:], ve[:], AF.Exp, scale=-0.5)
        for m in range(nch):
            STT(P, dst[:, m, blk], banks[m][:], pv(gname, j, m, 1), ve[:], ALU.mult, ALU.mult)
            PS.put(banks[m])

    for nb in range(NB):
        blk = slice(nb * 512, (nb + 1) * 512)
        rmsn(cqn, Wd1, 384, 3, "qg", blk, 0)
        rmsn(ckvn, Wd2, 256, 2, "kvg", blk, 1)
        pa, pb = PS.get(), PS.get()
        for k in range(KC):
            MM(P, pa[0:64, :], Wd2[:, k, 256:320], C.xT[:, k, blk], k == 0, k == KC - 1)
        for k in range(KC):
            MM(P, pb[0:64, :], Wd3[:, k, :], C.xT[:, k, blk], k == 0, k == KC - 1)
        rope(kpe[:, blk], pa[0:64, :], pb[0:64, :], blk, 0)
        PS.put(pa); PS.put(pb)

    st_e = {"ei": 0}
    for h in range(8):
        Wq, Wqs, Wkv = C.wslot([(0, [3, 192], wqb[:, :, h * 192:(h + 1) * 192]),
                                (576, [3, 64], wqs[:, :, h * 64:(h + 1) * 64]),
                                (768, [2, 256], wkv[0:128, :, h * 256:(h + 1) * 256])])
        for nb in range(NB):
            blk = slice(nb * 512, (nb + 1) * 512)
            ps = PS.get()
            for k in range(3):
                MM(P, ps[:], Wq[:, k, 0:128], cqn[:, k, blk], k == 0, k == 2)
            ACT(P, qn[:, blk], ps[:], AF.Copy)
            PS.put(ps)
            pa, pb = PS.get(), PS.get()
            for k in range(3):
                MM(P, pa[0:64, :], Wq[:, k, 128:192], cqn[:, k, blk], k == 0, k == 2)
            for k in range(3):
                MM(P, pb[0:64, :], Wqs[:, k, :], cqn[:, k, blk], k == 0, k == 2)
            rope(qpe[:, blk], pa[0:64, :], pb[0:64, :], blk, 0)
            PS.put(pa); PS.put(pb)
            ps = PS.get()
            for k in range(2):
                MM(P, ps[:], Wkv[:, k, 0:128], ckvn[:, k, blk], k == 0, k == 1)
            ACT(P, kn[:, blk], ps[:], AF.Copy)
            PS.put(ps)
        for t4 in range(4):
            ps = PS.get()
            for tt in range(4):
                t = t4 * 4 + tt
                for k in range(2):
                    MM(P, ps[:, tt * 128:(tt + 1) * 128], ckvn[:, k, t * 128:(t + 1) * 128], Wkv[:, k, 128:256],
                       k == 0, k == 1, inc=(k == 1 and tt == 3))
            ACT(P, V[:, t4 * 4:(t4 + 1) * 4, :], ps[:], AF.Copy)
            PS.put(ps)
        for qb in range(NB):
            num, den = PS.get(), PS.get()
            nkt = 4 * qb + 4
            def sc_stage(kt):
                q0 = max(512 * qb, 128 * kt)
                N = 512 * qb + 512 - q0
                c0 = q0 - 512 * qb
                kts = slice(kt * 128, (kt + 1) * 128)
                sc = PS.get()
                MM(P, sc[:, 0:N], kn[:, kts], qn[:, q0:q0 + N], True, False)
                MM(P, sc[:, 0:N], kpe[:, kts], qpe[:, q0:q0 + N], False, True)
                Eb = E[st_e["ei"] % 3]; st_e["ei"] += 1
                ACT(P, Eb[:, 0:N], sc[:, 0:N], AF.Exp, scale=SCALE)
                PS.put(sc)
                if kt >= 4 * qb:
                    MSET(P, "dve", Eb[64:128, 0:64], 0.0)
                return Eb, N, c0

            pend = sc_stage(0)
            for kt in range(nkt):
                nxt = sc_stage(kt + 1) if kt + 1 < nkt else None
                Eb, N, c0 = pend
                MM(P, num[:, c0:c0 + N], V[:, kt, :], Eb[:, 0:N], kt == 0, kt == nkt - 1, inc=False)
                MM(P, den[:, c0:c0 + N], C.ones[:], Eb[:, 0:N], kt == 0, kt == nkt - 1, inc=True)
                pend = nxt
            rc = rs[qb % 2]
            P.op("dve", lambda e, o=rc[:], i=den[:]: e.reciprocal(out=o.ap, in_=i.ap), reads=[den[:]], writes=[rc[:]])
            TT(P, "dve", oT[:, h, qb * 512:(qb + 1) * 512], num[:], rc[:], ALU.mult)
            PS.put(num); PS.put(den)

    Wo1 = C.wslot([(0, [4, D], wo[:, 0:4, :])])[0]
    Wo2 = C.wslot([(0, [4, D], wo[:, 4:8, :])])[0]
    for mo in range(KC):
        for nb in range(NB):
            blk = slice(nb * 512, (nb + 1) * 512)
            ps = PS.get()
            for k in range(KC):
                W = Wo1 if k < 4 else Wo2
                MM(P, ps[:], W[:, k % 4, mo * 128:(mo + 1) * 128], oT[:, k, blk], k == 0, k == KC - 1)
            STT(P, C.xres[:, mo, blk], C.xres[:, mo, blk], ALPHA, ps[:], ALU.mult, ALU.add)
            PS.put(ps)


_NC_CACHE = {}


def _host_prep(inp):
    f = np.float32
    pvec = np.zeros((128, PV_N), f)

    def put(key, vec):
        v = np.asarray(vec, f)
        n = v.shape[0] // 128
        o = PV_COLS[key]
        pvec[:, o:o + n] = v.reshape(n, 128).T

    for l in range(DEPTH):
        put(("lmg", l), inp["ln_mix_g"][l]); put(("lmb", l), inp["ln_mix_b"][l])
        put(("lfg", l), inp["ln_ffn_g"][l]); put(("lfb", l), inp["ln_ffn_b"][l])
    for j in range(2):
        put(("pscale", j), inp["pool_scale"][j])
        cw = np.asarray(inp["lru_conv_w"][j], f)
        o = PV_COLS[("convw", j)]
        for k in range(4):
            pvec[:, o + k * KC:o + (k + 1) * KC] = cw[k].reshape(KC, 128).T
        put(("convb", j), inp["lru_conv_b"][j]); put(("ba", j), inp["lru_b_a"][j])
        put(("bx", j), inp["lru_b_x"][j]); put(("lam", j), inp["lru_lambda"][j])
        put(("qg", j), inp["mla_q_norm_g"][j]); put(("kvg", j), inp["mla_kv_norm_g"][j])
    inv_freq = (10000.0 ** (-np.arange(0, 64, 2, dtype=f) / f(64))).astype(f)
    invf2 = np.concatenate([inv_freq, inv_freq]).reshape(64, 1).astype(f)
    invc = np.zeros((128, 64), f)
    for g, w in enumerate((2, 4, 8, 16)):
        invc[:, g * 16:(g + 1) * 16] = (1.0 / np.minimum(np.arange(16) + 1, w)).astype(f)[None, :]
    wd = np.asarray(inp["mla_w_down"], f)
    w_down_sw = np.ascontiguousarray(np.concatenate([wd[:, :, 672:704], wd[:, :, 640:672]], axis=2))
    wq = np.asarray(inp["mla_w_qb"], f).reshape(2, 384, 8, 192)
    w_qb_sw = np.ascontiguousarray(np.concatenate([wq[..., 160:192], wq[..., 128:160]], axis=3).reshape(2, 384, 512))
    shared = {"pvec": pvec, "invf2": invf2, "invc": invc, "w_down_sw": w_down_sw, "w_qb_sw": w_qb_sw}
    for nm in ("even_w_in", "pool_w", "lru_w_a", "lru_w_x", "even_w_out", "mla_w_down", "mla_w_qb",
               "mla_w_kvb", "mla_w_o", "mlp_w1", "mlp_w2"):
        shared[nm] = np.ascontiguousarray(np.asarray(inp[nm], f))
    return shared


LAUNCH_GROUPS = [(0, 1, 2, 3)]


def kernel(**inp):
    shared = _host_prep(inp)
    x = np.asarray(inp["x"], np.float32)
    pos = np.asarray(inp["positions"], np.int32)
    cur = [np.ascontiguousarray(x[b].T) for b in range(NCORES)]
    for grp in LAUNCH_GROUPS:
        if grp not in _NC_CACHE:
            _NC_CACHE[grp] = build(grp)
        nc = _NC_CACHE[grp]
        in_maps = []
        for b in range(NCORES):
            m = dict(shared)
            m["xT"] = cur[b]
            m["pos"] = np.ascontiguousarray(pos[b][None, :])
            in_maps.append(m)
        res = run_bass_kernel_spmd(nc, in_maps, core_ids=list(range(NCORES)))
        cur = [np.ascontiguousarray(res.results[b]["yT"]) for b in range(NCORES)]
    return np.stack([c.T for c in cur], axis=0).astype(np.float32)
```

```python
import math
import numpy as np
import concourse.bass as bass
import concourse.mybir as mybir
from concourse.bass_utils import run_bass_kernel_spmd

F32 = mybir.dt.float32
BF16 = mybir.dt.bfloat16
I32 = mybir.dt.int32
AF = mybir.ActivationFunctionType
ALU = mybir.AluOpType

NCORES = 8
S = 2048
D = 1024
DEPTH = 4
KC = D // 128
NB = S // 512
ALPHA = (2 * DEPTH) ** 0.25
LN_EPS = 1e-5
RMS_EPS = 1e-6


class View:
    __slots__ = ("buf", "ap", "ivs")

    def __init__(self, buf, ap, ivs):
        self.buf, self.ap, self.ivs = buf, ap, ivs


def _merge(ivs):
    ivs = sorted(ivs)
    out = [list(ivs[0])]
    for lo, hi in ivs[1:]:
        if lo <= out[-1][1]:
            out[-1][1] = max(out[-1][1], hi)
        else:
            out.append([lo, hi])
    return tuple((a, b) for a, b in out)


def _overlap(a, b):
    for lo, hi in a:
        for lo2, hi2 in b:
            if lo < hi2 and lo2 < hi:
                return True
    return False


def _covers(a, b):
    for lo2, hi2 in b:
        ok = False
        for lo, hi in a:
            if lo <= lo2 and hi2 <= hi:
                ok = True
                break
        if not ok:
            return False
    return True


class Buf:
    def __init__(self, prog, name, shape, dtype, space="sbuf"):
        self.prog, self.name, self.shape, self.dtype = prog, name, list(shape), dtype
        self.esz = 2 if dtype == BF16 else 4
        nc = prog.nc
        if space == "sbuf":
            self.t = nc.alloc_sbuf_tensor(name, self.shape, dtype)
        else:
            self.t = nc.alloc_psum_tensor(name, self.shape, dtype)
        st = []
        acc = 1
        for n in reversed(self.shape[1:]):
            st.append(acc)
            acc *= n
        self.strides = list(reversed(st))
        self.hist = []

    def __getitem__(self, idx):
        if not isinstance(idx, tuple):
            idx = (idx,)
        ap = self.t[idx]
        fidx = list(idx[1:]) + [slice(None)] * (len(self.shape) - len(idx))
        rngs = []
        for i, n in zip(fidx, self.shape[1:]):
            if isinstance(i, int):
                rngs.append((i, i + 1))
            else:
                a, b, _ = i.indices(n)
                rngs.append((a, b))
        outer = rngs[:-1]
        cnt = 1
        for a, b in outer:
            cnt *= (b - a)
        la, lb = rngs[-1]
        if cnt > 64:
            lo = sum(a * s for (a, b), s in zip(rngs, self.strides))
            hi = sum((b - 1) * s for (a, b), s in zip(rngs, self.strides)) + 1
            ivs = ((lo * self.esz, hi * self.esz),)
        else:
            starts = [0]
            for (a, b), s in zip(outer, self.strides[:-1]):
                starts = [o + j * s for o in starts for j in range(a, b)]
            ivs = _merge([((o + la) * self.esz, (o + lb) * self.esz) for o in starts])
        return View(self, ap, ivs)


class Prog:
    ENGS = ("pe", "act", "dve", "pool", "sp")

    def __init__(self, nc, n_dma_sems=24):
        self.nc = nc
        self.streams = {e: [] for e in self.ENGS}
        self.sems = {}
        self.count = {}
        self.waited = {e: {} for e in self.ENGS}
        for e in self.ENGS:
            self.sems[e] = nc.alloc_semaphore(name="s_" + e)
            self.count[e] = 0
        self.dma_sems = []
        for i in range(n_dma_sems):
            nm = "q%d" % i
            self.sems[nm] = nc.alloc_semaphore(name="s_" + nm)
            self.count[nm] = 0
            self.dma_sems.append(nm)
        self.dma_rr = 0
        self.n_ops = 0

    def _deps(self, eng, reads, writes):
        need = {}
        for v in reads:
            for rec in v.buf.hist:
                if rec[2] and _overlap(rec[3], v.ivs):
                    if not (eng == "pe" and rec[0] == "pe"):
                        if need.get(rec[0], 0) < rec[1]:
                            need[rec[0]] = rec[1]
        for v in writes:
            for rec in v.buf.hist:
                if _overlap(rec[3], v.ivs):
                    if not (eng == "pe" and rec[0] == "pe"):
                        if need.get(rec[0], 0) < rec[1]:
                            need[rec[0]] = rec[1]
        return need

    def _record(self, who, cnt, reads, writes):
        for v in reads:
            h = v.buf.hist
            for rec in h:
                if (not rec[2]) and rec[0] == who and rec[3] == v.ivs:
                    rec[1] = max(rec[1], cnt)
                    break
            else:
                h.append([who, cnt, False, v.ivs])
        for v in writes:
            h = v.buf.hist
            h[:] = [rec for rec in h if not _covers(v.ivs, rec[3])]
            h.append([who, cnt, True, v.ivs])

    def _waits(self, eng, need):
        w = []
        for f, c in need.items():
            if self.waited[eng].get(f, 0) < c:
                self.waited[eng][f] = c
                w.append((f, c))
        return w

    def op(self, eng, fn, reads=(), writes=(), inc=True):
        need = self._deps(eng, reads, writes)
        waits = self._waits(eng, need)
        cnt = self.count[eng] + 1
        if inc:
            self.count[eng] = cnt
        self._record(eng, cnt, reads, writes)
        self.streams[eng].append((waits, fn, (eng, 1) if inc else None))
        self.n_ops += 1

    def dma(self, queue, out, in_, out_view=None, in_view=None, **kw):
        reads = [in_view] if in_view is not None else []
        writes = [out_view] if out_view is not None else []
        need = self._deps(queue, reads, writes)
        q = self.dma_sems[self.dma_rr % len(self.dma_sems)]
        self.dma_rr += 1
        if self.count[q] > 0:
            need[q] = max(need.get(q, 0), self.count[q])
        waits = self._waits(queue, need)
        self.count[q] += 16
        cnt = self.count[q]
        self._record(q, cnt, reads, writes)
        self.streams[queue].append((waits, lambda e: e.dma_start(out=out, in_=in_, **kw), (q, 16)))
        self.n_ops += 1
        return (q, cnt)

    def wait_on(self, eng, who, cnt):
        w = self._waits(eng, {who: cnt})
        if w:
            self.streams[eng].append((w, None, None))

    def emit(self):
        nc = self.nc
        engobj = {"pe": "tensor", "act": "scalar", "dve": "vector", "pool": "gpsimd", "sp": "sync"}
        with nc.Block() as block:
            for e in self.ENGS:
                stream = self.streams[e]

                def body(eng, stream=stream):
                    for waits, fn, inc in stream:
                        for (f, c) in waits:
                            eng.wait_ge(self.sems[f], c)
                        if fn is None:
                            continue
                        ins = fn(eng)
                        if inc is not None:
                            ins.then_inc(self.sems[inc[0]], inc[1])

                getattr(block, engobj[e])(body)


class Arr:
    def __init__(self, buf, byte_off, dtype, shape):
        self.buf, self.off, self.dtype, self.shape = buf, byte_off, dtype, list(shape)
        self.esz = 2 if dtype == BF16 else 4
        n = 1
        for d in shape[1:]:
            n *= d
        assert byte_off % 4 == 0 and (n * self.esz) % 4 == 0
        lo, hi = byte_off // 4, (byte_off + n * self.esz) // 4
        assert hi <= buf.shape[1], (buf.name, hi, buf.shape)
        ap = buf.t[0:shape[0], lo:hi]
        if dtype != F32:
            ap = ap.bitcast(dtype)
        fd = shape[1:]
        if len(fd) == 2:
            ap = ap.rearrange("p (a b) -> p a b", a=fd[0])
        elif len(fd) == 3:
            ap = ap.rearrange("p (a b c) -> p a b c", a=fd[0], b=fd[1])
        self.full = ap
        st, acc = [], 1
        for d in reversed(fd):
            st.append(acc)
            acc *= d
        self.strides = list(reversed(st))
        self.nbytes = n * self.esz

    def __getitem__(self, idx):
        if not isinstance(idx, tuple):
            idx = (idx,)
        ap = self.full[idx]
        fidx = list(idx[1:]) + [slice(None)] * (len(self.shape) - len(idx))
        rngs = []
        for i, n in zip(fidx, self.shape[1:]):
            if isinstance(i, int):
                rngs.append((i, i + 1))
            else:
                a, b, _ = i.indices(n)
                rngs.append((a, b))
        outer = rngs[:-1]
        cnt = 1
        for a, b in outer:
            cnt *= (b - a)
        la, lb = rngs[-1]
        if cnt > 64:
            lo = sum(a * s for (a, b), s in zip(rngs, self.strides))
            hi = sum((b - 1) * s for (a, b), s in zip(rngs, self.strides)) + 1
            ivs = ((self.off + lo * self.esz, self.off + hi * self.esz),)
        else:
            starts = [0]
            for (a, b), s in zip(outer, self.strides[:-1]):
                starts = [o + j * s for o in starts for j in range(a, b)]
            ivs = _merge([(self.off + (o + la) * self.esz, self.off + (o + lb) * self.esz) for o in starts])
        return View(self.buf, ap, ivs)


class Carver:
    def __init__(self, buf):
        self.buf, self.pos = buf, 0

    def arr(self, dtype, shape):
        a = Arr(self.buf, self.pos, dtype, shape)
        self.pos += (a.nbytes + 31) // 32 * 32
        assert self.pos <= self.buf.shape[1] * 4, (self.buf.name, self.pos)
        return a


class PsumPool:
    def __init__(self, prog):
        self.banks = [Buf(prog, "psb%d" % i, [128, 512], F32, space="psum") for i in range(8)]
        self.free = list(range(8))

    def get(self):
        assert self.free, "out of PSUM banks"
        return self.banks[self.free.pop(0)]

    def put(self, b):
        self.free.append(self.banks.index(b))


def MM(P, out, lhsT, rhs, start, stop, inc=None):
    if inc is None:
        inc = stop
    P.op("pe", lambda e: e.matmul(out.ap, lhsT=lhsT.ap, rhs=rhs.ap, start=start, stop=stop),
         reads=[lhsT, rhs], writes=[out], inc=inc)


def ACT(P, out, in_, func, scale=1.0, bias=None):
    reads = [in_]
    kw = {}
    if isinstance(scale, View):
        reads.append(scale)
        kw["scale"] = scale.ap
    else:
        kw["scale"] = float(scale)
    if isinstance(bias, View):
        reads.append(bias)
        kw["bias"] = bias.ap
    elif bias is not None:
        kw["bias"] = float(bias)
    P.op("act", lambda e: e.activation(out=out.ap, in_=in_.ap, func=func, **kw), reads=reads, writes=[out])


def TS(P, eng, out, in0, s1, op0, s2=None, op1=None):
    reads = [in0]
    a1 = s1
    if isinstance(s1, View):
        reads.append(s1)
        a1 = s1.ap
    a2 = s2
    if isinstance(s2, View):
        reads.append(s2)
        a2 = s2.ap
    if op1 is None:
        P.op(eng, lambda e: e.tensor_scalar(out=out.ap, in0=in0.ap, scalar1=a1, scalar2=None, op0=op0),
             reads=reads, writes=[out])
    else:
        P.op(eng, lambda e: e.tensor_scalar(out=out.ap, in0=in0.ap, scalar1=a1, scalar2=a2, op0=op0, op1=op1),
             reads=reads, writes=[out])


def TT(P, eng, out, in0, in1, op):
    P.op(eng, lambda e: e.tensor_tensor(out=out.ap, in0=in0.ap, in1=in1.ap, op=op), reads=[in0, in1], writes=[out])


def STT(P, out, in0, sc, in1, op0, op1):
    reads = [in0, in1]
    a = sc
    if isinstance(sc, View):
        reads.append(sc)
        a = sc.ap
    P.op("dve", lambda e: e.scalar_tensor_tensor(out=out.ap, in0=in0.ap, scalar=a, in1=in1.ap, op0=op0, op1=op1),
         reads=reads, writes=[out])


def CP(P, eng, out, in_):
    P.op(eng, lambda e: e.tensor_copy(out=out.ap, in_=in_.ap), reads=[in_], writes=[out])


def MSET(P, eng, out, val):
    P.op(eng, lambda e: e.memset(out.ap, val), writes=[out])


def _pvec_layout():
    cols = {}
    n = 0
    for l in range(DEPTH):
        for nm in ("lmg", "lmb", "lfg", "lfb"):
            cols[(nm, l)] = n
            n += KC
    for j in range(2):
        cols[("pscale", j)] = n; n += 4
        cols[("convw", j)] = n; n += 4 * KC
        for nm in ("convb", "ba", "bx", "lam"):
            cols[(nm, j)] = n; n += KC
        cols[("qg", j)] = n; n += 3
        cols[("kvg", j)] = n; n += 2
    return cols, n


PV_COLS, PV_N = _pvec_layout()
DV_N = 2 * 4 * KC


class Ctx:
    pass


def build(layers=(0, 1, 2, 3)):
    nc = bass.Bass("TRN2", target_bir_lowering=False)
    P = Prog(nc)
    C = Ctx()
    C.P, C.nc = P, nc
    dr = {}

    def din(name, shape, dt=F32):
        dr[name] = nc.dram_tensor(name, list(shape), dt, kind="ExternalInput").ap()
        return dr[name]

    din("xT", [D, S]); din("pos", [1, S], I32); din("pvec", [128, PV_N]); din("invf2", [64, 1]); din("invc", [128, 64])
    din("even_w_in", [2, D, 2560]); din("pool_w", [2, 4, 128, 128]); din("lru_w_a", [2, 8, 128, 128])
    din("lru_w_x", [2, 8, 128, 128]); din("even_w_out", [2, 1536, D])
    din("mla_w_down", [2, D, 704]); din("w_down_sw", [2, D, 64]); din("mla_w_qb", [2, 384, 1536])
    din("w_qb_sw", [2, 384, 512]); din("mla_w_kvb", [2, 256, 2048]); din("mla_w_o", [2, D, D])
    din("mlp_w1", [DEPTH, D, 4096]); din("mlp_w2", [DEPTH, 4096, D])
    yT = nc.dram_tensor("yT", [D, S], F32, kind="ExternalOutput").ap()
    C.dr = dr

    xres_b = Buf(P, "xres", [128, KC * S], F32)
    xt_b = Buf(P, "xtb", [128, KC * S // 2], F32)
    wr_b = [Buf(P, "wring%d" % i, [128, 2048], F32) for i in range(4)]
    rope_b = Buf(P, "rope", [128, 2 * S], F32)
    cst_b = Buf(P, "cst", [128, PV_N + DV_N + 64 + 8 + 64 + 512 + 512 + 32], F32)
    scr_b = Buf(P, "scr", [128, 14080], F32)
    C.xres = Arr(xres_b, 0, F32, [128, KC, S])
    C.xT = Arr(xt_b, 0, BF16, [128, KC, S])
    cc = Carver(cst_b)
    C.pv = cc.arr(F32, [128, PV_N + DV_N])
    C.invc = cc.arr(F32, [128, 64])
    C.invf2 = cc.arr(F32, [64, 1])
    C.ones = cc.arr(BF16, [128, 128])
    C.halfc = cc.arr(F32, [128, 512])
    C.onec = cc.arr(F32, [128, 8])
    C.nhalfc = cc.arr(F32, [128, 512])
    C.cos2 = Arr(rope_b, 0, F32, [64, S])
    C.sins = Arr(rope_b, S * 4, F32, [64, S])
    C.scr = scr_b
    C.PS = PsumPool(P)
    C.wr_b = wr_b
    C.wi = 0

    def pv(nm, idx, c0=0, n=1, p=128):
        o = PV_COLS[(nm, idx)] + c0
        return C.pv[0:p, o:o + n]

    def dv(j, which, c0=0, n=1):
        o = PV_N + j * 4 * KC + which * KC + c0
        return C.pv[:, o:o + n]

    C.pvf, C.dvf = pv, dv

    def wslot(loads):
        buf = wr_b[C.wi % len(wr_b)]
        C.wi += 1
        arrs = []
        for off, shp, src in loads:
            a = Arr(buf, off * 2, BF16, [128] + list(shp))
            v = a[:]
            P.dma("pool", v.ap, src, out_view=v)
            arrs.append(a)
        return arrs

    C.wslot = wslot

    v = C.pv[:, 0:PV_N]
    P.dma("sp", v.ap, dr["pvec"], out_view=v)
    v = C.invc[:]
    P.dma("sp", v.ap, dr["invc"], out_view=v)
    v = C.invf2[:]
    P.dma("sp", v.ap, dr["invf2"], out_view=v)
    xsrc = dr["xT"].rearrange("(c p) s -> p c s", p=128)
    for c in range(KC):
        v = C.xres[:, c, :]
        P.dma("sp", v.ap, xsrc[:, c, :], out_view=v)
    for c in range(KC):
        for hf in range(2):
            v = C.xT[:, c, hf * 1024:(hf + 1) * 1024]
            P.dma("pool", v.ap, xsrc[:, c, hf * 1024:(hf + 1) * 1024], out_view=v)
    MSET(P, "dve", C.ones[:], 1.0)
    MSET(P, "dve", C.halfc[:], 0.5)
    MSET(P, "dve", C.onec[:], 1.0)
    MSET(P, "dve", C.nhalfc[:], -0.5)

    if any(l % 2 == 1 for l in layers):
        emit_rope(C)
    for l in layers:
        if l % 2 == 0:
            emit_even(C, l)
        else:
            emit_odd(C, l)
        emit_ln(C, lambda k, l=l: pv("lmg", l, k, 1), lambda k, l=l: pv("lmb", l, k, 1))
        emit_mlp(C, l)
        emit_ln(C, lambda k, l=l: pv("lfg", l, k, 1), lambda k, l=l: pv("lfb", l, k, 1))

    ysrc = yT.rearrange("(c p) s -> p c s", p=128)
    for c in range(KC):
        v = C.xres[:, c, :]
        P.dma("sp", ysrc[:, c, :], v.ap, in_view=v)
    for q in P.dma_sems:
        if P.count[q] > 0:
            P.wait_on("sp", q, P.count[q])
    P.emit()
    return nc


def emit_rope(C):
    P = C.P
    cv = Carver(C.scr)
    pi_ = cv.arr(I32, [64, S])
    ang = cv.arr(F32, [64, S])
    kk = cv.arr(F32, [64, S])
    y = cv.arr(F32, [64, S])
    acc = cv.arr(F32, [64, S])
    sn = cv.arr(F32, [64, S])
    v = pi_[:]
    P.dma("sp", v.ap, C.dr["pos"].partition_broadcast(64), out_view=v)
    CP(P, "dve", ang[:], pi_[:])
    TS(P, "dve", ang[:], ang[:], C.invf2[:, 0:1], ALU.mult)
    MAGIC = 12582912.0
    TS(P, "dve", kk[:], ang[:], 1.0 / (2 * math.pi), ALU.mult, MAGIC, ALU.add)
    TS(P, "dve", kk[:], kk[:], MAGIC, ALU.subtract)
    c1 = 6.28125
    c2 = float(np.float32(2 * math.pi - c1))
    c3 = float(2 * math.pi - c1 - c2)
    for c in (c1, c2, c3):
        STT(P, ang[:], kk[:], -c, ang[:], ALU.mult, ALU.add)
    TS(P, "dve", ang[:], ang[:], 0.5, ALU.mult)
    TT(P, "dve", y[:], ang[:], ang[:], ALU.mult)
    sc_ = [-1.0 / 6, 1.0 / 120, -1.0 / 5040, 1.0 / 362880, -1.0 / 39916800, 1.0 / 6227020800]
    TS(P, "dve", acc[:], y[:], sc_[5], ALU.mult)
    for k in (4, 3, 2, 1, 0):
        STT(P, acc[:], acc[:], sc_[k], y[:], ALU.add, ALU.mult)
    STT(P, sn[:], acc[:], 1.0, ang[:], ALU.add, ALU.mult)
    cc_ = [-0.5, 1.0 / 24, -1.0 / 720, 1.0 / 40320, -1.0 / 3628800, 1.0 / 479001600, -1.0 / 87178291200]
    TS(P, "dve", acc[:], y[:], cc_[6], ALU.mult)
    for k in (5, 4, 3, 2, 1, 0):
        STT(P, acc[:], acc[:], cc_[k], y[:], ALU.add, ALU.mult)
    TS(P, "dve", acc[:], acc[:], 1.0, ALU.add)
    STT(P, C.sins[:], sn[:], 2.0, acc[:], ALU.mult, ALU.mult)
    TT(P, "dve", y[:], sn[:], sn[:], ALU.mult)
    TS(P, "dve", C.cos2[:], y[:], -2.0, ALU.mult, 1.0, ALU.add)
    TS(P, "dve", C.sins[0:32, :], C.sins[0:32, :], -1.0, ALU.mult)


def emit_ln(C, g, b):
    P, PS = C.P, C.PS
    cv = Carver(C.scr)
    sq = cv.arr(BF16, [128, KC, 512])
    mean = cv.arr(F32, [128, 512])
    ve = cv.arr(F32, [128, 512])
    m2 = cv.arr(F32, [128, 512])
    tt_ = [cv.arr(F32, [128, 512]) for _ in range(3)]
    for nb in range(NB):
        blk = slice(nb * 512, (nb + 1) * 512)
        ACT(P, C.xT[:, :, blk], C.xres[:, :, blk], AF.Copy)
        ACT(P, sq[:], C.xres[:, :, blk], AF.Square)
        s1, s2 = PS.get(), PS.get()
        for k in range(KC):
            MM(P, s1[:], C.ones[:], C.xT[:, k, blk], k == 0, k == KC - 1)
        for k in range(KC):
            MM(P, s2[:], C.ones[:], sq[:, k, :], k == 0, k == KC - 1)
        TS(P, "dve", mean[:], s1[:], 1.0 / D, ALU.mult)
        TS(P, "dve", ve[:], s2[:], 1.0 / D, ALU.mult, LN_EPS, ALU.add)
        PS.put(s1); PS.put(s2)
        TT(P, "dve", m2[:], mean[:], mean[:], ALU.mult)
        TT(P, "dve", ve[:], ve[:], m2[:], ALU.subtract)
        ACT(P, ve[:], ve[:], AF.Ln)
        ACT(P, ve[:], ve[:], AF.Exp, scale=-0.5)
        for k in range(KC):
            t = tt_[k % 3]
            TT(P, "dve", t[:], C.xres[:, k, blk], mean[:], ALU.subtract)
            TT(P, "pool", t[:], t[:], ve[:], ALU.mult)
            TS(P, "dve", C.xres[:, k, blk], t[:], g(k), ALU.mult, b(k), ALU.add)
            ACT(P, C.xT[:, k, blk], t[:], AF.Identity, scale=g(k), bias=b(k))


def emit_mlp(C, l):
    P, PS = C.P, C.PS
    cv = Carver(C.scr)
    hb = [cv.arr(BF16, [128, 4, S]) for _ in range(2)]
    tmp = [cv.arr(F32, [128, 512]) for _ in range(3)]
    w1 = C.dr["mlp_w1"][l].rearrange("(kc p) n -> p kc n", p=128)
    w2 = C.dr["mlp_w2"][l].rearrange("(kc p) n -> p kc n", p=128)
    st = {"ti": 0}

    def mlp1(c):
        (W1,) = C.wslot([(0, [KC, 512], w1[:, :, c * 512:(c + 1) * 512])])
        h = hb[c % 2]
        order = [(m, nb) for m in range(4) for nb in range(NB)] if c > 0 else \
                [(m, nb) for nb in range(NB) for m in range(4)]
        for m, nb in order:
            if True:
                blk = slice(nb * 512, (nb + 1) * 512)
                ps = PS.get()
                for k in range(KC):
                    MM(P, ps[:], W1[:, k, m * 128:(m + 1) * 128], C.xT[:, k, blk], k == 0, k == KC - 1)
                t = tmp[st["ti"] % 3]; st["ti"] += 1
                ACT(P, t[:], ps[:], AF.Relu)
                PS.put(ps)
                TT(P, "pool", h[:, m, blk], t[:], t[:], ALU.mult)

    def mlp2(c):
        (W2,) = C.wslot([(0, [4, D], w2[:, c * 4:(c + 1) * 4, :])])
        h = hb[c % 2]
        order = [(mo, nb) for mo in range(KC) for nb in range(NB)] if c < 7 else \
                [(mo, nb) for nb in range(NB) for mo in range(KC)]
        for mo, nb in order:
            if True:
                blk = slice(nb * 512, (nb + 1) * 512)
                ps = PS.get()
                for k in range(4):
                    MM(P, ps[:], W2[:, k, mo * 128:(mo + 1) * 128], h[:, k, blk], k == 0, k == 3)
                if c == 0:
                    STT(P, C.xres[:, mo, blk], C.xres[:, mo, blk], ALPHA, ps[:], ALU.mult, ALU.add)
                else:
                    TT(P, "dve", C.xres[:, mo, blk], C.xres[:, mo, blk], ps[:], ALU.add)
                PS.put(ps)

    mlp1(0)
    for c in range(8):
        if c + 1 < 8:
            mlp1(c + 1)
        mlp2(c)


def emit_even(C, l):
    P, PS = C.P, C.PS
    j = l // 2
    pv, dv = C.pvf, C.dvf
    cv = Carver(C.scr)
    mixT = cv.arr(BF16, [128, 4, S])
    HW = 1024
    A = [cv.arr(F32, [128, 16 + HW]) for _ in range(2)]
    Bg = [cv.arr(F32, [128, HW]) for _ in range(2)]
    E0 = cv.arr(F32, [128, 16 + HW])
    E1 = cv.arr(F32, [128, 16 + HW])
    C2 = Arr(C.scr, E0.off, F32, [128, HW])
    R = Arr(C.scr, E1.off, F32, [128, HW])
    I_ = cv.arr(F32, [128, HW])
    T = cv.arr(F32, [128, HW])
    cb = [cv.arr(BF16, [128, HW]) for _ in range(2)]
    carry = cv.arr(F32, [128, 8])
    sm = cv.arr(F32, [128, 16])
    w_in = C.dr["even_w_in"][j].rearrange("(kc p) n -> p kc n", p=128)
    w_out = C.dr["even_w_out"][j].rearrange("(kc p) n -> p kc n", p=128)

    cfA, hcfA, hbaA, hbxA = dv(j, 0, 0, KC), dv(j, 1, 0, KC), dv(j, 2, 0, KC), dv(j, 3, 0, KC)
    ACT(P, cfA, pv("lam", j, 0, KC), AF.Exp, scale=-1.0)
    TS(P, "dve", cfA, cfA, 1.0, ALU.add)
    ACT(P, cfA, cfA, AF.Ln)
    TS(P, "dve", cfA, cfA, -8.0, ALU.mult)
    TS(P, "dve", hcfA, cfA, 0.5, ALU.mult)
    TS(P, "dve", hbaA, pv("ba", j, 0, KC), 0.5, ALU.mult)
    TS(P, "dve", hbxA, pv("bx", j, 0, KC), 0.5, ALU.mult)

    GK = math.sqrt(2.0 / math.pi)
    items = [(u, hf) for u in range(12) for hf in range(2)]
    W = {}

    def stageP(i):
        u, hf = items[i]
        a = A[i % 2]
        if hf == 0:
            if u < 4:
                W[u] = C.wslot([(0, [KC, 128], w_in[:, :, u * 128:(u + 1) * 128]),
                                (1024, [128], C.dr["pool_w"][j, u])])
            else:
                h = u - 4
                W[u] = C.wslot([
                    (0, [KC, 128], w_in[:, :, 512 + h * 128:512 + (h + 1) * 128]),
                    (1024, [KC, 128], w_in[:, :, 1536 + h * 128:1536 + (h + 1) * 128]),
                    (2048, [128], C.dr["lru_w_a"][j, h]),
                    (2176, [128], C.dr["lru_w_x"][j, h])])
            MSET(P, "dve", a[:, 0:16], 0.0)
        else:
            CP(P, "dve", a[:, 0:16], A[(i - 1) % 2][:, HW:HW + 16])
        Wu = W[u][0]
        for q in range(2):
            blk = slice(hf * HW + q * 512, hf * HW + (q + 1) * 512)
            ps = PS.get()
            for k in range(KC):
                MM(P, ps[:], Wu[:, k, :], C.xT[:, k, blk], k == 0, k == KC - 1)
            ACT(P, a[:, 16 + q * 512:16 + (q + 1) * 512], ps[:], AF.Copy)
            PS.put(ps)
        if u >= 4:
            Wg = W[u][1]
            for q in range(2):
                blk = slice(hf * HW + q * 512, hf * HW + (q + 1) * 512)
                ps = PS.get()
                for k in range(KC):
                    MM(P, ps[:], Wg[:, k, :], C.xT[:, k, blk], k == 0, k == KC - 1)
                ACT(P, Bg[i % 2][:, q * 512:(q + 1) * 512], ps[:], AF.Copy)
                PS.put(ps)

    def stageE(i):
        u, hf = items[i]
        a = A[i % 2]
        ui = u % 4
        cbi = cb[i % 2]
        if u < 4:
            g = u
            Wp = W[u][1]
            w = 2 << g
            src, sh, lo, ei = a, 1, 0, 0
            exts = [E0, E1]
            while sh < w:
                dst = exts[ei % 2]; ei += 1
                lo2 = lo + sh
                TT(P, "dve", dst[:, lo2:16 + HW], src[:, lo2:16 + HW], src[:, lo2 - sh:16 + HW - sh], ALU.add)
                src, lo, sh = dst, lo2, sh * 2
            STT(P, T[:], src[:, 16:16 + HW], 1.0 / w, a[:, 16:16 + HW], ALU.mult, ALU.subtract)
            if hf == 0:
                TT(P, "dve", sm[:], src[:, 16:32], C.invc[:, g * 16:(g + 1) * 16], ALU.mult)
                TT(P, "dve", T[:, 0:16], sm[:], a[:, 16:32], ALU.subtract)
            ACT(P, cbi[:], T[:], AF.Copy)
            for q in range(2):
                blk = slice(hf * HW + q * 512, hf * HW + (q + 1) * 512)
                ps = PS.get()
                MM(P, ps[:], Wp[:], cbi[:, q * 512:(q + 1) * 512], True, True)
                ACT(P, mixT[:, ui, blk], ps[:], AF.Copy, scale=pv("pscale", j, g, 1))
                PS.put(ps)
        else:
            h = u - 4
            Wa, Wx = W[u][2], W[u][3]
            bg = Bg[i % 2]
            cw = lambda k: pv("convw", j, k * KC + h, 1)
            TS(P, "dve", C2[:], a[:, 16:16 + HW], cw(3), ALU.mult, pv("convb", j, h, 1), ALU.add)
            for k in (2, 1, 0):
                STT(P, C2[:], a[:, 13 + k:13 + k + HW], cw(k), C2[:], ALU.mult, ALU.add)
            ACT(P, cbi[:], C2[:], AF.Copy)
            for q in range(2):
                ps = PS.get()
                MM(P, ps[:], Wa[:], cbi[:, q * 512:(q + 1) * 512], True, True)
                ACT(P, R[:, q * 512:(q + 1) * 512], ps[:], AF.Tanh, scale=0.5, bias=dv(j, 2, h, 1))
                PS.put(ps)
                ps = PS.get()
                MM(P, ps[:], Wx[:], cbi[:, q * 512:(q + 1) * 512], True, True)
                ACT(P, I_[:, q * 512:(q + 1) * 512], ps[:], AF.Tanh, scale=0.5, bias=dv(j, 3, h, 1))
                PS.put(ps)
            ACT(P, T[:], R[:], AF.Exp, scale=dv(j, 0, h, 1), bias=dv(j, 0, h, 1))
            ACT(P, R[:], R[:], AF.Exp, scale=dv(j, 1, h, 1), bias=dv(j, 1, h, 1))
            ACT(P, T[:], T[:], AF.Ln, scale=-1.0, bias=C.onec[:, 0:1])
            ACT(P, T[:], T[:], AF.Exp, scale=0.5)
            STT(P, I_[:], I_[:], 1.0, T[:], ALU.add, ALU.mult)
            STT(P, I_[:], I_[:], 0.5, C2[:], ALU.mult, ALU.mult)
            Hb = T
            if hf == 0:
                P.op("dve", lambda e, o=Hb[:], a_=R[:], b_=I_[:]: e.tensor_tensor_scan(
                    out=o.ap, data0=a_.ap, data1=b_.ap, initial=0.0, op0=ALU.mult, op1=ALU.add),
                    reads=[R[:], I_[:]], writes=[Hb[:]])
            else:
                init = carry[:, h:h + 1]
                P.op("dve", lambda e, o=Hb[:], a_=R[:], b_=I_[:], i0=init: e.tensor_tensor_scan(
                    out=o.ap, data0=a_.ap, data1=b_.ap, initial=i0.ap, op0=ALU.mult, op1=ALU.add),
                    reads=[R[:], I_[:], init], writes=[Hb[:]])
            CP(P, "dve", carry[:, h:h + 1], Hb[:, HW - 1:HW])
            Cg = C2
            ACT(P, Cg[:], bg[:], AF.Square)
            TS(P, "dve", Cg[:], Cg[:], 0.044715, ALU.mult, 1.0, ALU.add)
            TT(P, "dve", Cg[:], Cg[:], bg[:], ALU.mult)
            ACT(P, Cg[:], Cg[:], AF.Tanh, scale=GK)
            STT(P, bg[:], Cg[:], 1.0, bg[:], ALU.add, ALU.mult)
            STT(P, mixT[:, ui, hf * HW:(hf + 1) * HW], bg[:], 0.5, Hb[:], ALU.mult, ALU.mult)
        if ui == 3 and hf == 1:
            grp = u // 4
            (Wo,) = C.wslot([(0, [4, D], w_out[:, grp * 4:(grp + 1) * 4, :])])
            order = [(mo, nb) for mo in range(KC) for nb in range(NB)] if grp < 2 else \
                    [(mo, nb) for nb in range(NB) for mo in range(KC)]
            for mo, nb in order:
                if True:
                    blk = slice(nb * 512, (nb + 1) * 512)
                    ps = PS.get()
                    for k in range(4):
                        MM(P, ps[:], Wo[:, k, mo * 128:(mo + 1) * 128], mixT[:, k, blk], k == 0, k == 3)
                    if grp == 0:
                        STT(P, C.xres[:, mo, blk], C.xres[:, mo, blk], ALPHA, ps[:], ALU.mult, ALU.add)
                    else:
                        TT(P, "dve", C.xres[:, mo, blk], C.xres[:, mo, blk], ps[:], ALU.add)
                    PS.put(ps)

    stageP(0)
    for i in range(len(items)):
        if i + 1 < len(items):
            stageP(i + 1)
        stageE(i)


def emit_odd(C, l):
    P, PS = C.P, C.PS
    j = l // 2
    pv = C.pvf
    cv = Carver(C.scr)
    cqn = cv.arr(BF16, [128, 3, S])
    ckvn = cv.arr(BF16, [128, 2, S])
    kpe = cv.arr(BF16, [64, S])
    qn = cv.arr(BF16, [128, S])
    qpe = cv.arr(BF16, [64, S])
    kn = cv.arr(BF16, [128, S])
    V = cv.arr(BF16, [128, 16, 128])
    E = [cv.arr(BF16, [128, 512]) for _ in range(3)]
    sq = cv.arr(BF16, [128, 3, 512])
    rs = [cv.arr(F32, [128, 512]) for _ in range(2)]
    r1 = [cv.arr(F32, [64, 512]) for _ in range(1)]
    r2 = [cv.arr(F32, [64, 512]) for _ in range(1)]
    oT = Arr(C.xT.buf, 0, BF16, [128, KC, S])
    wd = C.dr["mla_w_down"][j].rearrange("(kc p) n -> p kc n", p=128)
    wds = C.dr["w_down_sw"][j].rearrange("(kc p) n -> p kc n", p=128)
    wqb = C.dr["mla_w_qb"][j].rearrange("(kc p) n -> p kc n", p=128)
    wqs = C.dr["w_qb_sw"][j].rearrange("(kc p) n -> p kc n", p=128)
    wkv = C.dr["mla_w_kvb"][j].rearrange("(kc p) n -> p kc n", p=128)
    wo = C.dr["mla_w_o"][j].rearrange("(kc p) n -> p kc n", p=128)
    SCALE = 192.0 ** -0.5

    (Wd1,) = C.wslot([(0, [KC, 384], wd[:, :, 0:384])])
    Wd2, Wd3 = C.wslot([(0, [KC, 320], wd[:, :, 384:704]), (2560, [KC, 64], wds)])

    def rope(dst, pa, pb, blk, ri):
        TT(P, "dve", r1[ri][:], pa, C.cos2[:, blk], ALU.mult)
        TT(P, "dve", r2[ri][:], pb, C.sins[:, blk], ALU.mult)
        TT(P, "dve", dst, r1[ri][:], r2[ri][:], ALU.add)

    def rmsn(dst, W, ncol, nch, gname, blk, ri):
        banks = []
        for m in range(nch):
            ps = PS.get()
            for k in range(KC):
                MM(P, ps[:], W[:, k, m * 128:(m + 1) * 128], C.xT[:, k, blk], k == 0, k == KC - 1)
            ACT(P, sq[:, m, :], ps[:], AF.Square)
            banks.append(ps)
        s2 = PS.get()
        for m in range(nch):
            MM(P, s2[:], C.ones[:], sq[:, m, :], m == 0, m == nch - 1)
        ve = rs[ri]
        TS(P, "dve", ve[:], s2[:], 1.0 / ncol, ALU.mult, RMS_EPS, ALU.add)
        PS.put(s2)
        ACT(P, ve[:], ve[:], AF.Ln)
        ACT(P, ve[:], ve[:], AF.Exp, scale=-0.5)
        for m in range(nch):
            STT(P, dst[:, m, blk], banks[m][:], pv(gname, j, m, 1), ve[:], ALU.mult, ALU.mult)
            PS.put(banks[m])

    for nb in range(NB):
        blk = slice(nb * 512, (nb + 1) * 512)
        rmsn(cqn, Wd1, 384, 3, "qg", blk, 0)
        rmsn(ckvn, Wd2, 256, 2, "kvg", blk, 1)
        pa, pb = PS.get(), PS.get()
        for k in range(KC):
            MM(P, pa[0:64, :], Wd2[:, k, 256:320], C.xT[:, k, blk], k == 0, k == KC - 1)
        for k in range(KC):
            MM(P, pb[0:64, :], Wd3[:, k, :], C.xT[:, k, blk], k == 0, k == KC - 1)
        rope(kpe[:, blk], pa[0:64, :], pb[0:64, :], blk, 0)
        PS.put(pa); PS.put(pb)

    st_e = {"ei": 0}
    for h in range(8):
        Wq, Wqs, Wkv = C.wslot([(0, [3, 192], wqb[:, :, h * 192:(h + 1) * 192]),
                                (576, [3, 64], wqs[:, :, h * 64:(h + 1) * 64]),
                                (768, [2, 256], wkv[0:128, :, h * 256:(h + 1) * 256])])
        for nb in range(NB):
            blk = slice(nb * 512, (nb + 1) * 512)
            ps = PS.get()
            for k in range(3):
                MM(P, ps[:], Wq[:, k, 0:128], cqn[:, k, blk], k == 0, k == 2)
            ACT(P, qn[:, blk], ps[:], AF.Copy)
            PS.put(ps)
            pa, pb = PS.get(), PS.get()
            for k in range(3):
                MM(P, pa[0:64, :], Wq[:, k, 128:192], cqn[:, k, blk], k == 0, k == 2)
            for k in range(3):
                MM(P, pb[0:64, :], Wqs[:, k, :], cqn[:, k, blk], k == 0, k == 2)
            rope(qpe[:, blk], pa[0:64, :], pb[0:64, :], blk, 0)
            PS.put(pa); PS.put(pb)
            ps = PS.get()
            for k in range(2):
                MM(P, ps[:], Wkv[:, k, 0:128], ckvn[:, k, blk], k == 0, k == 1)
            ACT(P, kn[:, blk], ps[:], AF.Copy)
            PS.put(ps)
        for t4 in range(4):
            ps = PS.get()
            for tt in range(4):
                t = t4 * 4 + tt
                for k in range(2):
                    MM(P, ps[:, tt * 128:(tt + 1) * 128], ckvn[:, k, t * 128:(t + 1) * 128], Wkv[:, k, 128:256],
                       k == 0, k == 1, inc=(k == 1 and tt == 3))
            ACT(P, V[:, t4 * 4:(t4 + 1) * 4, :], ps[:], AF.Copy)
            PS.put(ps)
        for qb in range(NB):
            num, den = PS.get(), PS.get()
            nkt = 4 * qb + 4
            def sc_stage(kt):
                q0 = max(512 * qb, 128 * kt)
                N = 512 * qb + 512 - q0
                c0 = q0 - 512 * qb
                kts = slice(kt * 128, (kt + 1) * 128)
                sc = PS.get()
                MM(P, sc[:, 0:N], kn[:, kts], qn[:, q0:q0 + N], True, False)
                MM(P, sc[:, 0:N], kpe[:, kts], qpe[:, q0:q0 + N], False, True)
                Eb = E[st_e["ei"] % 3]; st_e["ei"] += 1
                ACT(P, Eb[:, 0:N], sc[:, 0:N], AF.Exp, scale=SCALE)
                PS.put(sc)
                if kt >= 4 * qb:
                    MSET(P, "dve", Eb[64:128, 0:64], 0.0)
                return Eb, N, c0

            pend = sc_stage(0)
            for kt in range(nkt):
                nxt = sc_stage(kt + 1) if kt + 1 < nkt else None
                Eb, N, c0 = pend
                MM(P, num[:, c0:c0 + N], V[:, kt, :], Eb[:, 0:N], kt == 0, kt == nkt - 1, inc=False)
                MM(P, den[:, c0:c0 + N], C.ones[:], Eb[:, 0:N], kt == 0, kt == nkt - 1, inc=True)
                pend = nxt
            rc = rs[qb % 2]
            P.op("dve", lambda e, o=rc[:], i=den[:]: e.reciprocal(out=o.ap, in_=i.ap), reads=[den[:]], writes=[rc[:]])
            TT(P, "dve", oT[:, h, qb * 512:(qb + 1) * 512], num[:], rc[:], ALU.mult)
            PS.put(num); PS.put(den)

    Wo1 = C.wslot([(0, [4, D], wo[:, 0:4, :])])[0]
    Wo2 = C.wslot([(0, [4, D], wo[:, 4:8, :])])[0]
    for nb in range(NB):
        for mo in range(KC):
            blk = slice(nb * 512, (nb + 1) * 512)
            ps = PS.get()
            for k in range(KC):
                W = Wo1 if k < 4 else Wo2
                MM(P, ps[:], W[:, k % 4, mo * 128:(mo + 1) * 128], oT[:, k, blk], k == 0, k == KC - 1)
            STT(P, C.xres[:, mo, blk], C.xres[:, mo, blk], ALPHA, ps[:], ALU.mult, ALU.add)
            PS.put(ps)


_NC_CACHE = {}


def _host_prep(inp):
    f = np.float32
    pvec = np.zeros((128, PV_N), f)

    def put(key, vec):
        v = np.asarray(vec, f)
        n = v.shape[0] // 128
        o = PV_COLS[key]
        pvec[:, o:o + n] = v.reshape(n, 128).T

    for l in range(DEPTH):
        put(("lmg", l), inp["ln_mix_g"][l]); put(("lmb", l), inp["ln_mix_b"][l])
        put(("lfg", l), inp["ln_ffn_g"][l]); put(("lfb", l), inp["ln_ffn_b"][l])
    for j in range(2):
        put(("pscale", j), inp["pool_scale"][j])
        cw = np.asarray(inp["lru_conv_w"][j], f)
        o = PV_COLS[("convw", j)]
        for k in range(4):
            pvec[:, o + k * KC:o + (k + 1) * KC] = cw[k].reshape(KC, 128).T
        put(("convb", j), inp["lru_conv_b"][j]); put(("ba", j), inp["lru_b_a"][j])
        put(("bx", j), inp["lru_b_x"][j]); put(("lam", j), inp["lru_lambda"][j])
        put(("qg", j), inp["mla_q_norm_g"][j]); put(("kvg", j), inp["mla_kv_norm_g"][j])
    inv_freq = (10000.0 ** (-np.arange(0, 64, 2, dtype=f) / f(64))).astype(f)
    invf2 = np.concatenate([inv_freq, inv_freq]).reshape(64, 1).astype(f)
    invc = np.zeros((128, 64), f)
    for g, w in enumerate((2, 4, 8, 16)):
        invc[:, g * 16:(g + 1) * 16] = (1.0 / np.minimum(np.arange(16) + 1, w)).astype(f)[None, :]
    wd = np.asarray(inp["mla_w_down"], f)
    w_down_sw = np.ascontiguousarray(np.concatenate([wd[:, :, 672:704], wd[:, :, 640:672]], axis=2))
    wq = np.asarray(inp["mla_w_qb"], f).reshape(2, 384, 8, 192)
    w_qb_sw = np.ascontiguousarray(np.concatenate([wq[..., 160:192], wq[..., 128:160]], axis=3).reshape(2, 384, 512))
    shared = {"pvec": pvec, "invf2": invf2, "invc": invc, "w_down_sw": w_down_sw, "w_qb_sw": w_qb_sw}
    for nm in ("even_w_in", "pool_w", "lru_w_a", "lru_w_x", "even_w_out", "mla_w_down", "mla_w_qb",
               "mla_w_kvb", "mla_w_o", "mlp_w1", "mlp_w2"):
        shared[nm] = np.ascontiguousarray(np.asarray(inp[nm], f))
    return shared


LAUNCH_GROUPS = [(0, 1, 2, 3)]


def kernel(**inp):
    shared = _host_prep(inp)
    x = np.asarray(inp["x"], np.float32)
    pos = np.asarray(inp["positions"], np.int32)
    cur = [np.ascontiguousarray(x[b].T) for b in range(NCORES)]
    for grp in LAUNCH_GROUPS:
        if grp not in _NC_CACHE:
            _NC_CACHE[grp] = build(grp)
        nc = _NC_CACHE[grp]
        in_maps = []
        for b in range(NCORES):
            m = dict(shared)
            m["xT"] = cur[b]
            m["pos"] = np.ascontiguousarray(pos[b][None, :])
            in_maps.append(m)
        res = run_bass_kernel_spmd(nc, in_maps, core_ids=list(range(NCORES)))
        cur = [np.ascontiguousarray(res.results[b]["yT"]) for b in range(NCORES)]
    return np.stack([c.T for c in cur], axis=0).astype(np.float32)
```

```python
import math
import numpy as np
import concourse.bass as bass
import concourse.mybir as mybir
from concourse.bass_utils import run_bass_kernel_spmd

F32 = mybir.dt.float32
BF16 = mybir.dt.bfloat16
I32 = mybir.dt.int32
AF = mybir.ActivationFunctionType
ALU = mybir.AluOpType

NCORES = 8
S = 2048
D = 1024
DEPTH = 4
KC = D // 128
NB = S // 512
ALPHA = (2 * DEPTH) ** 0.25
LN_EPS = 1e-5
RMS_EPS = 1e-6


class View:
    __slots__ = ("buf", "ap", "ivs")

    def __init__(self, buf, ap, ivs):
        self.buf, self.ap, self.ivs = buf, ap, ivs


def _merge(ivs):
    ivs = sorted(ivs)
    out = [list(ivs[0])]
    for lo, hi in ivs[1:]:
        if lo <= out[-1][1]:
            out[-1][1] = max(out[-1][1], hi)
        else:
            out.append([lo, hi])
    return tuple((a, b) for a, b in out)


def _overlap(a, b):
    for lo, hi in a:
        for lo2, hi2 in b:
            if lo < hi2 and lo2 < hi:
                return True
    return False


def _covers(a, b):
    for lo2, hi2 in b:
        ok = False
        for lo, hi in a:
            if lo <= lo2 and hi2 <= hi:
                ok = True
                break
        if not ok:
            return False
    return True


class Buf:
    def __init__(self, prog, name, shape, dtype, space="sbuf"):
        self.prog, self.name, self.shape, self.dtype = prog, name, list(shape), dtype
        self.esz = 2 if dtype == BF16 else 4
        nc = prog.nc
        if space == "sbuf":
            self.t = nc.alloc_sbuf_tensor(name, self.shape, dtype)
        else:
            self.t = nc.alloc_psum_tensor(name, self.shape, dtype)
        st = []
        acc = 1
        for n in reversed(self.shape[1:]):
            st.append(acc)
            acc *= n
        self.strides = list(reversed(st))
        self.hist = []

    def __getitem__(self, idx):
        if not isinstance(idx, tuple):
            idx = (idx,)
        ap = self.t[idx]
        fidx = list(idx[1:]) + [slice(None)] * (len(self.shape) - len(idx))
        rngs = []
        for i, n in zip(fidx, self.shape[1:]):
            if isinstance(i, int):
                rngs.append((i, i + 1))
            else:
                a, b, _ = i.indices(n)
                rngs.append((a, b))
        outer = rngs[:-1]
        cnt = 1
        for a, b in outer:
            cnt *= (b - a)
        la, lb = rngs[-1]
        if cnt > 64:
            lo = sum(a * s for (a, b), s in zip(rngs, self.strides))
            hi = sum((b - 1) * s for (a, b), s in zip(rngs, self.strides)) + 1
            ivs = ((lo * self.esz, hi * self.esz),)
        else:
            starts = [0]
            for (a, b), s in zip(outer, self.strides[:-1]):
                starts = [o + j * s for o in starts for j in range(a, b)]
            ivs = _merge([((o + la) * self.esz, (o + lb) * self.esz) for o in starts])
        return View(self, ap, ivs)


class Prog:
    ENGS = ("pe", "act", "dve", "pool", "sp")

    def __init__(self, nc, n_dma_sems=24):
        self.nc = nc
        self.streams = {e: [] for e in self.ENGS}
        self.sems = {}
        self.count = {}
        self.waited = {e: {} for e in self.ENGS}
        for e in self.ENGS:
            self.sems[e] = nc.alloc_semaphore(name="s_" + e)
            self.count[e] = 0
        self.dma_sems = []
        for i in range(n_dma_sems):
            nm = "q%d" % i
            self.sems[nm] = nc.alloc_semaphore(name="s_" + nm)
            self.count[nm] = 0
            self.dma_sems.append(nm)
        self.dma_rr = 0
        self.n_ops = 0

    def _deps(self, eng, reads, writes):
        need = {}
        for v in reads:
            for rec in v.buf.hist:
                if rec[2] and _overlap(rec[3], v.ivs):
                    if not (eng == "pe" and rec[0] == "pe"):
                        if need.get(rec[0], 0) < rec[1]:
                            need[rec[0]] = rec[1]
        for v in writes:
            for rec in v.buf.hist:
                if _overlap(rec[3], v.ivs):
                    if not (eng == "pe" and rec[0] == "pe"):
                        if need.get(rec[0], 0) < rec[1]:
                            need[rec[0]] = rec[1]
        return need

    def _record(self, who, cnt, reads, writes):
        for v in reads:
            h = v.buf.hist
            for rec in h:
                if (not rec[2]) and rec[0] == who and rec[3] == v.ivs:
                    rec[1] = max(rec[1], cnt)
                    break
            else:
                h.append([who, cnt, False, v.ivs])
        for v in writes:
            h = v.buf.hist
            h[:] = [rec for rec in h if not _covers(v.ivs, rec[3])]
            h.append([who, cnt, True, v.ivs])

    def _waits(self, eng, need):
        w = []
        for f, c in need.items():
            if self.waited[eng].get(f, 0) < c:
                self.waited[eng][f] = c
                w.append((f, c))
        return w

    def op(self, eng, fn, reads=(), writes=(), inc=True):
        need = self._deps(eng, reads, writes)
        waits = self._waits(eng, need)
        cnt = self.count[eng] + 1
        if inc:
            self.count[eng] = cnt
        self._record(eng, cnt, reads, writes)
        self.streams[eng].append((waits, fn, (eng, 1) if inc else None))
        self.n_ops += 1

    def dma(self, queue, out, in_, out_view=None, in_view=None, **kw):
        reads = [in_view] if in_view is not None else []
        writes = [out_view] if out_view is not None else []
        need = self._deps(queue, reads, writes)
        q = self.dma_sems[self.dma_rr % len(self.dma_sems)]
        self.dma_rr += 1
        if self.count[q] > 0:
            need[q] = max(need.get(q, 0), self.count[q])
        waits = self._waits(queue, need)
        self.count[q] += 16
        cnt = self.count[q]
        self._record(q, cnt, reads, writes)
        self.streams[queue].append((waits, lambda e: e.dma_start(out=out, in_=in_, **kw), (q, 16)))
        self.n_ops += 1
        return (q, cnt)

    def wait_on(self, eng, who, cnt):
        w = self._waits(eng, {who: cnt})
        if w:
            self.streams[eng].append((w, None, None))

    def emit(self):
        nc = self.nc
        engobj = {"pe": "tensor", "act": "scalar", "dve": "vector", "pool": "gpsimd", "sp": "sync"}
        with nc.Block() as block:
            for e in self.ENGS:
                stream = self.streams[e]

                def body(eng, stream=stream):
                    for waits, fn, inc in stream:
                        for (f, c) in waits:
                            eng.wait_ge(self.sems[f], c)
                        if fn is None:
                            continue
                        ins = fn(eng)
                        if inc is not None:
                            ins.then_inc(self.sems[inc[0]], inc[1])

                getattr(block, engobj[e])(body)


class Arr:
    def __init__(self, buf, byte_off, dtype, shape):
        self.buf, self.off, self.dtype, self.shape = buf, byte_off, dtype, list(shape)
        self.esz = 2 if dtype == BF16 else 4
        n = 1
        for d in shape[1:]:
            n *= d
        assert byte_off % 4 == 0 and (n * self.esz) % 4 == 0
        lo, hi = byte_off // 4, (byte_off + n * self.esz) // 4
        assert hi <= buf.shape[1], (buf.name, hi, buf.shape)
        ap = buf.t[0:shape[0], lo:hi]
        if dtype != F32:
            ap = ap.bitcast(dtype)
        fd = shape[1:]
        if len(fd) == 2:
            ap = ap.rearrange("p (a b) -> p a b", a=fd[0])
        elif len(fd) == 3:
            ap = ap.rearrange("p (a b c) -> p a b c", a=fd[0], b=fd[1])
        self.full = ap
        st, acc = [], 1
        for d in reversed(fd):
            st.append(acc)
            acc *= d
        self.strides = list(reversed(st))
        self.nbytes = n * self.esz

    def __getitem__(self, idx):
        if not isinstance(idx, tuple):
            idx = (idx,)
        ap = self.full[idx]
        fidx = list(idx[1:]) + [slice(None)] * (len(self.shape) - len(idx))
        rngs = []
        for i, n in zip(fidx, self.shape[1:]):
            if isinstance(i, int):
                rngs.append((i, i + 1))
            else:
                a, b, _ = i.indices(n)
                rngs.append((a, b))
        outer = rngs[:-1]
        cnt = 1
        for a, b in outer:
            cnt *= (b - a)
        la, lb = rngs[-1]
        if cnt > 64:
            lo = sum(a * s for (a, b), s in zip(rngs, self.strides))
            hi = sum((b - 1) * s for (a, b), s in zip(rngs, self.strides)) + 1
            ivs = ((self.off + lo * self.esz, self.off + hi * self.esz),)
        else:
            starts = [0]
            for (a, b), s in zip(outer, self.strides[:-1]):
                starts = [o + j * s for o in starts for j in range(a, b)]
            ivs = _merge([(self.off + (o + la) * self.esz, self.off + (o + lb) * self.esz) for o in starts])
        return View(self.buf, ap, ivs)


class Carver:
    def __init__(self, buf):
        self.buf, self.pos = buf, 0

    def arr(self, dtype, shape):
        a = Arr(self.buf, self.pos, dtype, shape)
        self.pos += (a.nbytes + 31) // 32 * 32
        assert self.pos <= self.buf.shape[1] * 4, (self.buf.name, self.pos)
        return a


class PsumPool:
    def __init__(self, prog):
        self.banks = [Buf(prog, "psb%d" % i, [128, 512], F32, space="psum") for i in range(8)]
        self.free = list(range(8))

    def get(self):
        assert self.free, "out of PSUM banks"
        return self.banks[self.free.pop(0)]

    def put(self, b):
        self.free.append(self.banks.index(b))


def MM(P, out, lhsT, rhs, start, stop, inc=None):
    if inc is None:
        inc = stop
    P.op("pe", lambda e: e.matmul(out.ap, lhsT=lhsT.ap, rhs=rhs.ap, start=start, stop=stop),
         reads=[lhsT, rhs], writes=[out], inc=inc)


def ACT(P, out, in_, func, scale=1.0, bias=None):
    reads = [in_]
    kw = {}
    if isinstance(scale, View):
        reads.append(scale)
        kw["scale"] = scale.ap
    else:
        kw["scale"] = float(scale)
    if isinstance(bias, View):
        reads.append(bias)
        kw["bias"] = bias.ap
    elif bias is not None:
        kw["bias"] = float(bias)
    P.op("act", lambda e: e.activation(out=out.ap, in_=in_.ap, func=func, **kw), reads=reads, writes=[out])


def TS(P, eng, out, in0, s1, op0, s2=None, op1=None):
    reads = [in0]
    a1 = s1
    if isinstance(s1, View):
        reads.append(s1)
        a1 = s1.ap
    a2 = s2
    if isinstance(s2, View):
        reads.append(s2)
        a2 = s2.ap
    if op1 is None:
        P.op(eng, lambda e: e.tensor_scalar(out=out.ap, in0=in0.ap, scalar1=a1, scalar2=None, op0=op0),
             reads=reads, writes=[out])
    else:
        P.op(eng, lambda e: e.tensor_scalar(out=out.ap, in0=in0.ap, scalar1=a1, scalar2=a2, op0=op0, op1=op1),
             reads=reads, writes=[out])


def TT(P, eng, out, in0, in1, op):
    P.op(eng, lambda e: e.tensor_tensor(out=out.ap, in0=in0.ap, in1=in1.ap, op=op), reads=[in0, in1], writes=[out])


def STT(P, out, in0, sc, in1, op0, op1):
    reads = [in0, in1]
    a = sc
    if isinstance(sc, View):
        reads.append(sc)
        a = sc.ap
    P.op("dve", lambda e: e.scalar_tensor_tensor(out=out.ap, in0=in0.ap, scalar=a, in1=in1.ap, op0=op0, op1=op1),
         reads=reads, writes=[out])


def CP(P, eng, out, in_):
    P.op(eng, lambda e: e.tensor_copy(out=out.ap, in_=in_.ap), reads=[in_], writes=[out])


def MSET(P, eng, out, val):
    P.op(eng, lambda e: e.memset(out.ap, val), writes=[out])


def _pvec_layout():
    cols = {}
    n = 0
    for l in range(DEPTH):
        for nm in ("lmg", "lmb", "lfg", "lfb"):
            cols[(nm, l)] = n
            n += KC
    for j in range(2):
        cols[("pscale", j)] = n; n += 4
        cols[("convw", j)] = n; n += 4 * KC
        for nm in ("convb", "ba", "bx", "lam"):
            cols[(nm, j)] = n; n += KC
        cols[("qg", j)] = n; n += 3
        cols[("kvg", j)] = n; n += 2
    return cols, n


PV_COLS, PV_N = _pvec_layout()
DV_N = 2 * 4 * KC


class Ctx:
    pass


def build(layers=(0, 1, 2, 3)):
    nc = bass.Bass("TRN2", target_bir_lowering=False)
    P = Prog(nc)
    C = Ctx()
    C.P, C.nc = P, nc
    dr = {}

    def din(name, shape, dt=F32):
        dr[name] = nc.dram_tensor(name, list(shape), dt, kind="ExternalInput").ap()
        return dr[name]

    din("xT", [D, S]); din("pos", [1, S], I32); din("pvec", [128, PV_N]); din("invf2", [64, 1]); din("invc", [128, 64])
    din("even_w_in", [2, D, 2560]); din("pool_w", [2, 4, 128, 128]); din("lru_w_a", [2, 8, 128, 128])
    din("lru_w_x", [2, 8, 128, 128]); din("even_w_out", [2, 1536, D])
    din("mla_w_down", [2, D, 704]); din("w_down_sw", [2, D, 64]); din("mla_w_qb", [2, 384, 1536])
    din("w_qb_sw", [2, 384, 512]); din("mla_w_kvb", [2, 256, 2048]); din("mla_w_o", [2, D, D])
    din("mlp_w1", [DEPTH, D, 4096]); din("mlp_w2", [DEPTH, 4096, D])
    yT = nc.dram_tensor("yT", [D, S], F32, kind="ExternalOutput").ap()
    C.dr = dr

    xres_b = Buf(P, "xres", [128, KC * S], F32)
    xt_b = Buf(P, "xtb", [128, KC * S // 2], F32)
    wr_b = [Buf(P, "wring%d" % i, [128, 2048], F32) for i in range(4)]
    rope_b = Buf(P, "rope", [128, 2 * S], F32)
    cst_b = Buf(P, "cst", [128, PV_N + DV_N + 64 + 8 + 64 + 512 + 512 + 32], F32)
    scr_b = Buf(P, "scr", [128, 14080], F32)
    C.xres = Arr(xres_b, 0, F32, [128, KC, S])
    C.xT = Arr(xt_b, 0, BF16, [128, KC, S])
    cc = Carver(cst_b)
    C.pv = cc.arr(F32, [128, PV_N + DV_N])
    C.invc = cc.arr(F32, [128, 64])
    C.invf2 = cc.arr(F32, [64, 1])
    C.ones = cc.arr(BF16, [128, 128])
    C.halfc = cc.arr(F32, [128, 512])
    C.onec = cc.arr(F32, [128, 8])
    C.nhalfc = cc.arr(F32, [128, 512])
    C.cos2 = Arr(rope_b, 0, F32, [64, S])
    C.sins = Arr(rope_b, S * 4, F32, [64, S])
    C.scr = scr_b
    C.PS = PsumPool(P)
    C.wr_b = wr_b
    C.wi = 0

    def pv(nm, idx, c0=0, n=1, p=128):
        o = PV_COLS[(nm, idx)] + c0
        return C.pv[0:p, o:o + n]

    def dv(j, which, c0=0, n=1):
        o = PV_N + j * 4 * KC + which * KC + c0
        return C.pv[:, o:o + n]

    C.pvf, C.dvf = pv, dv

    def wslot(loads):
        buf = wr_b[C.wi % len(wr_b)]
        C.wi += 1
        arrs = []
        for off, shp, src in loads:
            a = Arr(buf, off * 2, BF16, [128] + list(shp))
            v = a[:]
            P.dma("pool", v.ap, src, out_view=v)
            arrs.append(a)
        return arrs

    C.wslot = wslot

    v = C.pv[:, 0:PV_N]
    P.dma("sp", v.ap, dr["pvec"], out_view=v)
    v = C.invc[:]
    P.dma("sp", v.ap, dr["invc"], out_view=v)
    v = C.invf2[:]
    P.dma("sp", v.ap, dr["invf2"], out_view=v)
    xsrc = dr["xT"].rearrange("(c p) s -> p c s", p=128)
    for c in range(KC):
        v = C.xres[:, c, :]
        P.dma("sp", v.ap, xsrc[:, c, :], out_view=v)
    for hf in range(2):
        for c in range(KC):
            v = C.xT[:, c, hf * 1024:(hf + 1) * 1024]
            P.dma("pool", v.ap, xsrc[:, c, hf * 1024:(hf + 1) * 1024], out_view=v)
    MSET(P, "dve", C.ones[:], 1.0)
    MSET(P, "dve", C.halfc[:], 0.5)
    MSET(P, "dve", C.onec[:], 1.0)
    MSET(P, "dve", C.nhalfc[:], -0.5)

    if any(l % 2 == 1 for l in layers):
        emit_rope(C)
    for l in layers:
        if l % 2 == 0:
            emit_even(C, l)
        else:
            emit_odd(C, l)
        emit_ln(C, lambda k, l=l: pv("lmg", l, k, 1), lambda k, l=l: pv("lmb", l, k, 1))
        emit_mlp(C, l)
        emit_ln(C, lambda k, l=l: pv("lfg", l, k, 1), lambda k, l=l: pv("lfb", l, k, 1))

    ysrc = yT.rearrange("(c p) s -> p c s", p=128)
    for c in range(KC):
        v = C.xres[:, c, :]
        P.dma("sp", ysrc[:, c, :], v.ap, in_view=v)
    for q in P.dma_sems:
        if P.count[q] > 0:
            P.wait_on("sp", q, P.count[q])
    P.emit()
    return nc


def emit_rope(C):
    P = C.P
    cv = Carver(C.scr)
    pi_ = cv.arr(I32, [64, S])
    ang = cv.arr(F32, [64, S])
    kk = cv.arr(F32, [64, S])
    y = cv.arr(F32, [64, S])
    acc = cv.arr(F32, [64, S])
    sn = cv.arr(F32, [64, S])
    v = pi_[:]
    P.dma("sp", v.ap, C.dr["pos"].partition_broadcast(64), out_view=v)
    CP(P, "dve", ang[:], pi_[:])
    TS(P, "dve", ang[:], ang[:], C.invf2[:, 0:1], ALU.mult)
    MAGIC = 12582912.0
    TS(P, "dve", kk[:], ang[:], 1.0 / (2 * math.pi), ALU.mult, MAGIC, ALU.add)
    TS(P, "dve", kk[:], kk[:], MAGIC, ALU.subtract)
    c1 = 6.28125
    c2 = float(np.float32(2 * math.pi - c1))
    c3 = float(2 * math.pi - c1 - c2)
    for c in (c1, c2, c3):
        STT(P, ang[:], kk[:], -c, ang[:], ALU.mult, ALU.add)
    TS(P, "dve", ang[:], ang[:], 0.5, ALU.mult)
    TT(P, "dve", y[:], ang[:], ang[:], ALU.mult)
    sc_ = [-1.0 / 6, 1.0 / 120, -1.0 / 5040, 1.0 / 362880, -1.0 / 39916800, 1.0 / 6227020800]
    TS(P, "dve", acc[:], y[:], sc_[5], ALU.mult)
    for k in (4, 3, 2, 1, 0):
        STT(P, acc[:], acc[:], sc_[k], y[:], ALU.add, ALU.mult)
    STT(P, sn[:], acc[:], 1.0, ang[:], ALU.add, ALU.mult)
    cc_ = [-0.5, 1.0 / 24, -1.0 / 720, 1.0 / 40320, -1.0 / 3628800, 1.0 / 479001600, -1.0 / 87178291200]
    TS(P, "dve", acc[:], y[:], cc_[6], ALU.mult)
    for k in (5, 4, 3, 2, 1, 0):
        STT(P, acc[:], acc[:], cc_[k], y[:], ALU.add, ALU.mult)
    TS(P, "dve", acc[:], acc[:], 1.0, ALU.add)
    STT(P, C.sins[:], sn[:], 2.0, acc[:], ALU.mult, ALU.mult)
    TT(P, "dve", y[:], sn[:], sn[:], ALU.mult)
    TS(P, "dve", C.cos2[:], y[:], -2.0, ALU.mult, 1.0, ALU.add)
    TS(P, "dve", C.sins[0:32, :], C.sins[0:32, :], -1.0, ALU.mult)


def emit_ln(C, g, b):
    P, PS = C.P, C.PS
    cv = Carver(C.scr)
    sq = cv.arr(BF16, [128, KC, 512])
    mean = cv.arr(F32, [128, 512])
    ve = cv.arr(F32, [128, 512])
    m2 = cv.arr(F32, [128, 512])
    tt_ = [cv.arr(F32, [128, 512]) for _ in range(3)]
    for nb in range(NB):
        blk = slice(nb * 512, (nb + 1) * 512)
        ACT(P, C.xT[:, :, blk], C.xres[:, :, blk], AF.Copy)
        ACT(P, sq[:], C.xres[:, :, blk], AF.Square)
        s1, s2 = PS.get(), PS.get()
        for k in range(KC):
            MM(P, s1[:], C.ones[:], C.xT[:, k, blk], k == 0, k == KC - 1)
        for k in range(KC):
            MM(P, s2[:], C.ones[:], sq[:, k, :], k == 0, k == KC - 1)
        TS(P, "dve", mean[:], s1[:], 1.0 / D, ALU.mult)
        TS(P, "dve", ve[:], s2[:], 1.0 / D, ALU.mult, LN_EPS, ALU.add)
        PS.put(s1); PS.put(s2)
        TT(P, "dve", m2[:], mean[:], mean[:], ALU.mult)
        TT(P, "dve", ve[:], ve[:], m2[:], ALU.subtract)
        ACT(P, ve[:], ve[:], AF.Ln)
        ACT(P, ve[:], ve[:], AF.Exp, scale=-0.5)
        for k in range(KC):
            t = tt_[k % 3]
            TT(P, "dve", t[:], C.xres[:, k, blk], mean[:], ALU.subtract)
            TT(P, "pool", t[:], t[:], ve[:], ALU.mult)
            ACT(P, C.xres[:, k, blk], t[:], AF.Identity, scale=g(k), bias=b(k))
            ACT(P, C.xT[:, k, blk], t[:], AF.Identity, scale=g(k), bias=b(k))


def emit_mlp(C, l):
    P, PS = C.P, C.PS
    cv = Carver(C.scr)
    hb = [cv.arr(BF16, [128, 4, S]) for _ in range(2)]
    tmp = [cv.arr(F32, [128, 512]) for _ in range(3)]
    w1 = C.dr["mlp_w1"][l].rearrange("(kc p) n -> p kc n", p=128)
    w2 = C.dr["mlp_w2"][l].rearrange("(kc p) n -> p kc n", p=128)
    st = {"ti": 0}

    def mlp1(c):
        (W1,) = C.wslot([(0, [KC, 512], w1[:, :, c * 512:(c + 1) * 512])])
        h = hb[c % 2]
        order = [(m, nb) for m in range(4) for nb in range(NB)] if c > 0 else \
                [(m, nb) for nb in range(NB) for m in range(4)]
        for m, nb in order:
            if True:
                blk = slice(nb * 512, (nb + 1) * 512)
                ps = PS.get()
                for k in range(KC):
                    MM(P, ps[:], W1[:, k, m * 128:(m + 1) * 128], C.xT[:, k, blk], k == 0, k == KC - 1)
                t = tmp[st["ti"] % 3]; st["ti"] += 1
                ACT(P, t[:], ps[:], AF.Relu)
                PS.put(ps)
                TT(P, "pool", h[:, m, blk], t[:], t[:], ALU.mult)

    def mlp2(c):
        (W2,) = C.wslot([(0, [4, D], w2[:, c * 4:(c + 1) * 4, :])])
        h = hb[c % 2]
        order = [(mo, nb) for mo in range(KC) for nb in range(NB)] if c < 7 else \
                [(mo, nb) for nb in range(NB) for mo in range(KC)]
        for mo, nb in order:
            if True:
                blk = slice(nb * 512, (nb + 1) * 512)
                ps = PS.get()
                for k in range(4):
                    MM(P, ps[:], W2[:, k, mo * 128:(mo + 1) * 128], h[:, k, blk], k == 0, k == 3)
                if c == 0:
                    STT(P, C.xres[:, mo, blk], C.xres[:, mo, blk], ALPHA, ps[:], ALU.mult, ALU.add)
                else:
                    TT(P, "dve", C.xres[:, mo, blk], C.xres[:, mo, blk], ps[:], ALU.add)
                PS.put(ps)

    mlp1(0)
    for c in range(8):
        if c + 1 < 8:
            mlp1(c + 1)
        mlp2(c)


def emit_even(C, l):
    P, PS = C.P, C.PS
    j = l // 2
    pv, dv = C.pvf, C.dvf
    cv = Carver(C.scr)
    mixT = cv.arr(BF16, [128, 4, S])
    HW = 1024
    A = [cv.arr(F32, [128, 16 + HW]) for _ in range(2)]
    Bg = [cv.arr(F32, [128, HW]) for _ in range(2)]
    E0 = cv.arr(F32, [128, 16 + HW])
    E1 = cv.arr(F32, [128, 16 + HW])
    C2 = Arr(C.scr, E0.off, F32, [128, HW])
    R = Arr(C.scr, E1.off, F32, [128, HW])
    I_ = cv.arr(F32, [128, HW])
    T = cv.arr(F32, [128, HW])
    cb = [cv.arr(BF16, [128, HW]) for _ in range(2)]
    carry = cv.arr(F32, [128, 8])
    sm = cv.arr(F32, [128, 16])
    w_in = C.dr["even_w_in"][j].rearrange("(kc p) n -> p kc n", p=128)
    w_out = C.dr["even_w_out"][j].rearrange("(kc p) n -> p kc n", p=128)

    cfA, hcfA, hbaA, hbxA = dv(j, 0, 0, KC), dv(j, 1, 0, KC), dv(j, 2, 0, KC), dv(j, 3, 0, KC)
    ACT(P, cfA, pv("lam", j, 0, KC), AF.Exp, scale=-1.0)
    TS(P, "dve", cfA, cfA, 1.0, ALU.add)
    ACT(P, cfA, cfA, AF.Ln)
    TS(P, "dve", cfA, cfA, -8.0, ALU.mult)
    TS(P, "dve", hcfA, cfA, 0.5, ALU.mult)
    TS(P, "dve", hbaA, pv("ba", j, 0, KC), 0.5, ALU.mult)
    TS(P, "dve", hbxA, pv("bx", j, 0, KC), 0.5, ALU.mult)

    GK = math.sqrt(2.0 / math.pi)
    items = [(u, hf) for u in range(12) for hf in range(2)]
    W = {}

    def stageP(i):
        u, hf = items[i]
        a = A[i % 2]
        if hf == 0:
            if u < 4:
                W[u] = C.wslot([(0, [KC, 128], w_in[:, :, u * 128:(u + 1) * 128]),
                                (1024, [128], C.dr["pool_w"][j, u])])
            else:
                h = u - 4
                W[u] = C.wslot([
                    (0, [KC, 128], w_in[:, :, 512 + h * 128:512 + (h + 1) * 128]),
                    (1024, [KC, 128], w_in[:, :, 1536 + h * 128:1536 + (h + 1) * 128]),
                    (2048, [128], C.dr["lru_w_a"][j, h]),
                    (2176, [128], C.dr["lru_w_x"][j, h])])
            MSET(P, "dve", a[:, 0:16], 0.0)
        else:
            CP(P, "dve", a[:, 0:16], A[(i - 1) % 2][:, HW:HW + 16])
        Wu = W[u][0]
        for q in range(2):
            blk = slice(hf * HW + q * 512, hf * HW + (q + 1) * 512)
            ps = PS.get()
            for k in range(KC):
                MM(P, ps[:], Wu[:, k, :], C.xT[:, k, blk], k == 0, k == KC - 1)
            ACT(P, a[:, 16 + q * 512:16 + (q + 1) * 512], ps[:], AF.Copy)
            PS.put(ps)
        if u >= 4:
            Wg = W[u][1]
            for q in range(2):
                blk = slice(hf * HW + q * 512, hf * HW + (q + 1) * 512)
                ps = PS.get()
                for k in range(KC):
                    MM(P, ps[:], Wg[:, k, :], C.xT[:, k, blk], k == 0, k == KC - 1)
                ACT(P, Bg[i % 2][:, q * 512:(q + 1) * 512], ps[:], AF.Copy)
                PS.put(ps)

    def stageE(i):
        u, hf = items[i]
        a = A[i % 2]
        ui = u % 4
        cbi = cb[i % 2]
        if u < 4:
            g = u
            Wp = W[u][1]
            w = 2 << g
            src, sh, lo, ei = a, 1, 0, 0
            exts = [E0, E1]
            while sh < w:
                dst = exts[ei % 2]; ei += 1
                lo2 = lo + sh
                TT(P, "dve", dst[:, lo2:16 + HW], src[:, lo2:16 + HW], src[:, lo2 - sh:16 + HW - sh], ALU.add)
                src, lo, sh = dst, lo2, sh * 2
            STT(P, T[:], src[:, 16:16 + HW], 1.0 / w, a[:, 16:16 + HW], ALU.mult, ALU.subtract)
            if hf == 0:
                TT(P, "dve", sm[:], src[:, 16:32], C.invc[:, g * 16:(g + 1) * 16], ALU.mult)
                TT(P, "dve", T[:, 0:16], sm[:], a[:, 16:32], ALU.subtract)
            ACT(P, cbi[:], T[:], AF.Copy)
            for q in range(2):
                blk = slice(hf * HW + q * 512, hf * HW + (q + 1) * 512)
                ps = PS.get()
                MM(P, ps[:], Wp[:], cbi[:, q * 512:(q + 1) * 512], True, True)
                ACT(P, mixT[:, ui, blk], ps[:], AF.Copy, scale=pv("pscale", j, g, 1))
                PS.put(ps)
        else:
            h = u - 4
            Wa, Wx = W[u][2], W[u][3]
            bg = Bg[i % 2]
            cw = lambda k: pv("convw", j, k * KC + h, 1)
            TS(P, "dve", C2[:], a[:, 16:16 + HW], cw(3), ALU.mult, pv("convb", j, h, 1), ALU.add)
            for k in (2, 1, 0):
                STT(P, C2[:], a[:, 13 + k:13 + k + HW], cw(k), C2[:], ALU.mult, ALU.add)
            ACT(P, cbi[:], C2[:], AF.Copy)
            for q in range(2):
                ps = PS.get()
                MM(P, ps[:], Wa[:], cbi[:, q * 512:(q + 1) * 512], True, True)
                ACT(P, R[:, q * 512:(q + 1) * 512], ps[:], AF.Tanh, scale=0.5, bias=dv(j, 2, h, 1))
                PS.put(ps)
                ps = PS.get()
                MM(P, ps[:], Wx[:], cbi[:, q * 512:(q + 1) * 512], True, True)
                ACT(P, I_[:, q * 512:(q + 1) * 512], ps[:], AF.Tanh, scale=0.5, bias=dv(j, 3, h, 1))
                PS.put(ps)
            ACT(P, T[:], R[:], AF.Exp, scale=dv(j, 0, h, 1), bias=dv(j, 0, h, 1))
            ACT(P, R[:], R[:], AF.Exp, scale=dv(j, 1, h, 1), bias=dv(j, 1, h, 1))
            ACT(P, T[:], T[:], AF.Ln, scale=-1.0, bias=C.onec[:, 0:1])
            ACT(P, T[:], T[:], AF.Exp, scale=0.5)
            STT(P, I_[:], I_[:], 1.0, T[:], ALU.add, ALU.mult)
            STT(P, I_[:], I_[:], 0.5, C2[:], ALU.mult, ALU.mult)
            Hb = T
            if hf == 0:
                P.op("dve", lambda e, o=Hb[:], a_=R[:], b_=I_[:]: e.tensor_tensor_scan(
                    out=o.ap, data0=a_.ap, data1=b_.ap, initial=0.0, op0=ALU.mult, op1=ALU.add),
                    reads=[R[:], I_[:]], writes=[Hb[:]])
            else:
                init = carry[:, h:h + 1]
                P.op("dve", lambda e, o=Hb[:], a_=R[:], b_=I_[:], i0=init: e.tensor_tensor_scan(
                    out=o.ap, data0=a_.ap, data1=b_.ap, initial=i0.ap, op0=ALU.mult, op1=ALU.add),
                    reads=[R[:], I_[:], init], writes=[Hb[:]])
            CP(P, "dve", carry[:, h:h + 1], Hb[:, HW - 1:HW])
            Cg = C2
            ACT(P, Cg[:], bg[:], AF.Square)
            TS(P, "dve", Cg[:], Cg[:], 0.044715, ALU.mult, 1.0, ALU.add)
            TT(P, "dve", Cg[:], Cg[:], bg[:], ALU.mult)
            ACT(P, Cg[:], Cg[:], AF.Tanh, scale=GK)
            STT(P, bg[:], Cg[:], 1.0, bg[:], ALU.add, ALU.mult)
            STT(P, mixT[:, ui, hf * HW:(hf + 1) * HW], bg[:], 0.5, Hb[:], ALU.mult, ALU.mult)
        if ui == 3 and hf == 1:
            grp = u // 4
            (Wo,) = C.wslot([(0, [4, D], w_out[:, grp * 4:(grp + 1) * 4, :])])
            order = [(mo, nb) for mo in range(KC) for nb in range(NB)] if grp < 2 else \
                    [(mo, nb) for nb in range(NB) for mo in range(KC)]
            for mo, nb in order:
                if True:
                    blk = slice(nb * 512, (nb + 1) * 512)
                    ps = PS.get()
                    for k in range(4):
                        MM(P, ps[:], Wo[:, k, mo * 128:(mo + 1) * 128], mixT[:, k, blk], k == 0, k == 3)
                    if grp == 0:
                        STT(P, C.xres[:, mo, blk], C.xres[:, mo, blk], ALPHA, ps[:], ALU.mult, ALU.add)
                    else:
                        TT(P, "dve", C.xres[:, mo, blk], C.xres[:, mo, blk], ps[:], ALU.add)
                    PS.put(ps)

    stageP(0)
    for i in range(len(items)):
        if i + 1 < len(items):
            stageP(i + 1)
        stageE(i)


def emit_odd(C, l):
    P, PS = C.P, C.PS
    j = l // 2
    pv = C.pvf
    cv = Carver(C.scr)
    cqn = cv.arr(BF16, [128, 3, S])
    ckvn = cv.arr(BF16, [128, 2, S])
    kpe = cv.arr(BF16, [64, S])
    qn = cv.arr(BF16, [128, S])
    qpe = cv.arr(BF16, [64, S])
    kn = cv.arr(BF16, [128, S])
    V = cv.arr(BF16, [128, 16, 128])
    E = [cv.arr(BF16, [128, 512]) for _ in range(3)]
    sq = cv.arr(BF16, [128, 3, 512])
    rs = [cv.arr(F32, [128, 512]) for _ in range(2)]
    r1 = [cv.arr(F32, [64, 512]) for _ in range(1)]
    r2 = [cv.arr(F32, [64, 512]) for _ in range(1)]
    oT = Arr(C.xT.buf, 0, BF16, [128, KC, S])
    wd = C.dr["mla_w_down"][j].rearrange("(kc p) n -> p kc n", p=128)
    wds = C.dr["w_down_sw"][j].rearrange("(kc p) n -> p kc n", p=128)
    wqb = C.dr["mla_w_qb"][j].rearrange("(kc p) n -> p kc n", p=128)
    wqs = C.dr["w_qb_sw"][j].rearrange("(kc p) n -> p kc n", p=128)
    wkv = C.dr["mla_w_kvb"][j].rearrange("(kc p) n -> p kc n", p=128)
    wo = C.dr["mla_w_o"][j].rearrange("(kc p) n -> p kc n", p=128)
    SCALE = 192.0 ** -0.5

    (Wd1,) = C.wslot([(0, [KC, 384], wd[:, :, 0:384])])
    Wd2, Wd3 = C.wslot([(0, [KC, 320], wd[:, :, 384:704]), (2560, [KC, 64], wds)])

    def rope(dst, pa, pb, blk, ri):
        TT(P, "dve", r1[ri][:], pa, C.cos2[:, blk], ALU.mult)
        TT(P, "dve", r2[ri][:], pb, C.sins[:, blk], ALU.mult)
        TT(P, "dve", dst, r1[ri][:], r2[ri][:], ALU.add)

    def rmsn(dst, W, ncol, nch, gname, blk, ri):
        banks = []
        for m in range(nch):
            ps = PS.get()
            for k in range(KC):
                MM(P, ps[:], W[:, k, m * 128:(m + 1) * 128], C.xT[:, k, blk], k == 0, k == KC - 1)
            ACT(P, sq[:, m, :], ps[:], AF.Square)
            banks.append(ps)
        s2 = PS.get()
        for m in range(nch):
            MM(P, s2[:], C.ones[:], sq[:, m, :], m == 0, m == nch - 1)
        ve = rs[ri]
        TS(P, "dve", ve[:], s2[:], 1.0 / ncol, ALU.mult, RMS_EPS, ALU.add)
        PS.put(s2)
        ACT(P, ve[:], ve[:], AF.Ln)
        ACT(P, ve[:], ve[:], AF.Exp, scale=-0.5)
        for m in range(nch):
            STT(P, dst[:, m, blk], banks[m][:], pv(gname, j, m, 1), ve[:], ALU.mult, ALU.mult)
            PS.put(banks[m])

    for nb in range(NB):
        blk = slice(nb * 512, (nb + 1) * 512)
        rmsn(cqn, Wd1, 384, 3, "qg", blk, 0)
        rmsn(ckvn, Wd2, 256, 2, "kvg", blk, 1)
        pa, pb = PS.get(), PS.get()
        for k in range(KC):
            MM(P, pa[0:64, :], Wd2[:, k, 256:320], C.xT[:, k, blk], k == 0, k == KC - 1)
        for k in range(KC):
            MM(P, pb[0:64, :], Wd3[:, k, :], C.xT[:, k, blk], k == 0, k == KC - 1)
        rope(kpe[:, blk], pa[0:64, :], pb[0:64, :], blk, 0)
        PS.put(pa); PS.put(pb)

    st_e = {"ei": 0}
    for h in range(8):
        Wq, Wqs, Wkv = C.wslot([(0, [3, 192], wqb[:, :, h * 192:(h + 1) * 192]),
                                (576, [3, 64], wqs[:, :, h * 64:(h + 1) * 64]),
                                (768, [2, 256], wkv[0:128, :, h * 256:(h + 1) * 256])])
        for nb in range(NB):
            blk = slice(nb * 512, (nb + 1) * 512)
            ps = PS.get()
            for k in range(3):
                MM(P, ps[:], Wq[:, k, 0:128], cqn[:, k, blk], k == 0, k == 2)
            ACT(P, qn[:, blk], ps[:], AF.Copy)
            PS.put(ps)
            pa, pb = PS.get(), PS.get()
            for k in range(3):
                MM(P, pa[0:64, :], Wq[:, k, 128:192], cqn[:, k, blk], k == 0, k == 2)
            for k in range(3):
                MM(P, pb[0:64, :], Wqs[:, k, :], cqn[:, k, blk], k == 0, k == 2)
            rope(qpe[:, blk], pa[0:64, :], pb[0:64, :], blk, 0)
            PS.put(pa); PS.put(pb)
            ps = PS.get()
            for k in range(2):
                MM(P, ps[:], Wkv[:, k, 0:128], ckvn[:, k, blk], k == 0, k == 1)
            ACT(P, kn[:, blk], ps[:], AF.Copy)
            PS.put(ps)
        for t4 in range(4):
            ps = PS.get()
            for tt in range(4):
                t = t4 * 4 + tt
                for k in range(2):
                    MM(P, ps[:, tt * 128:(tt + 1) * 128], ckvn[:, k, t * 128:(t + 1) * 128], Wkv[:, k, 128:256],
                       k == 0, k == 1, inc=(k == 1 and tt == 3))
            ACT(P, V[:, t4 * 4:(t4 + 1) * 4, :], ps[:], AF.Copy)
            PS.put(ps)
        for qb in range(NB):
            num, den = PS.get(), PS.get()
            nkt = 4 * qb + 4
            def sc_stage(kt):
                q0 = max(512 * qb, 128 * kt)
                N = 512 * qb + 512 - q0
                c0 = q0 - 512 * qb
                kts = slice(kt * 128, (kt + 1) * 128)
                sc = PS.get()
                MM(P, sc[:, 0:N], kn[:, kts], qn[:, q0:q0 + N], True, False)
                MM(P, sc[:, 0:N], kpe[:, kts], qpe[:, q0:q0 + N], False, True)
                Eb = E[st_e["ei"] % 3]; st_e["ei"] += 1
                ACT(P, Eb[:, 0:N], sc[:, 0:N], AF.Exp, scale=SCALE)
                PS.put(sc)
                if kt >= 4 * qb:
                    MSET(P, "dve", Eb[64:128, 0:64], 0.0)
                return Eb, N, c0

            pend = sc_stage(0)
            for kt in range(nkt):
                nxt = sc_stage(kt + 1) if kt + 1 < nkt else None
                Eb, N, c0 = pend
                MM(P, num[:, c0:c0 + N], V[:, kt, :], Eb[:, 0:N], kt == 0, kt == nkt - 1, inc=False)
                MM(P, den[:, c0:c0 + N], C.ones[:], Eb[:, 0:N], kt == 0, kt == nkt - 1, inc=True)
                pend = nxt
            rc = rs[qb % 2]
            P.op("dve", lambda e, o=rc[:], i=den[:]: e.reciprocal(out=o.ap, in_=i.ap), reads=[den[:]], writes=[rc[:]])
            TT(P, "dve", oT[:, h, qb * 512:(qb + 1) * 512], num[:], rc[:], ALU.mult)
            PS.put(num); PS.put(den)

    Wo1 = C.wslot([(0, [4, D], wo[:, 0:4, :])])[0]
    Wo2 = C.wslot([(0, [4, D], wo[:, 4:8, :])])[0]
    for nb in range(NB):
        for mo in range(KC):
            blk = slice(nb * 512, (nb + 1) * 512)
            ps = PS.get()
            for k in range(KC):
                W = Wo1 if k < 4 else Wo2
                MM(P, ps[:], W[:, k % 4, mo * 128:(mo + 1) * 128], oT[:, k, blk], k == 0, k == KC - 1)
            STT(P, C.xres[:, mo, blk], C.xres[:, mo, blk], ALPHA, ps[:], ALU.mult, ALU.add)
            PS.put(ps)


_NC_CACHE = {}


def _host_prep(inp):
    f = np.float32
    pvec = np.zeros((128, PV_N), f)

    def put(key, vec):
        v = np.asarray(vec, f)
        n = v.shape[0] // 128
        o = PV_COLS[key]
        pvec[:, o:o + n] = v.reshape(n, 128).T

    for l in range(DEPTH):
        put(("lmg", l), inp["ln_mix_g"][l]); put(("lmb", l), inp["ln_mix_b"][l])
        put(("lfg", l), inp["ln_ffn_g"][l]); put(("lfb", l), inp["ln_ffn_b"][l])
    for j in range(2):
        put(("pscale", j), inp["pool_scale"][j])
        cw = np.asarray(inp["lru_conv_w"][j], f)
        o = PV_COLS[("convw", j)]
        for k in range(4):
            pvec[:, o + k * KC:o + (k + 1) * KC] = cw[k].reshape(KC, 128).T
        put(("convb", j), inp["lru_conv_b"][j]); put(("ba", j), inp["lru_b_a"][j])
        put(("bx", j), inp["lru_b_x"][j]); put(("lam", j), inp["lru_lambda"][j])
        put(("qg", j), inp["mla_q_norm_g"][j]); put(("kvg", j), inp["mla_kv_norm_g"][j])
    inv_freq = (10000.0 ** (-np.arange(0, 64, 2, dtype=f) / f(64))).astype(f)
    invf2 = np.concatenate([inv_freq, inv_freq]).reshape(64, 1).astype(f)
    invc = np.zeros((128, 64), f)
    for g, w in enumerate((2, 4, 8, 16)):
        invc[:, g * 16:(g + 1) * 16] = (1.0 / np.minimum(np.arange(16) + 1, w)).astype(f)[None, :]
    wd = np.asarray(inp["mla_w_down"], f)
    w_down_sw = np.ascontiguousarray(np.concatenate([wd[:, :, 672:704], wd[:, :, 640:672]], axis=2))
    wq = np.asarray(inp["mla_w_qb"], f).reshape(2, 384, 8, 192)
    w_qb_sw = np.ascontiguousarray(np.concatenate([wq[..., 160:192], wq[..., 128:160]], axis=3).reshape(2, 384, 512))
    shared = {"pvec": pvec, "invf2": invf2, "invc": invc, "w_down_sw": w_down_sw, "w_qb_sw": w_qb_sw}
    for nm in ("even_w_in", "pool_w", "lru_w_a", "lru_w_x", "even_w_out", "mla_w_down", "mla_w_qb",
               "mla_w_kvb", "mla_w_o", "mlp_w1", "mlp_w2"):
        shared[nm] = np.ascontiguousarray(np.asarray(inp[nm], f))
    return shared


LAUNCH_GROUPS = [(0, 1, 2, 3)]


def kernel(**inp):
    shared = _host_prep(inp)
    x = np.asarray(inp["x"], np.float32)
    pos = np.asarray(inp["positions"], np.int32)
    cur = [np.ascontiguousarray(x[b].T) for b in range(NCORES)]
    for grp in LAUNCH_GROUPS:
        if grp not in _NC_CACHE:
            _NC_CACHE[grp] = build(grp)
        nc = _NC_CACHE[grp]
        in_maps = []
        for b in range(NCORES):
            m = dict(shared)
            m["xT"] = cur[b]
            m["pos"] = np.ascontiguousarray(pos[b][None, :])
            in_maps.append(m)
        res = run_bass_kernel_spmd(nc, in_maps, core_ids=list(range(NCORES)))
        cur = [np.ascontiguousarray(res.results[b]["yT"]) for b in range(NCORES)]
    return np.stack([c.T for c in cur], axis=0).astype(np.float32)
```

```python
import math
import numpy as np
import concourse.bass as bass
import concourse.mybir as mybir
from concourse.bass_utils import run_bass_kernel_spmd

F32 = mybir.dt.float32
BF16 = mybir.dt.bfloat16
I32 = mybir.dt.int32
AF = mybir.ActivationFunctionType
ALU = mybir.AluOpType

NCORES = 8
S = 2048
D = 1024
DEPTH = 4
KC = D // 128
NB = S // 512
ALPHA = (2 * DEPTH) ** 0.25
LN_EPS = 1e-5
RMS_EPS = 1e-6


class View:
    __slots__ = ("buf", "ap", "ivs")

    def __init__(self, buf, ap, ivs):
        self.buf, self.ap, self.ivs = buf, ap, ivs


def _merge(ivs):
    ivs = sorted(ivs)
    out = [list(ivs[0])]
    for lo, hi in ivs[1:]:
        if lo <= out[-1][1]:
            out[-1][1] = max(out[-1][1], hi)
        else:
            out.append([lo, hi])
    return tuple((a, b) for a, b in out)


def _overlap(a, b):
    for lo, hi in a:
        for lo2, hi2 in b:
            if lo < hi2 and lo2 < hi:
                return True
    return False


def _covers(a, b):
    for lo2, hi2 in b:
        ok = False
        for lo, hi in a:
            if lo <= lo2 and hi2 <= hi:
                ok = True
                break
        if not ok:
            return False
    return True


class Buf:
    def __init__(self, prog, name, shape, dtype, space="sbuf"):
        self.prog, self.name, self.shape, self.dtype = prog, name, list(shape), dtype
        self.esz = 2 if dtype == BF16 else 4
        nc = prog.nc
        if space == "sbuf":
            self.t = nc.alloc_sbuf_tensor(name, self.shape, dtype)
        else:
            self.t = nc.alloc_psum_tensor(name, self.shape, dtype)
        st = []
        acc = 1
        for n in reversed(self.shape[1:]):
            st.append(acc)
            acc *= n
        self.strides = list(reversed(st))
        self.hist = []

    def __getitem__(self, idx):
        if not isinstance(idx, tuple):
            idx = (idx,)
        ap = self.t[idx]
        fidx = list(idx[1:]) + [slice(None)] * (len(self.shape) - len(idx))
        rngs = []
        for i, n in zip(fidx, self.shape[1:]):
            if isinstance(i, int):
                rngs.append((i, i + 1))
            else:
                a, b, _ = i.indices(n)
                rngs.append((a, b))
        outer = rngs[:-1]
        cnt = 1
        for a, b in outer:
            cnt *= (b - a)
        la, lb = rngs[-1]
        if cnt > 64:
            lo = sum(a * s for (a, b), s in zip(rngs, self.strides))
            hi = sum((b - 1) * s for (a, b), s in zip(rngs, self.strides)) + 1
            ivs = ((lo * self.esz, hi * self.esz),)
        else:
            starts = [0]
            for (a, b), s in zip(outer, self.strides[:-1]):
                starts = [o + j * s for o in starts for j in range(a, b)]
            ivs = _merge([((o + la) * self.esz, (o + lb) * self.esz) for o in starts])
        return View(self, ap, ivs)


class Prog:
    ENGS = ("pe", "act", "dve", "pool", "sp")

    def __init__(self, nc, n_dma_sems=24):
        self.nc = nc
        self.streams = {e: [] for e in self.ENGS}
        self.sems = {}
        self.count = {}
        self.waited = {e: {} for e in self.ENGS}
        for e in self.ENGS:
            self.sems[e] = nc.alloc_semaphore(name="s_" + e)
            self.count[e] = 0
        self.dma_sems = []
        for i in range(n_dma_sems):
            nm = "q%d" % i
            self.sems[nm] = nc.alloc_semaphore(name="s_" + nm)
            self.count[nm] = 0
            self.dma_sems.append(nm)
        self.dma_rr = 0
        self.n_ops = 0

    def _deps(self, eng, reads, writes):
        need = {}
        for v in reads:
            for rec in v.buf.hist:
                if rec[2] and _overlap(rec[3], v.ivs):
                    if not (eng == "pe" and rec[0] == "pe"):
                        if need.get(rec[0], 0) < rec[1]:
                            need[rec[0]] = rec[1]
        for v in writes:
            for rec in v.buf.hist:
                if _overlap(rec[3], v.ivs):
                    if not (eng == "pe" and rec[0] == "pe"):
                        if need.get(rec[0], 0) < rec[1]:
                            need[rec[0]] = rec[1]
        return need

    def _record(self, who, cnt, reads, writes):
        for v in reads:
            h = v.buf.hist
            for rec in h:
                if (not rec[2]) and rec[0] == who and rec[3] == v.ivs:
                    rec[1] = max(rec[1], cnt)
                    break
            else:
                h.append([who, cnt, False, v.ivs])
        for v in writes:
            h = v.buf.hist
            h[:] = [rec for rec in h if not _covers(v.ivs, rec[3])]
            h.append([who, cnt, True, v.ivs])

    def _waits(self, eng, need):
        w = []
        for f, c in need.items():
            if self.waited[eng].get(f, 0) < c:
                self.waited[eng][f] = c
                w.append((f, c))
        return w

    def op(self, eng, fn, reads=(), writes=(), inc=True):
        need = self._deps(eng, reads, writes)
        waits = self._waits(eng, need)
        cnt = self.count[eng] + 1
        if inc:
            self.count[eng] = cnt
        self._record(eng, cnt, reads, writes)
        self.streams[eng].append((waits, fn, (eng, 1) if inc else None))
        self.n_ops += 1

    def dma(self, queue, out, in_, out_view=None, in_view=None, **kw):
        reads = [in_view] if in_view is not None else []
        writes = [out_view] if out_view is not None else []
        need = self._deps(queue, reads, writes)
        q = self.dma_sems[self.dma_rr % len(self.dma_sems)]
        self.dma_rr += 1
        if self.count[q] > 0:
            need[q] = max(need.get(q, 0), self.count[q])
        waits = self._waits(queue, need)
        self.count[q] += 16
        cnt = self.count[q]
        self._record(q, cnt, reads, writes)
        self.streams[queue].append((waits, lambda e: e.dma_start(out=out, in_=in_, **kw), (q, 16)))
        self.n_ops += 1
        return (q, cnt)

    def wait_on(self, eng, who, cnt):
        w = self._waits(eng, {who: cnt})
        if w:
            self.streams[eng].append((w, None, None))

    def emit(self):
        nc = self.nc
        engobj = {"pe": "tensor", "act": "scalar", "dve": "vector", "pool": "gpsimd", "sp": "sync"}
        with nc.Block() as block:
            for e in self.ENGS:
                stream = self.streams[e]

                def body(eng, stream=stream):
                    for waits, fn, inc in stream:
                        for (f, c) in waits:
                            eng.wait_ge(self.sems[f], c)
                        if fn is None:
                            continue
                        ins = fn(eng)
                        if inc is not None:
                            ins.then_inc(self.sems[inc[0]], inc[1])

                getattr(block, engobj[e])(body)


class Arr:
    def __init__(self, buf, byte_off, dtype, shape):
        self.buf, self.off, self.dtype, self.shape = buf, byte_off, dtype, list(shape)
        self.esz = 2 if dtype == BF16 else 4
        n = 1
        for d in shape[1:]:
            n *= d
        assert byte_off % 4 == 0 and (n * self.esz) % 4 == 0
        lo, hi = byte_off // 4, (byte_off + n * self.esz) // 4
        assert hi <= buf.shape[1], (buf.name, hi, buf.shape)
        ap = buf.t[0:shape[0], lo:hi]
        if dtype != F32:
            ap = ap.bitcast(dtype)
        fd = shape[1:]
        if len(fd) == 2:
            ap = ap.rearrange("p (a b) -> p a b", a=fd[0])
        elif len(fd) == 3:
            ap = ap.rearrange("p (a b c) -> p a b c", a=fd[0], b=fd[1])
        self.full = ap
        st, acc = [], 1
        for d in reversed(fd):
            st.append(acc)
            acc *= d
        self.strides = list(reversed(st))
        self.nbytes = n * self.esz

    def __getitem__(self, idx):
        if not isinstance(idx, tuple):
            idx = (idx,)
        ap = self.full[idx]
        fidx = list(idx[1:]) + [slice(None)] * (len(self.shape) - len(idx))
        rngs = []
        for i, n in zip(fidx, self.shape[1:]):
            if isinstance(i, int):
                rngs.append((i, i + 1))
            else:
                a, b, _ = i.indices(n)
                rngs.append((a, b))
        outer = rngs[:-1]
        cnt = 1
        for a, b in outer:
            cnt *= (b - a)
        la, lb = rngs[-1]
        if cnt > 64:
            lo = sum(a * s for (a, b), s in zip(rngs, self.strides))
            hi = sum((b - 1) * s for (a, b), s in zip(rngs, self.strides)) + 1
            ivs = ((self.off + lo * self.esz, self.off + hi * self.esz),)
        else:
            starts = [0]
            for (a, b), s in zip(outer, self.strides[:-1]):
                starts = [o + j * s for o in starts for j in range(a, b)]
            ivs = _merge([(self.off + (o + la) * self.esz, self.off + (o + lb) * self.esz) for o in starts])
        return View(self.buf, ap, ivs)


class Carver:
    def __init__(self, buf):
        self.buf, self.pos = buf, 0

    def arr(self, dtype, shape):
        a = Arr(self.buf, self.pos, dtype, shape)
        self.pos += (a.nbytes + 31) // 32 * 32
        assert self.pos <= self.buf.shape[1] * 4, (self.buf.name, self.pos)
        return a


class PsumPool:
    def __init__(self, prog):
        self.banks = [Buf(prog, "psb%d" % i, [128, 512], F32, space="psum") for i in range(8)]
        self.free = list(range(8))

    def get(self):
        assert self.free, "out of PSUM banks"
        return self.banks[self.free.pop(0)]

    def put(self, b):
        self.free.append(self.banks.index(b))


def MM(P, out, lhsT, rhs, start, stop, inc=None):
    if inc is None:
        inc = stop
    P.op("pe", lambda e: e.matmul(out.ap, lhsT=lhsT.ap, rhs=rhs.ap, start=start, stop=stop),
         reads=[lhsT, rhs], writes=[out], inc=inc)


def ACT(P, out, in_, func, scale=1.0, bias=None):
    reads = [in_]
    kw = {}
    if isinstance(scale, View):
        reads.append(scale)
        kw["scale"] = scale.ap
    else:
        kw["scale"] = float(scale)
    if isinstance(bias, View):
        reads.append(bias)
        kw["bias"] = bias.ap
    elif bias is not None:
        kw["bias"] = float(bias)
    P.op("act", lambda e: e.activation(out=out.ap, in_=in_.ap, func=func, **kw), reads=reads, writes=[out])


def TS(P, eng, out, in0, s1, op0, s2=None, op1=None):
    reads = [in0]
    a1 = s1
    if isinstance(s1, View):
        reads.append(s1)
        a1 = s1.ap
    a2 = s2
    if isinstance(s2, View):
        reads.append(s2)
        a2 = s2.ap
    if op1 is None:
        P.op(eng, lambda e: e.tensor_scalar(out=out.ap, in0=in0.ap, scalar1=a1, scalar2=None, op0=op0),
             reads=reads, writes=[out])
    else:
        P.op(eng, lambda e: e.tensor_scalar(out=out.ap, in0=in0.ap, scalar1=a1, scalar2=a2, op0=op0, op1=op1),
             reads=reads, writes=[out])


def TT(P, eng, out, in0, in1, op):
    P.op(eng, lambda e: e.tensor_tensor(out=out.ap, in0=in0.ap, in1=in1.ap, op=op), reads=[in0, in1], writes=[out])


def STT(P, out, in0, sc, in1, op0, op1):
    reads = [in0, in1]
    a = sc
    if isinstance(sc, View):
        reads.append(sc)
        a = sc.ap
    P.op("dve", lambda e: e.scalar_tensor_tensor(out=out.ap, in0=in0.ap, scalar=a, in1=in1.ap, op0=op0, op1=op1),
         reads=reads, writes=[out])


def CP(P, eng, out, in_):
    P.op(eng, lambda e: e.tensor_copy(out=out.ap, in_=in_.ap), reads=[in_], writes=[out])


def MSET(P, eng, out, val):
    P.op(eng, lambda e: e.memset(out.ap, val), writes=[out])


def _pvec_layout():
    cols = {}
    n = 0
    for l in range(DEPTH):
        for nm in ("lmg", "lmb", "lfg", "lfb"):
            cols[(nm, l)] = n
            n += KC
    for j in range(2):
        cols[("pscale", j)] = n; n += 4
        cols[("convw", j)] = n; n += 4 * KC
        for nm in ("convb", "ba", "bx", "lam"):
            cols[(nm, j)] = n; n += KC
        cols[("qg", j)] = n; n += 3
        cols[("kvg", j)] = n; n += 2
    return cols, n


PV_COLS, PV_N = _pvec_layout()
DV_N = 2 * 4 * KC


class Ctx:
    pass


def build(layers=(0, 1, 2, 3)):
    nc = bass.Bass("TRN2", target_bir_lowering=False)
    P = Prog(nc)
    C = Ctx()
    C.P, C.nc = P, nc
    dr = {}

    def din(name, shape, dt=F32):
        dr[name] = nc.dram_tensor(name, list(shape), dt, kind="ExternalInput").ap()
        return dr[name]

    din("xT", [D, S]); din("pos", [1, S], I32); din("pvec", [128, PV_N]); din("invf2", [64, 1]); din("invc", [128, 64])
    din("even_w_in", [2, D, 2560]); din("pool_w", [2, 4, 128, 128]); din("lru_w_a", [2, 8, 128, 128])
    din("lru_w_x", [2, 8, 128, 128]); din("even_w_out", [2, 1536, D])
    din("mla_w_down", [2, D, 704]); din("w_down_sw", [2, D, 64]); din("mla_w_qb", [2, 384, 1536])
    din("w_qb_sw", [2, 384, 512]); din("mla_w_kvb", [2, 256, 2048]); din("mla_w_o", [2, D, D])
    din("mlp_w1", [DEPTH, D, 4096]); din("mlp_w2", [DEPTH, 4096, D])
    yT = nc.dram_tensor("yT", [D, S], F32, kind="ExternalOutput").ap()
    C.dr = dr

    xres_b = Buf(P, "xres", [128, KC * S], F32)
    xt_b = Buf(P, "xtb", [128, KC * S // 2], F32)
    wr_b = [Buf(P, "wring%d" % i, [128, 2048], F32) for i in range(4)]
    rope_b = Buf(P, "rope", [128, 2 * S], F32)
    cst_b = Buf(P, "cst", [128, PV_N + DV_N + 64 + 8 + 64 + 512 + 512 + 32], F32)
    scr_b = Buf(P, "scr", [128, 14080], F32)
    C.xres = Arr(xres_b, 0, F32, [128, KC, S])
    C.xT = Arr(xt_b, 0, BF16, [128, KC, S])
    cc = Carver(cst_b)
    C.pv = cc.arr(F32, [128, PV_N + DV_N])
    C.invc = cc.arr(F32, [128, 64])
    C.invf2 = cc.arr(F32, [64, 1])
    C.ones = cc.arr(BF16, [128, 128])
    C.halfc = cc.arr(F32, [128, 512])
    C.onec = cc.arr(F32, [128, 8])
    C.nhalfc = cc.arr(F32, [128, 512])
    C.cos2 = Arr(rope_b, 0, F32, [64, S])
    C.sins = Arr(rope_b, S * 4, F32, [64, S])
    C.scr = scr_b
    C.PS = PsumPool(P)
    C.wr_b = wr_b
    C.wi = 0

    def pv(nm, idx, c0=0, n=1, p=128):
        o = PV_COLS[(nm, idx)] + c0
        return C.pv[0:p, o:o + n]

    def dv(j, which, c0=0, n=1):
        o = PV_N + j * 4 * KC + which * KC + c0
        return C.pv[:, o:o + n]

    C.pvf, C.dvf = pv, dv

    def wslot(loads):
        buf = wr_b[C.wi % len(wr_b)]
        C.wi += 1
        arrs = []
        for off, shp, src in loads:
            a = Arr(buf, off * 2, BF16, [128] + list(shp))
            v = a[:]
            P.dma("pool", v.ap, src, out_view=v)
            arrs.append(a)
        return arrs

    C.wslot = wslot

    v = C.pv[:, 0:PV_N]
    P.dma("sp", v.ap, dr["pvec"], out_view=v)
    v = C.invc[:]
    P.dma("sp", v.ap, dr["invc"], out_view=v)
    v = C.invf2[:]
    P.dma("sp", v.ap, dr["invf2"], out_view=v)
    xsrc = dr["xT"].rearrange("(c p) s -> p c s", p=128)
    for c in range(KC):
        v = C.xres[:, c, :]
        P.dma("sp", v.ap, xsrc[:, c, :], out_view=v)
    for hf in range(2):
        for c in range(KC):
            v = C.xT[:, c, hf * 1024:(hf + 1) * 1024]
            P.dma("pool", v.ap, xsrc[:, c, hf * 1024:(hf + 1) * 1024], out_view=v)
    MSET(P, "dve", C.ones[:], 1.0)
    MSET(P, "dve", C.halfc[:], 0.5)
    MSET(P, "dve", C.onec[:], 1.0)
    MSET(P, "dve", C.nhalfc[:], -0.5)

    if any(l % 2 == 1 for l in layers):
        emit_rope(C)
    for l in layers:
        if l % 2 == 0:
            emit_even(C, l)
        else:
            emit_odd(C, l)
        emit_ln(C, lambda k, l=l: pv("lmg", l, k, 1), lambda k, l=l: pv("lmb", l, k, 1))
        emit_mlp(C, l)
        emit_ln(C, lambda k, l=l: pv("lfg", l, k, 1), lambda k, l=l: pv("lfb", l, k, 1))

    ysrc = yT.rearrange("(c p) s -> p c s", p=128)
    for c in range(KC):
        v = C.xres[:, c, :]
        P.dma("sp", ysrc[:, c, :], v.ap, in_view=v)
    for q in P.dma_sems:
        if P.count[q] > 0:
            P.wait_on("sp", q, P.count[q])
    P.emit()
    return nc


def emit_rope(C):
    P = C.P
    cv = Carver(C.scr)
    pi_ = cv.arr(I32, [64, S])
    ang = cv.arr(F32, [64, S])
    kk = cv.arr(F32, [64, S])
    y = cv.arr(F32, [64, S])
    acc = cv.arr(F32, [64, S])
    sn = cv.arr(F32, [64, S])
    v = pi_[:]
    P.dma("sp", v.ap, C.dr["pos"].partition_broadcast(64), out_view=v)
    CP(P, "dve", ang[:], pi_[:])
    TS(P, "dve", ang[:], ang[:], C.invf2[:, 0:1], ALU.mult)
    MAGIC = 12582912.0
    TS(P, "dve", kk[:], ang[:], 1.0 / (2 * math.pi), ALU.mult, MAGIC, ALU.add)
    TS(P, "dve", kk[:], kk[:], MAGIC, ALU.subtract)
    c1 = 6.28125
    c2 = float(np.float32(2 * math.pi - c1))
    c3 = float(2 * math.pi - c1 - c2)
    for c in (c1, c2, c3):
        STT(P, ang[:], kk[:], -c, ang[:], ALU.mult, ALU.add)
    TS(P, "dve", ang[:], ang[:], 0.5, ALU.mult)
    TT(P, "dve", y[:], ang[:], ang[:], ALU.mult)
    sc_ = [-1.0 / 6, 1.0 / 120, -1.0 / 5040, 1.0 / 362880, -1.0 / 39916800, 1.0 / 6227020800]
    TS(P, "dve", acc[:], y[:], sc_[5], ALU.mult)
    for k in (4, 3, 2, 1, 0):
        STT(P, acc[:], acc[:], sc_[k], y[:], ALU.add, ALU.mult)
    STT(P, sn[:], acc[:], 1.0, ang[:], ALU.add, ALU.mult)
    cc_ = [-0.5, 1.0 / 24, -1.0 / 720, 1.0 / 40320, -1.0 / 3628800, 1.0 / 479001600, -1.0 / 87178291200]
    TS(P, "dve", acc[:], y[:], cc_[6], ALU.mult)
    for k in (5, 4, 3, 2, 1, 0):
        STT(P, acc[:], acc[:], cc_[k], y[:], ALU.add, ALU.mult)
    TS(P, "dve", acc[:], acc[:], 1.0, ALU.add)
    STT(P, C.sins[:], sn[:], 2.0, acc[:], ALU.mult, ALU.mult)
    TT(P, "dve", y[:], sn[:], sn[:], ALU.mult)
    TS(P, "dve", C.cos2[:], y[:], -2.0, ALU.mult, 1.0, ALU.add)
    TS(P, "dve", C.sins[0:32, :], C.sins[0:32, :], -1.0, ALU.mult)


def emit_ln(C, g, b):
    P, PS = C.P, C.PS
    cv = Carver(C.scr)
    sq = cv.arr(BF16, [128, KC, 512])
    mean = cv.arr(F32, [128, 512])
    ve = cv.arr(F32, [128, 512])
    m2 = cv.arr(F32, [128, 512])
    tt_ = [cv.arr(F32, [128, 512]) for _ in range(3)]
    for nb in range(NB):
        blk = slice(nb * 512, (nb + 1) * 512)
        CP(P, "dve", C.xT[:, :, blk], C.xres[:, :, blk])
        ACT(P, sq[:], C.xres[:, :, blk], AF.Square)
        s1, s2 = PS.get(), PS.get()
        for k in range(KC):
            MM(P, s1[:], C.ones[:], C.xT[:, k, blk], k == 0, k == KC - 1)
        for k in range(KC):
            MM(P, s2[:], C.ones[:], sq[:, k, :], k == 0, k == KC - 1)
        TS(P, "dve", mean[:], s1[:], 1.0 / D, ALU.mult)
        TS(P, "dve", ve[:], s2[:], 1.0 / D, ALU.mult, LN_EPS, ALU.add)
        PS.put(s1); PS.put(s2)
        TT(P, "dve", m2[:], mean[:], mean[:], ALU.mult)
        TT(P, "dve", ve[:], ve[:], m2[:], ALU.subtract)
        ACT(P, ve[:], ve[:], AF.Ln)
        ACT(P, ve[:], ve[:], AF.Exp, scale=-0.5)
        for k in range(KC):
            t = tt_[k % 3]
            TT(P, "dve", t[:], C.xres[:, k, blk], mean[:], ALU.subtract)
            TT(P, "pool", t[:], t[:], ve[:], ALU.mult)
            ACT(P, C.xres[:, k, blk], t[:], AF.Identity, scale=g(k), bias=b(k))
            ACT(P, C.xT[:, k, blk], t[:], AF.Identity, scale=g(k), bias=b(k))


def emit_mlp(C, l):
    P, PS = C.P, C.PS
    cv = Carver(C.scr)
    hb = [cv.arr(BF16, [128, 4, S]) for _ in range(2)]
    tmp = [cv.arr(F32, [128, 512]) for _ in range(3)]
    w1 = C.dr["mlp_w1"][l].rearrange("(kc p) n -> p kc n", p=128)
    w2 = C.dr["mlp_w2"][l].rearrange("(kc p) n -> p kc n", p=128)
    st = {"ti": 0}

    def mlp1(c):
        (W1,) = C.wslot([(0, [KC, 512], w1[:, :, c * 512:(c + 1) * 512])])
        h = hb[c % 2]
        order = [(m, nb) for m in range(4) for nb in range(NB)] if c > 0 else \
                [(m, nb) for nb in range(NB) for m in range(4)]
        for m, nb in order:
            if True:
                blk = slice(nb * 512, (nb + 1) * 512)
                ps = PS.get()
                for k in range(KC):
                    MM(P, ps[:], W1[:, k, m * 128:(m + 1) * 128], C.xT[:, k, blk], k == 0, k == KC - 1)
                t = tmp[st["ti"] % 3]; st["ti"] += 1
                ACT(P, t[:], ps[:], AF.Relu)
                PS.put(ps)
                TT(P, "pool", h[:, m, blk], t[:], t[:], ALU.mult)

    def mlp2(c):
        (W2,) = C.wslot([(0, [4, D], w2[:, c * 4:(c + 1) * 4, :])])
        h = hb[c % 2]
        order = [(mo, nb) for mo in range(KC) for nb in range(NB)] if c < 7 else \
                [(mo, nb) for nb in range(NB) for mo in range(KC)]
        for mo, nb in order:
            if True:
                blk = slice(nb * 512, (nb + 1) * 512)
                ps = PS.get()
                for k in range(4):
                    MM(P, ps[:], W2[:, k, mo * 128:(mo + 1) * 128], h[:, k, blk], k == 0, k == 3)
                if c == 0:
                    STT(P, C.xres[:, mo, blk], C.xres[:, mo, blk], ALPHA, ps[:], ALU.mult, ALU.add)
                else:
                    TT(P, "dve", C.xres[:, mo, blk], C.xres[:, mo, blk], ps[:], ALU.add)
                PS.put(ps)

    mlp1(0)
    for c in range(8):
        if c + 1 < 8:
            mlp1(c + 1)
        mlp2(c)


def emit_even(C, l):
    P, PS = C.P, C.PS
    j = l // 2
    pv, dv = C.pvf, C.dvf
    cv = Carver(C.scr)
    mixT = cv.arr(BF16, [128, 4, S])
    HW = 1024
    A = [cv.arr(F32, [128, 16 + HW]) for _ in range(2)]
    Bg = [cv.arr(F32, [128, HW]) for _ in range(2)]
    E0 = cv.arr(F32, [128, 16 + HW])
    E1 = cv.arr(F32, [128, 16 + HW])
    C2 = Arr(C.scr, E0.off, F32, [128, HW])
    R = Arr(C.scr, E1.off, F32, [128, HW])
    I_ = cv.arr(F32, [128, HW])
    T = cv.arr(F32, [128, HW])
    cb = [cv.arr(BF16, [128, HW]) for _ in range(2)]
    carry = cv.arr(F32, [128, 8])
    sm = cv.arr(F32, [128, 16])
    w_in = C.dr["even_w_in"][j].rearrange("(kc p) n -> p kc n", p=128)
    w_out = C.dr["even_w_out"][j].rearrange("(kc p) n -> p kc n", p=128)

    cfA, hcfA, hbaA, hbxA = dv(j, 0, 0, KC), dv(j, 1, 0, KC), dv(j, 2, 0, KC), dv(j, 3, 0, KC)
    ACT(P, cfA, pv("lam", j, 0, KC), AF.Exp, scale=-1.0)
    TS(P, "dve", cfA, cfA, 1.0, ALU.add)
    ACT(P, cfA, cfA, AF.Ln)
    TS(P, "dve", cfA, cfA, -8.0, ALU.mult)
    TS(P, "dve", hcfA, cfA, 0.5, ALU.mult)
    TS(P, "dve", hbaA, pv("ba", j, 0, KC), 0.5, ALU.mult)
    TS(P, "dve", hbxA, pv("bx", j, 0, KC), 0.5, ALU.mult)

    GK = math.sqrt(2.0 / math.pi)
    items = [(u, hf) for u in range(12) for hf in range(2)]
    W = {}

    def stageP(i):
        u, hf = items[i]
        a = A[i % 2]
        if hf == 0:
            if u < 4:
                W[u] = C.wslot([(0, [KC, 128], w_in[:, :, u * 128:(u + 1) * 128]),
                                (1024, [128], C.dr["pool_w"][j, u])])
            else:
                h = u - 4
                W[u] = C.wslot([
                    (0, [KC, 128], w_in[:, :, 512 + h * 128:512 + (h + 1) * 128]),
                    (1024, [KC, 128], w_in[:, :, 1536 + h * 128:1536 + (h + 1) * 128]),
                    (2048, [128], C.dr["lru_w_a"][j, h]),
                    (2176, [128], C.dr["lru_w_x"][j, h])])
            MSET(P, "dve", a[:, 0:16], 0.0)
        else:
            CP(P, "dve", a[:, 0:16], A[(i - 1) % 2][:, HW:HW + 16])
        Wu = W[u][0]
        for q in range(2):
            blk = slice(hf * HW + q * 512, hf * HW + (q + 1) * 512)
            ps = PS.get()
            for k in range(KC):
                MM(P, ps[:], Wu[:, k, :], C.xT[:, k, blk], k == 0, k == KC - 1)
            ACT(P, a[:, 16 + q * 512:16 + (q + 1) * 512], ps[:], AF.Copy)
            PS.put(ps)
        if u >= 4:
            Wg = W[u][1]
            for q in range(2):
                blk = slice(hf * HW + q * 512, hf * HW + (q + 1) * 512)
                ps = PS.get()
                for k in range(KC):
                    MM(P, ps[:], Wg[:, k, :], C.xT[:, k, blk], k == 0, k == KC - 1)
                ACT(P, Bg[i % 2][:, q * 512:(q + 1) * 512], ps[:], AF.Copy)
                PS.put(ps)

    def stageE(i):
        u, hf = items[i]
        a = A[i % 2]
        ui = u % 4
        cbi = cb[i % 2]
        if u < 4:
            g = u
            Wp = W[u][1]
            w = 2 << g
            src, sh, lo, ei = a, 1, 0, 0
            exts = [E0, E1]
            while sh < w:
                dst = exts[ei % 2]; ei += 1
                lo2 = lo + sh
                TT(P, "dve", dst[:, lo2:16 + HW], src[:, lo2:16 + HW], src[:, lo2 - sh:16 + HW - sh], ALU.add)
                src, lo, sh = dst, lo2, sh * 2
            STT(P, T[:], src[:, 16:16 + HW], 1.0 / w, a[:, 16:16 + HW], ALU.mult, ALU.subtract)
            if hf == 0:
                TT(P, "dve", sm[:], src[:, 16:32], C.invc[:, g * 16:(g + 1) * 16], ALU.mult)
                TT(P, "dve", T[:, 0:16], sm[:], a[:, 16:32], ALU.subtract)
            ACT(P, cbi[:], T[:], AF.Copy)
            for q in range(2):
                blk = slice(hf * HW + q * 512, hf * HW + (q + 1) * 512)
                ps = PS.get()
                MM(P, ps[:], Wp[:], cbi[:, q * 512:(q + 1) * 512], True, True)
                ACT(P, mixT[:, ui, blk], ps[:], AF.Copy, scale=pv("pscale", j, g, 1))
                PS.put(ps)
        else:
            h = u - 4
            Wa, Wx = W[u][2], W[u][3]
            bg = Bg[i % 2]
            cw = lambda k: pv("convw", j, k * KC + h, 1)
            TS(P, "dve", C2[:], a[:, 16:16 + HW], cw(3), ALU.mult, pv("convb", j, h, 1), ALU.add)
            for k in (2, 1, 0):
                STT(P, C2[:], a[:, 13 + k:13 + k + HW], cw(k), C2[:], ALU.mult, ALU.add)
            ACT(P, cbi[:], C2[:], AF.Copy)
            for q in range(2):
                ps = PS.get()
                MM(P, ps[:], Wa[:], cbi[:, q * 512:(q + 1) * 512], True, True)
                ACT(P, R[:, q * 512:(q + 1) * 512], ps[:], AF.Tanh, scale=0.5, bias=dv(j, 2, h, 1))
                PS.put(ps)
                ps = PS.get()
                MM(P, ps[:], Wx[:], cbi[:, q * 512:(q + 1) * 512], True, True)
                ACT(P, I_[:, q * 512:(q + 1) * 512], ps[:], AF.Tanh, scale=0.5, bias=dv(j, 3, h, 1))
                PS.put(ps)
            ACT(P, T[:], R[:], AF.Exp, scale=dv(j, 0, h, 1), bias=dv(j, 0, h, 1))
            ACT(P, R[:], R[:], AF.Exp, scale=dv(j, 1, h, 1), bias=dv(j, 1, h, 1))
            ACT(P, T[:], T[:], AF.Ln, scale=-1.0, bias=C.onec[:, 0:1])
            ACT(P, T[:], T[:], AF.Exp, scale=0.5)
            STT(P, I_[:], I_[:], 1.0, T[:], ALU.add, ALU.mult)
            STT(P, I_[:], I_[:], 0.5, C2[:], ALU.mult, ALU.mult)
            Hb = T
            if hf == 0:
                P.op("dve", lambda e, o=Hb[:], a_=R[:], b_=I_[:]: e.tensor_tensor_scan(
                    out=o.ap, data0=a_.ap, data1=b_.ap, initial=0.0, op0=ALU.mult, op1=ALU.add),
                    reads=[R[:], I_[:]], writes=[Hb[:]])
            else:
                init = carry[:, h:h + 1]
                P.op("dve", lambda e, o=Hb[:], a_=R[:], b_=I_[:], i0=init: e.tensor_tensor_scan(
                    out=o.ap, data0=a_.ap, data1=b_.ap, initial=i0.ap, op0=ALU.mult, op1=ALU.add),
                    reads=[R[:], I_[:], init], writes=[Hb[:]])
            CP(P, "dve", carry[:, h:h + 1], Hb[:, HW - 1:HW])
            Cg = C2
            ACT(P, Cg[:], bg[:], AF.Square)
            TS(P, "dve", Cg[:], Cg[:], 0.044715, ALU.mult, 1.0, ALU.add)
            TT(P, "dve", Cg[:], Cg[:], bg[:], ALU.mult)
            ACT(P, Cg[:], Cg[:], AF.Tanh, scale=GK)
            STT(P, bg[:], Cg[:], 1.0, bg[:], ALU.add, ALU.mult)
            STT(P, mixT[:, ui, hf * HW:(hf + 1) * HW], bg[:], 0.5, Hb[:], ALU.mult, ALU.mult)
        if ui == 3 and hf == 1:
            grp = u // 4
            (Wo,) = C.wslot([(0, [4, D], w_out[:, grp * 4:(grp + 1) * 4, :])])
            order = [(mo, nb) for mo in range(KC) for nb in range(NB)] if grp < 2 else \
                    [(mo, nb) for nb in range(NB) for mo in range(KC)]
            for mo, nb in order:
                if True:
                    blk = slice(nb * 512, (nb + 1) * 512)
                    ps = PS.get()
                    for k in range(4):
                        MM(P, ps[:], Wo[:, k, mo * 128:(mo + 1) * 128], mixT[:, k, blk], k == 0, k == 3)
                    if grp == 0:
                        STT(P, C.xres[:, mo, blk], C.xres[:, mo, blk], ALPHA, ps[:], ALU.mult, ALU.add)
                    else:
                        TT(P, "dve", C.xres[:, mo, blk], C.xres[:, mo, blk], ps[:], ALU.add)
                    PS.put(ps)

    stageP(0)
    for i in range(len(items)):
        if i + 1 < len(items):
            stageP(i + 1)
        stageE(i)


def emit_odd(C, l):
    P, PS = C.P, C.PS
    j = l // 2
    pv = C.pvf
    cv = Carver(C.scr)
    cqn = cv.arr(BF16, [128, 3, S])
    ckvn = cv.arr(BF16, [128, 2, S])
    kpe = cv.arr(BF16, [128, S])
    qn = cv.arr(BF16, [128, S])
    qpe = cv.arr(BF16, [128, S])
    MSET(P, "dve", kpe[64:128, :], 0.0)
    MSET(P, "dve", qpe[64:128, :], 0.0)
    kn = cv.arr(BF16, [128, S])
    V = cv.arr(BF16, [128, 16, 128])
    E = [cv.arr(BF16, [128, 512]) for _ in range(3)]
    sq = cv.arr(BF16, [128, 3, 512])
    rs = [cv.arr(F32, [128, 512]) for _ in range(2)]
    r1 = [cv.arr(F32, [64, 512]) for _ in range(1)]
    r2 = [cv.arr(F32, [64, 512]) for _ in range(1)]
    oT = Arr(C.xT.buf, 0, BF16, [128, KC, S])
    wd = C.dr["mla_w_down"][j].rearrange("(kc p) n -> p kc n", p=128)
    wds = C.dr["w_down_sw"][j].rearrange("(kc p) n -> p kc n", p=128)
    wqb = C.dr["mla_w_qb"][j].rearrange("(kc p) n -> p kc n", p=128)
    wqs = C.dr["w_qb_sw"][j].rearrange("(kc p) n -> p kc n", p=128)
    wkv = C.dr["mla_w_kvb"][j].rearrange("(kc p) n -> p kc n", p=128)
    wo = C.dr["mla_w_o"][j].rearrange("(kc p) n -> p kc n", p=128)
    SCALE = 192.0 ** -0.5

    (Wd1,) = C.wslot([(0, [KC, 384], wd[:, :, 0:384])])
    Wd2, Wd3 = C.wslot([(0, [KC, 320], wd[:, :, 384:704]), (2560, [KC, 64], wds)])

    def rope(dst, pa, pb, blk, ri):
        TT(P, "dve", r1[ri][:], pa, C.cos2[:, blk], ALU.mult)
        TT(P, "dve", r2[ri][:], pb, C.sins[:, blk], ALU.mult)
        TT(P, "dve", dst, r1[ri][:], r2[ri][:], ALU.add)

    def rmsn(dst, W, ncol, nch, gname, blk, ri):
        banks = []
        for m in range(nch):
            ps = PS.get()
            for k in range(KC):
                MM(P, ps[:], W[:, k, m * 128:(m + 1) * 128], C.xT[:, k, blk], k == 0, k == KC - 1)
            ACT(P, sq[:, m, :], ps[:], AF.Square)
            banks.append(ps)
        s2 = PS.get()
        for m in range(nch):
            MM(P, s2[:], C.ones[:], sq[:, m, :], m == 0, m == nch - 1)
        ve = rs[ri]
        TS(P, "dve", ve[:], s2[:], 1.0 / ncol, ALU.mult, RMS_EPS, ALU.add)
        PS.put(s2)
        ACT(P, ve[:], ve[:], AF.Ln)
        ACT(P, ve[:], ve[:], AF.Exp, scale=-0.5)
        for m in range(nch):
            STT(P, dst[:, m, blk], banks[m][:], pv(gname, j, m, 1), ve[:], ALU.mult, ALU.mult)
            PS.put(banks[m])

    for nb in range(NB):
        blk = slice(nb * 512, (nb + 1) * 512)
        rmsn(cqn, Wd1, 384, 3, "qg", blk, 0)
        rmsn(ckvn, Wd2, 256, 2, "kvg", blk, 1)
        pa, pb = PS.get(), PS.get()
        for k in range(KC):
            MM(P, pa[0:64, :], Wd2[:, k, 256:320], C.xT[:, k, blk], k == 0, k == KC - 1)
        for k in range(KC):
            MM(P, pb[0:64, :], Wd3[:, k, :], C.xT[:, k, blk], k == 0, k == KC - 1)
        rope(kpe[0:64, blk], pa[0:64, :], pb[0:64, :], blk, 0)
        PS.put(pa); PS.put(pb)

    st_e = {"ei": 0}
    for h in range(8):
        Wq, Wqs, Wkv = C.wslot([(0, [3, 192], wqb[:, :, h * 192:(h + 1) * 192]),
                                (576, [3, 64], wqs[:, :, h * 64:(h + 1) * 64]),
                                (768, [2, 256], wkv[0:128, :, h * 256:(h + 1) * 256])])
        for nb in range(NB):
            blk = slice(nb * 512, (nb + 1) * 512)
            ps = PS.get()
            for k in range(3):
                MM(P, ps[:], Wq[:, k, 0:128], cqn[:, k, blk], k == 0, k == 2)
            ACT(P, qn[:, blk], ps[:], AF.Copy)
            PS.put(ps)
            pa, pb = PS.get(), PS.get()
            for k in range(3):
                MM(P, pa[0:64, :], Wq[:, k, 128:192], cqn[:, k, blk], k == 0, k == 2)
            for k in range(3):
                MM(P, pb[0:64, :], Wqs[:, k, :], cqn[:, k, blk], k == 0, k == 2)
            rope(qpe[0:64, blk], pa[0:64, :], pb[0:64, :], blk, 0)
            PS.put(pa); PS.put(pb)
            ps = PS.get()
            for k in range(2):
                MM(P, ps[:], Wkv[:, k, 0:128], ckvn[:, k, blk], k == 0, k == 1)
            ACT(P, kn[:, blk], ps[:], AF.Copy)
            PS.put(ps)
        for t4 in range(4):
            ps = PS.get()
            for tt in range(4):
                t = t4 * 4 + tt
                for k in range(2):
                    MM(P, ps[:, tt * 128:(tt + 1) * 128], ckvn[:, k, t * 128:(t + 1) * 128], Wkv[:, k, 128:256],
                       k == 0, k == 1, inc=(k == 1 and tt == 3))
            ACT(P, V[:, t4 * 4:(t4 + 1) * 4, :], ps[:], AF.Copy)
            PS.put(ps)
        for qb in range(NB):
            num, den = PS.get(), PS.get()
            nkt = 4 * qb + 4
            def sc_stage(kt):
                q0 = max(512 * qb, 128 * kt)
                N = 512 * qb + 512 - q0
                c0 = q0 - 512 * qb
                kts = slice(kt * 128, (kt + 1) * 128)
                sc = PS.get()
                MM(P, sc[:, 0:N], kn[:, kts], qn[:, q0:q0 + N], True, False)
                MM(P, sc[:, 0:N], kpe[:, kts], qpe[:, q0:q0 + N], False, True)
                Eb = E[st_e["ei"] % 3]; st_e["ei"] += 1
                ACT(P, Eb[:, 0:N], sc[:, 0:N], AF.Exp, scale=SCALE)
                PS.put(sc)
                if kt >= 4 * qb:
                    MSET(P, "dve", Eb[64:128, 0:64], 0.0)
                return Eb, N, c0

            pend = sc_stage(0)
            for kt in range(nkt):
                nxt = sc_stage(kt + 1) if kt + 1 < nkt else None
                Eb, N, c0 = pend
                MM(P, num[:, c0:c0 + N], V[:, kt, :], Eb[:, 0:N], kt == 0, kt == nkt - 1, inc=False)
                MM(P, den[:, c0:c0 + N], C.ones[:], Eb[:, 0:N], kt == 0, kt == nkt - 1, inc=True)
                pend = nxt
            rc = rs[qb % 2]
            P.op("dve", lambda e, o=rc[:], i=den[:]: e.reciprocal(out=o.ap, in_=i.ap), reads=[den[:]], writes=[rc[:]])
            TT(P, "dve", oT[:, h, qb * 512:(qb + 1) * 512], num[:], rc[:], ALU.mult)
            PS.put(num); PS.put(den)

    Wo1 = C.wslot([(0, [4, D], wo[:, 0:4, :])])[0]
    Wo2 = C.wslot([(0, [4, D], wo[:, 4:8, :])])[0]
    for nb in range(NB):
        for mo in range(KC):
            blk = slice(nb * 512, (nb + 1) * 512)
            ps = PS.get()
            for k in range(KC):
                W = Wo1 if k < 4 else Wo2
                MM(P, ps[:], W[:, k % 4, mo * 128:(mo + 1) * 128], oT[:, k, blk], k == 0, k == KC - 1)
            STT(P, C.xres[:, mo, blk], C.xres[:, mo, blk], ALPHA, ps[:], ALU.mult, ALU.add)
            PS.put(ps)


_NC_CACHE = {}


def _host_prep(inp):
    f = np.float32
    pvec = np.zeros((128, PV_N), f)

    def put(key, vec):
        v = np.asarray(vec, f)
        n = v.shape[0] // 128
        o = PV_COLS[key]
        pvec[:, o:o + n] = v.reshape(n, 128).T

    for l in range(DEPTH):
        put(("lmg", l), inp["ln_mix_g"][l]); put(("lmb", l), inp["ln_mix_b"][l])
        put(("lfg", l), inp["ln_ffn_g"][l]); put(("lfb", l), inp["ln_ffn_b"][l])
    for j in range(2):
        put(("pscale", j), inp["pool_scale"][j])
        cw = np.asarray(inp["lru_conv_w"][j], f)
        o = PV_COLS[("convw", j)]
        for k in range(4):
            pvec[:, o + k * KC:o + (k + 1) * KC] = cw[k].reshape(KC, 128).T
        put(("convb", j), inp["lru_conv_b"][j]); put(("ba", j), inp["lru_b_a"][j])
        put(("bx", j), inp["lru_b_x"][j]); put(("lam", j), inp["lru_lambda"][j])
        put(("qg", j), inp["mla_q_norm_g"][j]); put(("kvg", j), inp["mla_kv_norm_g"][j])
    inv_freq = (10000.0 ** (-np.arange(0, 64, 2, dtype=f) / f(64))).astype(f)
    invf2 = np.concatenate([inv_freq, inv_freq]).reshape(64, 1).astype(f)
    invc = np.zeros((128, 64), f)
    for g, w in enumerate((2, 4, 8, 16)):
        invc[:, g * 16:(g + 1) * 16] = (1.0 / np.minimum(np.arange(16) + 1, w)).astype(f)[None, :]
    wd = np.asarray(inp["mla_w_down"], f)
    w_down_sw = np.ascontiguousarray(np.concatenate([wd[:, :, 672:704], wd[:, :, 640:672]], axis=2))
    wq = np.asarray(inp["mla_w_qb"], f).reshape(2, 384, 8, 192)
    w_qb_sw = np.ascontiguousarray(np.concatenate([wq[..., 160:192], wq[..., 128:160]], axis=3).reshape(2, 384, 512))
    shared = {"pvec": pvec, "invf2": invf2, "invc": invc, "w_down_sw": w_down_sw, "w_qb_sw": w_qb_sw}
    for nm in ("even_w_in", "pool_w", "lru_w_a", "lru_w_x", "even_w_out", "mla_w_down", "mla_w_qb",
               "mla_w_kvb", "mla_w_o", "mlp_w1", "mlp_w2"):
        shared[nm] = np.ascontiguousarray(np.asarray(inp[nm], f))
    return shared


LAUNCH_GROUPS = [(0, 1, 2, 3)]


def kernel(**inp):
    shared = _host_prep(inp)
    x = np.asarray(inp["x"], np.float32)
    pos = np.asarray(inp["positions"], np.int32)
    cur = [np.ascontiguousarray(x[b].T) for b in range(NCORES)]
    for grp in LAUNCH_GROUPS:
        if grp not in _NC_CACHE:
            _NC_CACHE[grp] = build(grp)
        nc = _NC_CACHE[grp]
        in_maps = []
        for b in range(NCORES):
            m = dict(shared)
            m["xT"] = cur[b]
            m["pos"] = np.ascontiguousarray(pos[b][None, :])
            in_maps.append(m)
        res = run_bass_kernel_spmd(nc, in_maps, core_ids=list(range(NCORES)))
        cur = [np.ascontiguousarray(res.results[b]["yT"]) for b in range(NCORES)]
    return np.stack([c.T for c in cur], axis=0).astype(np.float32)
```

```python
import math
import numpy as np
import concourse.bass as bass
import concourse.mybir as mybir
from concourse.bass_utils import run_bass_kernel_spmd

F32 = mybir.dt.float32
BF16 = mybir.dt.bfloat16
I32 = mybir.dt.int32
AF = mybir.ActivationFunctionType
ALU = mybir.AluOpType

NCORES = 8
S = 2048
D = 1024
DEPTH = 4
KC = D // 128
NB = S // 512
ALPHA = (2 * DEPTH) ** 0.25
LN_EPS = 1e-5
RMS_EPS = 1e-6


class View:
    __slots__ = ("buf", "ap", "ivs")

    def __init__(self, buf, ap, ivs):
        self.buf, self.ap, self.ivs = buf, ap, ivs


def _merge(ivs):
    ivs = sorted(ivs)
    out = [list(ivs[0])]
    for lo, hi in ivs[1:]:
        if lo <= out[-1][1]:
            out[-1][1] = max(out[-1][1], hi)
        else:
            out.append([lo, hi])
    return tuple((a, b) for a, b in out)


def _overlap(a, b):
    for lo, hi in a:
        for lo2, hi2 in b:
            if lo < hi2 and lo2 < hi:
                return True
    return False


def _covers(a, b):
    for lo2, hi2 in b:
        ok = False
        for lo, hi in a:
            if lo <= lo2 and hi2 <= hi:
                ok = True
                break
        if not ok:
            return False
    return True


class Buf:
    def __init__(self, prog, name, shape, dtype, space="sbuf"):
        self.prog, self.name, self.shape, self.dtype = prog, name, list(shape), dtype
        self.esz = 2 if dtype == BF16 else 4
        nc = prog.nc
        if space == "sbuf":
            self.t = nc.alloc_sbuf_tensor(name, self.shape, dtype)
        else:
            self.t = nc.alloc_psum_tensor(name, self.shape, dtype)
        st = []
        acc = 1
        for n in reversed(self.shape[1:]):
            st.append(acc)
            acc *= n
        self.strides = list(reversed(st))
        self.hist = []

    def __getitem__(self, idx):
        if not isinstance(idx, tuple):
            idx = (idx,)
        ap = self.t[idx]
        fidx = list(idx[1:]) + [slice(None)] * (len(self.shape) - len(idx))
        rngs = []
        for i, n in zip(fidx, self.shape[1:]):
            if isinstance(i, int):
                rngs.append((i, i + 1))
            else:
                a, b, _ = i.indices(n)
                rngs.append((a, b))
        outer = rngs[:-1]
        cnt = 1
        for a, b in outer:
            cnt *= (b - a)
        la, lb = rngs[-1]
        if cnt > 64:
            lo = sum(a * s for (a, b), s in zip(rngs, self.strides))
            hi = sum((b - 1) * s for (a, b), s in zip(rngs, self.strides)) + 1
            ivs = ((lo * self.esz, hi * self.esz),)
        else:
            starts = [0]
            for (a, b), s in zip(outer, self.strides[:-1]):
                starts = [o + j * s for o in starts for j in range(a, b)]
            ivs = _merge([((o + la) * self.esz, (o + lb) * self.esz) for o in starts])
        return View(self, ap, ivs)


class Prog:
    ENGS = ("pe", "act", "dve", "pool", "sp")

    def __init__(self, nc, n_dma_sems=24):
        self.nc = nc
        self.streams = {e: [] for e in self.ENGS}
        self.sems = {}
        self.count = {}
        self.waited = {e: {} for e in self.ENGS}
        for e in self.ENGS:
            self.sems[e] = nc.alloc_semaphore(name="s_" + e)
            self.count[e] = 0
        self.dma_sems = []
        for i in range(n_dma_sems):
            nm = "q%d" % i
            self.sems[nm] = nc.alloc_semaphore(name="s_" + nm)
            self.count[nm] = 0
            self.dma_sems.append(nm)
        self.dma_rr = 0
        self.n_ops = 0

    def _deps(self, eng, reads, writes):
        need = {}
        for v in reads:
            for rec in v.buf.hist:
                if rec[2] and _overlap(rec[3], v.ivs):
                    if not (eng == "pe" and rec[0] == "pe"):
                        if need.get(rec[0], 0) < rec[1]:
                            need[rec[0]] = rec[1]
        for v in writes:
            for rec in v.buf.hist:
                if _overlap(rec[3], v.ivs):
                    if not (eng == "pe" and rec[0] == "pe"):
                        if need.get(rec[0], 0) < rec[1]:
                            need[rec[0]] = rec[1]
        return need

    def _record(self, who, cnt, reads, writes):
        for v in reads:
            h = v.buf.hist
            for rec in h:
                if (not rec[2]) and rec[0] == who and rec[3] == v.ivs:
                    rec[1] = max(rec[1], cnt)
                    break
            else:
                h.append([who, cnt, False, v.ivs])
        for v in writes:
            h = v.buf.hist
            h[:] = [rec for rec in h if not _covers(v.ivs, rec[3])]
            h.append([who, cnt, True, v.ivs])

    def _waits(self, eng, need):
        w = []
        for f, c in need.items():
            if self.waited[eng].get(f, 0) < c:
                self.waited[eng][f] = c
                w.append((f, c))
        return w

    def op(self, eng, fn, reads=(), writes=(), inc=True):
        need = self._deps(eng, reads, writes)
        waits = self._waits(eng, need)
        cnt = self.count[eng] + 1
        if inc:
            self.count[eng] = cnt
        self._record(eng, cnt, reads, writes)
        self.streams[eng].append((waits, fn, (eng, 1) if inc else None))
        self.n_ops += 1

    def dma(self, queue, out, in_, out_view=None, in_view=None, **kw):
        reads = [in_view] if in_view is not None else []
        writes = [out_view] if out_view is not None else []
        need = self._deps(queue, reads, writes)
        q = self.dma_sems[self.dma_rr % len(self.dma_sems)]
        self.dma_rr += 1
        if self.count[q] > 0:
            need[q] = max(need.get(q, 0), self.count[q])
        waits = self._waits(queue, need)
        self.count[q] += 16
        cnt = self.count[q]
        self._record(q, cnt, reads, writes)
        self.streams[queue].append((waits, lambda e: e.dma_start(out=out, in_=in_, **kw), (q, 16)))
        self.n_ops += 1
        return (q, cnt)

    def wait_on(self, eng, who, cnt):
        w = self._waits(eng, {who: cnt})
        if w:
            self.streams[eng].append((w, None, None))

    def emit(self):
        nc = self.nc
        engobj = {"pe": "tensor", "act": "scalar", "dve": "vector", "pool": "gpsimd", "sp": "sync"}
        with nc.Block() as block:
            for e in self.ENGS:
                stream = self.streams[e]

                def body(eng, stream=stream):
                    for waits, fn, inc in stream:
                        for (f, c) in waits:
                            eng.wait_ge(self.sems[f], c)
                        if fn is None:
                            continue
                        ins = fn(eng)
                        if inc is not None:
                            ins.then_inc(self.sems[inc[0]], inc[1])

                getattr(block, engobj[e])(body)


class Arr:
    def __init__(self, buf, byte_off, dtype, shape):
        self.buf, self.off, self.dtype, self.shape = buf, byte_off, dtype, list(shape)
        self.esz = 2 if dtype == BF16 else 4
        n = 1
        for d in shape[1:]:
            n *= d
        assert byte_off % 4 == 0 and (n * self.esz) % 4 == 0
        lo, hi = byte_off // 4, (byte_off + n * self.esz) // 4
        assert hi <= buf.shape[1], (buf.name, hi, buf.shape)
        ap = buf.t[0:shape[0], lo:hi]
        if dtype != F32:
            ap = ap.bitcast(dtype)
        fd = shape[1:]
        if len(fd) == 2:
            ap = ap.rearrange("p (a b) -> p a b", a=fd[0])
        elif len(fd) == 3:
            ap = ap.rearrange("p (a b c) -> p a b c", a=fd[0], b=fd[1])
        self.full = ap
        st, acc = [], 1
        for d in reversed(fd):
            st.append(acc)
            acc *= d
        self.strides = list(reversed(st))
        self.nbytes = n * self.esz

    def __getitem__(self, idx):
        if not isinstance(idx, tuple):
            idx = (idx,)
        ap = self.full[idx]
        fidx = list(idx[1:]) + [slice(None)] * (len(self.shape) - len(idx))
        rngs = []
        for i, n in zip(fidx, self.shape[1:]):
            if isinstance(i, int):
                rngs.append((i, i + 1))
            else:
                a, b, _ = i.indices(n)
                rngs.append((a, b))
        outer = rngs[:-1]
        cnt = 1
        for a, b in outer:
            cnt *= (b - a)
        la, lb = rngs[-1]
        if cnt > 64:
            lo = sum(a * s for (a, b), s in zip(rngs, self.strides))
            hi = sum((b - 1) * s for (a, b), s in zip(rngs, self.strides)) + 1
            ivs = ((self.off + lo * self.esz, self.off + hi * self.esz),)
        else:
            starts = [0]
            for (a, b), s in zip(outer, self.strides[:-1]):
                starts = [o + j * s for o in starts for j in range(a, b)]
            ivs = _merge([(self.off + (o + la) * self.esz, self.off + (o + lb) * self.esz) for o in starts])
        return View(self.buf, ap, ivs)


class Carver:
    def __init__(self, buf):
        self.buf, self.pos = buf, 0

    def arr(self, dtype, shape):
        a = Arr(self.buf, self.pos, dtype, shape)
        self.pos += (a.nbytes + 31) // 32 * 32
        assert self.pos <= self.buf.shape[1] * 4, (self.buf.name, self.pos)
        return a


class PsumPool:
    def __init__(self, prog):
        self.banks = [Buf(prog, "psb%d" % i, [128, 512], F32, space="psum") for i in range(8)]
        self.free = list(range(8))

    def get(self):
        assert self.free, "out of PSUM banks"
        return self.banks[self.free.pop(0)]

    def put(self, b):
        self.free.append(self.banks.index(b))


def MM(P, out, lhsT, rhs, start, stop, inc=None):
    if inc is None:
        inc = stop
    P.op("pe", lambda e: e.matmul(out.ap, lhsT=lhsT.ap, rhs=rhs.ap, start=start, stop=stop),
         reads=[lhsT, rhs], writes=[out], inc=inc)


def ACT(P, out, in_, func, scale=1.0, bias=None):
    reads = [in_]
    kw = {}
    if isinstance(scale, View):
        reads.append(scale)
        kw["scale"] = scale.ap
    else:
        kw["scale"] = float(scale)
    if isinstance(bias, View):
        reads.append(bias)
        kw["bias"] = bias.ap
    elif bias is not None:
        kw["bias"] = float(bias)
    P.op("act", lambda e: e.activation(out=out.ap, in_=in_.ap, func=func, **kw), reads=reads, writes=[out])


def TS(P, eng, out, in0, s1, op0, s2=None, op1=None):
    reads = [in0]
    a1 = s1
    if isinstance(s1, View):
        reads.append(s1)
        a1 = s1.ap
    a2 = s2
    if isinstance(s2, View):
        reads.append(s2)
        a2 = s2.ap
    if op1 is None:
        P.op(eng, lambda e: e.tensor_scalar(out=out.ap, in0=in0.ap, scalar1=a1, scalar2=None, op0=op0),
             reads=reads, writes=[out])
    else:
        P.op(eng, lambda e: e.tensor_scalar(out=out.ap, in0=in0.ap, scalar1=a1, scalar2=a2, op0=op0, op1=op1),
             reads=reads, writes=[out])


def TT(P, eng, out, in0, in1, op):
    P.op(eng, lambda e: e.tensor_tensor(out=out.ap, in0=in0.ap, in1=in1.ap, op=op), reads=[in0, in1], writes=[out])


def STT(P, out, in0, sc, in1, op0, op1):
    reads = [in0, in1]
    a = sc
    if isinstance(sc, View):
        reads.append(sc)
        a = sc.ap
    P.op("dve", lambda e: e.scalar_tensor_tensor(out=out.ap, in0=in0.ap, scalar=a, in1=in1.ap, op0=op0, op1=op1),
         reads=reads, writes=[out])


def CP(P, eng, out, in_):
    P.op(eng, lambda e: e.tensor_copy(out=out.ap, in_=in_.ap), reads=[in_], writes=[out])


def MSET(P, eng, out, val):
    P.op(eng, lambda e: e.memset(out.ap, val), writes=[out])


def _pvec_layout():
    cols = {}
    n = 0
    for l in range(DEPTH):
        for nm in ("lmg", "lmb", "lfg", "lfb"):
            cols[(nm, l)] = n
            n += KC
    for j in range(2):
        cols[("pscale", j)] = n; n += 4
        cols[("convw", j)] = n; n += 4 * KC
        for nm in ("convb", "ba", "bx", "lam"):
            cols[(nm, j)] = n; n += KC
        cols[("qg", j)] = n; n += 3
        cols[("kvg", j)] = n; n += 2
    return cols, n


PV_COLS, PV_N = _pvec_layout()
DV_N = 2 * 4 * KC


class Ctx:
    pass


def build(layers=(0, 1, 2, 3)):
    nc = bass.Bass("TRN2", target_bir_lowering=False)
    P = Prog(nc)
    C = Ctx()
    C.P, C.nc = P, nc
    dr = {}

    def din(name, shape, dt=F32):
        dr[name] = nc.dram_tensor(name, list(shape), dt, kind="ExternalInput").ap()
        return dr[name]

    din("xT", [D, S]); din("pos", [1, S], I32); din("pvec", [128, PV_N]); din("invf2", [64, 1]); din("invc", [128, 64])
    din("even_w_in", [2, D, 2560]); din("pool_w", [2, 4, 128, 128]); din("lru_w_a", [2, 8, 128, 128])
    din("lru_w_x", [2, 8, 128, 128]); din("even_w_out", [2, 1536, D])
    din("mla_w_down", [2, D, 704]); din("w_down_sw", [2, D, 64]); din("mla_w_qb", [2, 384, 1536])
    din("w_qb_sw", [2, 384, 512]); din("mla_w_kvb", [2, 256, 2048]); din("mla_w_o", [2, D, D])
    din("mlp_w1", [DEPTH, D, 4096]); din("mlp_w2", [DEPTH, 4096, D])
    yT = nc.dram_tensor("yT", [D, S], F32, kind="ExternalOutput").ap()
    C.dr = dr

    xres_b = Buf(P, "xres", [128, KC * S], F32)
    xt_b = Buf(P, "xtb", [128, KC * S // 2], F32)
    wr_b = [Buf(P, "wring%d" % i, [128, 2048], F32) for i in range(4)]
    rope_b = Buf(P, "rope", [128, 2 * S], F32)
    cst_b = Buf(P, "cst", [128, PV_N + DV_N + 64 + 8 + 64 + 512 + 512 + 32], F32)
    scr_b = Buf(P, "scr", [128, 14080], F32)
    C.xres = Arr(xres_b, 0, F32, [128, KC, S])
    C.xT = Arr(xt_b, 0, BF16, [128, KC, S])
    cc = Carver(cst_b)
    C.pv = cc.arr(F32, [128, PV_N + DV_N])
    C.invc = cc.arr(F32, [128, 64])
    C.invf2 = cc.arr(F32, [64, 1])
    C.ones = cc.arr(BF16, [128, 128])
    C.halfc = cc.arr(F32, [128, 512])
    C.onec = cc.arr(F32, [128, 8])
    C.nhalfc = cc.arr(F32, [128, 512])
    C.cos2 = Arr(rope_b, 0, F32, [64, S])
    C.sins = Arr(rope_b, S * 4, F32, [64, S])
    C.scr = scr_b
    C.PS = PsumPool(P)
    C.wr_b = wr_b
    C.wi = 0

    def pv(nm, idx, c0=0, n=1, p=128):
        o = PV_COLS[(nm, idx)] + c0
        return C.pv[0:p, o:o + n]

    def dv(j, which, c0=0, n=1):
        o = PV_N + j * 4 * KC + which * KC + c0
        return C.pv[:, o:o + n]

    C.pvf, C.dvf = pv, dv

    def wslot(loads):
        buf = wr_b[C.wi % len(wr_b)]
        C.wi += 1
        arrs = []
        for off, shp, src in loads:
            a = Arr(buf, off * 2, BF16, [128] + list(shp))
            v = a[:]
            P.dma("pool", v.ap, src, out_view=v)
            arrs.append(a)
        return arrs

    C.wslot = wslot

    v = C.pv[:, 0:PV_N]
    P.dma("sp", v.ap, dr["pvec"], out_view=v)
    v = C.invc[:]
    P.dma("sp", v.ap, dr["invc"], out_view=v)
    v = C.invf2[:]
    P.dma("sp", v.ap, dr["invf2"], out_view=v)
    xsrc = dr["xT"].rearrange("(c p) s -> p c s", p=128)
    for c in range(KC):
        v = C.xres[:, c, :]
        P.dma("sp", v.ap, xsrc[:, c, :], out_view=v)
    for hf in range(2):
        for c in range(KC):
            v = C.xT[:, c, hf * 1024:(hf + 1) * 1024]
            P.dma("pool", v.ap, xsrc[:, c, hf * 1024:(hf + 1) * 1024], out_view=v)
    MSET(P, "dve", C.ones[:], 1.0)
    MSET(P, "dve", C.halfc[:], 0.5)
    MSET(P, "dve", C.onec[:], 1.0)
    MSET(P, "dve", C.nhalfc[:], -0.5)

    if any(l % 2 == 1 for l in layers):
        emit_rope(C)
    for l in layers:
        if l % 2 == 0:
            emit_even(C, l)
        else:
            emit_odd(C, l)
        emit_ln(C, lambda k, l=l: pv("lmg", l, k, 1), lambda k, l=l: pv("lmb", l, k, 1))
        emit_mlp(C, l)
        emit_ln(C, lambda k, l=l: pv("lfg", l, k, 1), lambda k, l=l: pv("lfb", l, k, 1))

    ysrc = yT.rearrange("(c p) s -> p c s", p=128)
    for c in range(KC):
        v = C.xres[:, c, :]
        P.dma("sp", ysrc[:, c, :], v.ap, in_view=v)
    for q in P.dma_sems:
        if P.count[q] > 0:
            P.wait_on("sp", q, P.count[q])
    P.emit()
    return nc


def emit_rope(C):
    P = C.P
    cv = Carver(C.scr)
    pi_ = cv.arr(I32, [64, S])
    ang = cv.arr(F32, [64, S])
    kk = cv.arr(F32, [64, S])
    y = cv.arr(F32, [64, S])
    acc = cv.arr(F32, [64, S])
    sn = cv.arr(F32, [64, S])
    v = pi_[:]
    P.dma("sp", v.ap, C.dr["pos"].partition_broadcast(64), out_view=v)
    CP(P, "dve", ang[:], pi_[:])
    TS(P, "dve", ang[:], ang[:], C.invf2[:, 0:1], ALU.mult)
    MAGIC = 12582912.0
    TS(P, "dve", kk[:], ang[:], 1.0 / (2 * math.pi), ALU.mult, MAGIC, ALU.add)
    TS(P, "dve", kk[:], kk[:], MAGIC, ALU.subtract)
    c1 = 6.28125
    c2 = float(np.float32(2 * math.pi - c1))
    c3 = float(2 * math.pi - c1 - c2)
    for c in (c1, c2, c3):
        STT(P, ang[:], kk[:], -c, ang[:], ALU.mult, ALU.add)
    TS(P, "dve", ang[:], ang[:], 0.5, ALU.mult)
    TT(P, "dve", y[:], ang[:], ang[:], ALU.mult)
    sc_ = [-1.0 / 6, 1.0 / 120, -1.0 / 5040, 1.0 / 362880, -1.0 / 39916800, 1.0 / 6227020800]
    TS(P, "dve", acc[:], y[:], sc_[5], ALU.mult)
    for k in (4, 3, 2, 1, 0):
        STT(P, acc[:], acc[:], sc_[k], y[:], ALU.add, ALU.mult)
    STT(P, sn[:], acc[:], 1.0, ang[:], ALU.add, ALU.mult)
    cc_ = [-0.5, 1.0 / 24, -1.0 / 720, 1.0 / 40320, -1.0 / 3628800, 1.0 / 479001600, -1.0 / 87178291200]
    TS(P, "dve", acc[:], y[:], cc_[6], ALU.mult)
    for k in (5, 4, 3, 2, 1, 0):
        STT(P, acc[:], acc[:], cc_[k], y[:], ALU.add, ALU.mult)
    TS(P, "dve", acc[:], acc[:], 1.0, ALU.add)
    STT(P, C.sins[:], sn[:], 2.0, acc[:], ALU.mult, ALU.mult)
    TT(P, "dve", y[:], sn[:], sn[:], ALU.mult)
    TS(P, "dve", C.cos2[:], y[:], -2.0, ALU.mult, 1.0, ALU.add)
    TS(P, "dve", C.sins[0:32, :], C.sins[0:32, :], -1.0, ALU.mult)


def emit_ln(C, g, b):
    P, PS = C.P, C.PS
    cv = Carver(C.scr)
    sq = [cv.arr(BF16, [128, KC, 512]) for _ in range(2)]
    mean = [cv.arr(F32, [128, 512]) for _ in range(2)]
    ve = [cv.arr(F32, [128, 512]) for _ in range(2)]
    m2 = [cv.arr(F32, [128, 512]) for _ in range(2)]
    tt_ = [cv.arr(F32, [128, 512]) for _ in range(3)]

    def head(nb):
        i = nb % 2
        blk = slice(nb * 512, (nb + 1) * 512)
        CP(P, "dve", C.xT[:, :, blk], C.xres[:, :, blk])
        ACT(P, sq[i][:], C.xres[:, :, blk], AF.Square)
        s1, s2 = PS.get(), PS.get()
        for k in range(KC):
            MM(P, s1[:], C.ones[:], C.xT[:, k, blk], k == 0, k == KC - 1)
        for k in range(KC):
            MM(P, s2[:], C.ones[:], sq[i][:, k, :], k == 0, k == KC - 1)
        TS(P, "dve", mean[i][:], s1[:], 1.0 / D, ALU.mult)
        TS(P, "dve", ve[i][:], s2[:], 1.0 / D, ALU.mult, LN_EPS, ALU.add)
        PS.put(s1); PS.put(s2)
        TT(P, "dve", m2[i][:], mean[i][:], mean[i][:], ALU.mult)
        TT(P, "dve", ve[i][:], ve[i][:], m2[i][:], ALU.subtract)
        ACT(P, ve[i][:], ve[i][:], AF.Ln)
        ACT(P, ve[i][:], ve[i][:], AF.Exp, scale=-0.5)

    def body(nb):
        i = nb % 2
        blk = slice(nb * 512, (nb + 1) * 512)
        for k in range(KC):
            t = tt_[k % 3]
            TT(P, "dve", t[:], C.xres[:, k, blk], mean[i][:], ALU.subtract)
            TT(P, "pool", t[:], t[:], ve[i][:], ALU.mult)
            ACT(P, C.xres[:, k, blk], t[:], AF.Identity, scale=g(k), bias=b(k))
            ACT(P, C.xT[:, k, blk], t[:], AF.Identity, scale=g(k), bias=b(k))

    head(0)
    for nb in range(NB):
        if nb + 1 < NB:
            head(nb + 1)
        body(nb)


def emit_mlp(C, l):
    P, PS = C.P, C.PS
    cv = Carver(C.scr)
    hb = [cv.arr(BF16, [128, 4, S]) for _ in range(2)]
    tmp = [cv.arr(F32, [128, 512]) for _ in range(3)]
    w1 = C.dr["mlp_w1"][l].rearrange("(kc p) n -> p kc n", p=128)
    w2 = C.dr["mlp_w2"][l].rearrange("(kc p) n -> p kc n", p=128)
    st = {"ti": 0}

    def mlp1(c):
        (W1,) = C.wslot([(0, [KC, 512], w1[:, :, c * 512:(c + 1) * 512])])
        h = hb[c % 2]
        order = [(m, nb) for m in range(4) for nb in range(NB)] if c > 0 else \
                [(m, nb) for nb in range(NB) for m in range(4)]
        for m, nb in order:
            if True:
                blk = slice(nb * 512, (nb + 1) * 512)
                ps = PS.get()
                for k in range(KC):
                    MM(P, ps[:], W1[:, k, m * 128:(m + 1) * 128], C.xT[:, k, blk], k == 0, k == KC - 1)
                t = tmp[st["ti"] % 3]; st["ti"] += 1
                ACT(P, t[:], ps[:], AF.Relu)
                PS.put(ps)
                TT(P, "pool", h[:, m, blk], t[:], t[:], ALU.mult)

    def mlp2(c):
        (W2,) = C.wslot([(0, [4, D], w2[:, c * 4:(c + 1) * 4, :])])
        h = hb[c % 2]
        order = [(mo, nb) for mo in range(KC) for nb in range(NB)] if c < 7 else \
                [(mo, nb) for nb in range(NB) for mo in range(KC)]
        for mo, nb in order:
            if True:
                blk = slice(nb * 512, (nb + 1) * 512)
                ps = PS.get()
                for k in range(4):
                    MM(P, ps[:], W2[:, k, mo * 128:(mo + 1) * 128], h[:, k, blk], k == 0, k == 3)
                if c == 0:
                    STT(P, C.xres[:, mo, blk], C.xres[:, mo, blk], ALPHA, ps[:], ALU.mult, ALU.add)
                else:
                    TT(P, "dve", C.xres[:, mo, blk], C.xres[:, mo, blk], ps[:], ALU.add)
                PS.put(ps)

    mlp1(0)
    for c in range(8):
        if c + 1 < 8:
            mlp1(c + 1)
        mlp2(c)


def emit_even(C, l):
    P, PS = C.P, C.PS
    j = l // 2
    pv, dv = C.pvf, C.dvf
    cv = Carver(C.scr)
    mixT = cv.arr(BF16, [128, 4, S])
    HW = 1024
    A = [cv.arr(F32, [128, 16 + HW]) for _ in range(2)]
    Bg = [cv.arr(F32, [128, HW]) for _ in range(2)]
    E0 = cv.arr(F32, [128, 16 + HW])
    E1 = cv.arr(F32, [128, 16 + HW])
    C2 = Arr(C.scr, E0.off, F32, [128, HW])
    R = Arr(C.scr, E1.off, F32, [128, HW])
    I_ = cv.arr(F32, [128, HW])
    T = cv.arr(F32, [128, HW])
    cb = [cv.arr(BF16, [128, HW]) for _ in range(2)]
    carry = cv.arr(F32, [128, 8])
    sm = cv.arr(F32, [128, 16])
    w_in = C.dr["even_w_in"][j].rearrange("(kc p) n -> p kc n", p=128)
    w_out = C.dr["even_w_out"][j].rearrange("(kc p) n -> p kc n", p=128)

    cfA, hcfA, hbaA, hbxA = dv(j, 0, 0, KC), dv(j, 1, 0, KC), dv(j, 2, 0, KC), dv(j, 3, 0, KC)
    ACT(P, cfA, pv("lam", j, 0, KC), AF.Exp, scale=-1.0)
    TS(P, "dve", cfA, cfA, 1.0, ALU.add)
    ACT(P, cfA, cfA, AF.Ln)
    TS(P, "dve", cfA, cfA, -8.0, ALU.mult)
    TS(P, "dve", hcfA, cfA, 0.5, ALU.mult)
    TS(P, "dve", hbaA, pv("ba", j, 0, KC), 0.5, ALU.mult)
    TS(P, "dve", hbxA, pv("bx", j, 0, KC), 0.5, ALU.mult)

    GK = math.sqrt(2.0 / math.pi)
    items = [(u, hf) for u in range(12) for hf in range(2)]
    W = {}

    def stageP(i):
        u, hf = items[i]
        a = A[i % 2]
        if hf == 0:
            if u < 4:
                W[u] = C.wslot([(0, [KC, 128], w_in[:, :, u * 128:(u + 1) * 128]),
                                (1024, [128], C.dr["pool_w"][j, u])])
            else:
                h = u - 4
                W[u] = C.wslot([
                    (0, [KC, 128], w_in[:, :, 512 + h * 128:512 + (h + 1) * 128]),
                    (1024, [KC, 128], w_in[:, :, 1536 + h * 128:1536 + (h + 1) * 128]),
                    (2048, [128], C.dr["lru_w_a"][j, h]),
                    (2176, [128], C.dr["lru_w_x"][j, h])])
            MSET(P, "dve", a[:, 0:16], 0.0)
        else:
            CP(P, "dve", a[:, 0:16], A[(i - 1) % 2][:, HW:HW + 16])
        Wu = W[u][0]
        for q in range(2):
            blk = slice(hf * HW + q * 512, hf * HW + (q + 1) * 512)
            ps = PS.get()
            for k in range(KC):
                MM(P, ps[:], Wu[:, k, :], C.xT[:, k, blk], k == 0, k == KC - 1)
            ACT(P, a[:, 16 + q * 512:16 + (q + 1) * 512], ps[:], AF.Copy)
            PS.put(ps)
        if u >= 4:
            Wg = W[u][1]
            for q in range(2):
                blk = slice(hf * HW + q * 512, hf * HW + (q + 1) * 512)
                ps = PS.get()
                for k in range(KC):
                    MM(P, ps[:], Wg[:, k, :], C.xT[:, k, blk], k == 0, k == KC - 1)
                ACT(P, Bg[i % 2][:, q * 512:(q + 1) * 512], ps[:], AF.Copy)
                PS.put(ps)

    def stageE(i):
        u, hf = items[i]
        a = A[i % 2]
        ui = u % 4
        cbi = cb[i % 2]
        if u < 4:
            g = u
            Wp = W[u][1]
            w = 2 << g
            src, sh, lo, ei = a, 1, 0, 0
            exts = [E0, E1]
            while sh < w:
                dst = exts[ei % 2]; ei += 1
                lo2 = lo + sh
                TT(P, "dve", dst[:, lo2:16 + HW], src[:, lo2:16 + HW], src[:, lo2 - sh:16 + HW - sh], ALU.add)
                src, lo, sh = dst, lo2, sh * 2
            STT(P, T[:], src[:, 16:16 + HW], 1.0 / w, a[:, 16:16 + HW], ALU.mult, ALU.subtract)
            if hf == 0:
                TT(P, "dve", sm[:], src[:, 16:32], C.invc[:, g * 16:(g + 1) * 16], ALU.mult)
                TT(P, "dve", T[:, 0:16], sm[:], a[:, 16:32], ALU.subtract)
            ACT(P, cbi[:], T[:], AF.Copy)
            for q in range(2):
                blk = slice(hf * HW + q * 512, hf * HW + (q + 1) * 512)
                ps = PS.get()
                MM(P, ps[:], Wp[:], cbi[:, q * 512:(q + 1) * 512], True, True)
                ACT(P, mixT[:, ui, blk], ps[:], AF.Copy, scale=pv("pscale", j, g, 1))
                PS.put(ps)
        else:
            h = u - 4
            Wa, Wx = W[u][2], W[u][3]
            bg = Bg[i % 2]
            cw = lambda k: pv("convw", j, k * KC + h, 1)
            TS(P, "dve", C2[:], a[:, 16:16 + HW], cw(3), ALU.mult, pv("convb", j, h, 1), ALU.add)
            for k in (2, 1, 0):
                STT(P, C2[:], a[:, 13 + k:13 + k + HW], cw(k), C2[:], ALU.mult, ALU.add)
            ACT(P, cbi[:], C2[:], AF.Copy)
            for q in range(2):
                ps = PS.get()
                MM(P, ps[:], Wa[:], cbi[:, q * 512:(q + 1) * 512], True, True)
                ACT(P, R[:, q * 512:(q + 1) * 512], ps[:], AF.Tanh, scale=0.5, bias=dv(j, 2, h, 1))
                PS.put(ps)
                ps = PS.get()
                MM(P, ps[:], Wx[:], cbi[:, q * 512:(q + 1) * 512], True, True)
                ACT(P, I_[:, q * 512:(q + 1) * 512], ps[:], AF.Tanh, scale=0.5, bias=dv(j, 3, h, 1))
                PS.put(ps)
            ACT(P, T[:], R[:], AF.Exp, scale=dv(j, 0, h, 1), bias=dv(j, 0, h, 1))
            ACT(P, R[:], R[:], AF.Exp, scale=dv(j, 1, h, 1), bias=dv(j, 1, h, 1))
            ACT(P, T[:], T[:], AF.Ln, scale=-1.0, bias=C.onec[:, 0:1])
            ACT(P, T[:], T[:], AF.Exp, scale=0.5)
            STT(P, I_[:], I_[:], 1.0, T[:], ALU.add, ALU.mult)
            STT(P, I_[:], I_[:], 0.5, C2[:], ALU.mult, ALU.mult)
            Hb = T
            if hf == 0:
                P.op("dve", lambda e, o=Hb[:], a_=R[:], b_=I_[:]: e.tensor_tensor_scan(
                    out=o.ap, data0=a_.ap, data1=b_.ap, initial=0.0, op0=ALU.mult, op1=ALU.add),
                    reads=[R[:], I_[:]], writes=[Hb[:]])
            else:
                init = carry[:, h:h + 1]
                P.op("dve", lambda e, o=Hb[:], a_=R[:], b_=I_[:], i0=init: e.tensor_tensor_scan(
                    out=o.ap, data0=a_.ap, data1=b_.ap, initial=i0.ap, op0=ALU.mult, op1=ALU.add),
                    reads=[R[:], I_[:], init], writes=[Hb[:]])
            CP(P, "dve", carry[:, h:h + 1], Hb[:, HW - 1:HW])
            Cg = C2
            ACT(P, Cg[:], bg[:], AF.Square)
            TS(P, "dve", Cg[:], Cg[:], 0.044715, ALU.mult, 1.0, ALU.add)
            TT(P, "dve", Cg[:], Cg[:], bg[:], ALU.mult)
            ACT(P, Cg[:], Cg[:], AF.Tanh, scale=GK)
            STT(P, bg[:], Cg[:], 1.0, bg[:], ALU.add, ALU.mult)
            STT(P, mixT[:, ui, hf * HW:(hf + 1) * HW], bg[:], 0.5, Hb[:], ALU.mult, ALU.mult)
        if ui == 3 and hf == 1:
            grp = u // 4
            (Wo,) = C.wslot([(0, [4, D], w_out[:, grp * 4:(grp + 1) * 4, :])])
            order = [(mo, nb) for mo in range(KC) for nb in range(NB)] if grp < 2 else \
                    [(mo, nb) for nb in range(NB) for mo in range(KC)]
            for mo, nb in order:
                if True:
                    blk = slice(nb * 512, (nb + 1) * 512)
                    ps = PS.get()
                    for k in range(4):
                        MM(P, ps[:], Wo[:, k, mo * 128:(mo + 1) * 128], mixT[:, k, blk], k == 0, k == 3)
                    if grp == 0:
                        STT(P, C.xres[:, mo, blk], C.xres[:, mo, blk], ALPHA, ps[:], ALU.mult, ALU.add)
                    else:
                        TT(P, "dve", C.xres[:, mo, blk], C.xres[:, mo, blk], ps[:], ALU.add)
                    PS.put(ps)

    stageP(0)
    for i in range(len(items)):
        if i + 1 < len(items):
            stageP(i + 1)
        stageE(i)


def emit_odd(C, l):
    P, PS = C.P, C.PS
    j = l // 2
    pv = C.pvf
    cv = Carver(C.scr)
    cqn = cv.arr(BF16, [128, 3, S])
    ckvn = cv.arr(BF16, [128, 2, S])
    kpe = cv.arr(BF16, [128, S])
    qn = cv.arr(BF16, [128, S])
    qpe = cv.arr(BF16, [128, S])
    MSET(P, "dve", kpe[64:128, :], 0.0)
    MSET(P, "dve", qpe[64:128, :], 0.0)
    kn = cv.arr(BF16, [128, S])
    V = cv.arr(BF16, [128, 16, 128])
    E = [cv.arr(BF16, [128, 512]) for _ in range(3)]
    sq = cv.arr(BF16, [128, 3, 512])
    rs = [cv.arr(F32, [128, 512]) for _ in range(2)]
    r1 = [cv.arr(F32, [64, 512]) for _ in range(1)]
    r2 = [cv.arr(F32, [64, 512]) for _ in range(1)]
    oT = Arr(C.xT.buf, 0, BF16, [128, KC, S])
    wd = C.dr["mla_w_down"][j].rearrange("(kc p) n -> p kc n", p=128)
    wds = C.dr["w_down_sw"][j].rearrange("(kc p) n -> p kc n", p=128)
    wqb = C.dr["mla_w_qb"][j].rearrange("(kc p) n -> p kc n", p=128)
    wqs = C.dr["w_qb_sw"][j].rearrange("(kc p) n -> p kc n", p=128)
    wkv = C.dr["mla_w_kvb"][j].rearrange("(kc p) n -> p kc n", p=128)
    wo = C.dr["mla_w_o"][j].rearrange("(kc p) n -> p kc n", p=128)
    SCALE = 192.0 ** -0.5

    (Wd1,) = C.wslot([(0, [KC, 384], wd[:, :, 0:384])])
    Wd2, Wd3 = C.wslot([(0, [KC, 320], wd[:, :, 384:704]), (2560, [KC, 64], wds)])

    def rope(dst, pa, pb, blk, ri):
        TT(P, "dve", r1[ri][:], pa, C.cos2[:, blk], ALU.mult)
        TT(P, "dve", r2[ri][:], pb, C.sins[:, blk], ALU.mult)
        TT(P, "dve", dst, r1[ri][:], r2[ri][:], ALU.add)

    def rmsn(dst, W, ncol, nch, gname, blk, ri):
        banks = []
        for m in range(nch):
            ps = PS.get()
            for k in range(KC):
                MM(P, ps[:], W[:, k, m * 128:(m + 1) * 128], C.xT[:, k, blk], k == 0, k == KC - 1)
            ACT(P, sq[:, m, :], ps[:], AF.Square)
            banks.append(ps)
        s2 = PS.get()
        for m in range(nch):
            MM(P, s2[:], C.ones[:], sq[:, m, :], m == 0, m == nch - 1)
        ve = rs[ri]
        TS(P, "dve", ve[:], s2[:], 1.0 / ncol, ALU.mult, RMS_EPS, ALU.add)
        PS.put(s2)
        ACT(P, ve[:], ve[:], AF.Ln)
        ACT(P, ve[:], ve[:], AF.Exp, scale=-0.5)
        for m in range(nch):
            STT(P, dst[:, m, blk], banks[m][:], pv(gname, j, m, 1), ve[:], ALU.mult, ALU.mult)
            PS.put(banks[m])

    for nb in range(NB):
        blk = slice(nb * 512, (nb + 1) * 512)
        rmsn(cqn, Wd1, 384, 3, "qg", blk, 0)
        rmsn(ckvn, Wd2, 256, 2, "kvg", blk, 1)
        pa, pb = PS.get(), PS.get()
        for k in range(KC):
            MM(P, pa[0:64, :], Wd2[:, k, 256:320], C.xT[:, k, blk], k == 0, k == KC - 1)
        for k in range(KC):
            MM(P, pb[0:64, :], Wd3[:, k, :], C.xT[:, k, blk], k == 0, k == KC - 1)
        rope(kpe[0:64, blk], pa[0:64, :], pb[0:64, :], blk, 0)
        PS.put(pa); PS.put(pb)

    st_e = {"ei": 0}
    for h in range(8):
        Wq, Wqs, Wkv = C.wslot([(0, [3, 192], wqb[:, :, h * 192:(h + 1) * 192]),
                                (576, [3, 64], wqs[:, :, h * 64:(h + 1) * 64]),
                                (768, [2, 256], wkv[0:128, :, h * 256:(h + 1) * 256])])
        for nb in range(NB):
            blk = slice(nb * 512, (nb + 1) * 512)
            ps = PS.get()
            for k in range(3):
                MM(P, ps[:], Wq[:, k, 0:128], cqn[:, k, blk], k == 0, k == 2)
            ACT(P, qn[:, blk], ps[:], AF.Copy)
            PS.put(ps)
            pa, pb = PS.get(), PS.get()
            for k in range(3):
                MM(P, pa[0:64, :], Wq[:, k, 128:192], cqn[:, k, blk], k == 0, k == 2)
            for k in range(3):
                MM(P, pb[0:64, :], Wqs[:, k, :], cqn[:, k, blk], k == 0, k == 2)
            rope(qpe[0:64, blk], pa[0:64, :], pb[0:64, :], blk, 0)
            PS.put(pa); PS.put(pb)
            ps = PS.get()
            for k in range(2):
                MM(P, ps[:], Wkv[:, k, 0:128], ckvn[:, k, blk], k == 0, k == 1)
            ACT(P, kn[:, blk], ps[:], AF.Copy)
            PS.put(ps)
        for t4 in range(4):
            ps = PS.get()
            for tt in range(4):
                t = t4 * 4 + tt
                for k in range(2):
                    MM(P, ps[:, tt * 128:(tt + 1) * 128], ckvn[:, k, t * 128:(t + 1) * 128], Wkv[:, k, 128:256],
                       k == 0, k == 1, inc=(k == 1 and tt == 3))
            ACT(P, V[:, t4 * 4:(t4 + 1) * 4, :], ps[:], AF.Copy)
            PS.put(ps)
        for qb in range(NB):
            num, den = PS.get(), PS.get()
            nkt = 4 * qb + 4
            def sc_stage(kt):
                q0 = max(512 * qb, 128 * kt)
                N = 512 * qb + 512 - q0
                c0 = q0 - 512 * qb
                kts = slice(kt * 128, (kt + 1) * 128)
                sc = PS.get()
                MM(P, sc[:, 0:N], kn[:, kts], qn[:, q0:q0 + N], True, False)
                MM(P, sc[:, 0:N], kpe[:, kts], qpe[:, q0:q0 + N], False, True)
                Eb = E[st_e["ei"] % 3]; st_e["ei"] += 1
                ACT(P, Eb[:, 0:N], sc[:, 0:N], AF.Exp, scale=SCALE)
                PS.put(sc)
                if kt >= 4 * qb:
                    MSET(P, "dve", Eb[64:128, 0:64], 0.0)
                return Eb, N, c0

            pend = sc_stage(0)
            for kt in range(nkt):
                nxt = sc_stage(kt + 1) if kt + 1 < nkt else None
                Eb, N, c0 = pend
                MM(P, num[:, c0:c0 + N], V[:, kt, :], Eb[:, 0:N], kt == 0, kt == nkt - 1, inc=False)
                MM(P, den[:, c0:c0 + N], C.ones[:], Eb[:, 0:N], kt == 0, kt == nkt - 1, inc=True)
                pend = nxt
            rc = rs[qb % 2]
            P.op("dve", lambda e, o=rc[:], i=den[:]: e.reciprocal(out=o.ap, in_=i.ap), reads=[den[:]], writes=[rc[:]])
            TT(P, "dve", oT[:, h, qb * 512:(qb + 1) * 512], num[:], rc[:], ALU.mult)
            PS.put(num); PS.put(den)

    Wo1 = C.wslot([(0, [4, D], wo[:, 0:4, :])])[0]
    Wo2 = C.wslot([(0, [4, D], wo[:, 4:8, :])])[0]
    for nb in range(NB):
        for mo in range(KC):
            blk = slice(nb * 512, (nb + 1) * 512)
            ps = PS.get()
            for k in range(KC):
                W = Wo1 if k < 4 else Wo2
                MM(P, ps[:], W[:, k % 4, mo * 128:(mo + 1) * 128], oT[:, k, blk], k == 0, k == KC - 1)
            STT(P, C.xres[:, mo, blk], C.xres[:, mo, blk], ALPHA, ps[:], ALU.mult, ALU.add)
            PS.put(ps)


_NC_CACHE = {}


def _host_prep(inp):
    f = np.float32
    pvec = np.zeros((128, PV_N), f)

    def put(key, vec):
        v = np.asarray(vec, f)
        n = v.shape[0] // 128
        o = PV_COLS[key]
        pvec[:, o:o + n] = v.reshape(n, 128).T

    for l in range(DEPTH):
        put(("lmg", l), inp["ln_mix_g"][l]); put(("lmb", l), inp["ln_mix_b"][l])
        put(("lfg", l), inp["ln_ffn_g"][l]); put(("lfb", l), inp["ln_ffn_b"][l])
    for j in range(2):
        put(("pscale", j), inp["pool_scale"][j])
        cw = np.asarray(inp["lru_conv_w"][j], f)
        o = PV_COLS[("convw", j)]
        for k in range(4):
            pvec[:, o + k * KC:o + (k + 1) * KC] = cw[k].reshape(KC, 128).T
        put(("convb", j), inp["lru_conv_b"][j]); put(("ba", j), inp["lru_b_a"][j])
        put(("bx", j), inp["lru_b_x"][j]); put(("lam", j), inp["lru_lambda"][j])
        put(("qg", j), inp["mla_q_norm_g"][j]); put(("kvg", j), inp["mla_kv_norm_g"][j])
    inv_freq = (10000.0 ** (-np.arange(0, 64, 2, dtype=f) / f(64))).astype(f)
    invf2 = np.concatenate([inv_freq, inv_freq]).reshape(64, 1).astype(f)
    invc = np.zeros((128, 64), f)
    for g, w in enumerate((2, 4, 8, 16)):
        invc[:, g * 16:(g + 1) * 16] = (1.0 / np.minimum(np.arange(16) + 1, w)).astype(f)[None, :]
    wd = np.asarray(inp["mla_w_down"], f)
    w_down_sw = np.ascontiguousarray(np.concatenate([wd[:, :, 672:704], wd[:, :, 640:672]], axis=2))
    wq = np.asarray(inp["mla_w_qb"], f).reshape(2, 384, 8, 192)
    w_qb_sw = np.ascontiguousarray(np.concatenate([wq[..., 160:192], wq[..., 128:160]], axis=3).reshape(2, 384, 512))
    shared = {"pvec": pvec, "invf2": invf2, "invc": invc, "w_down_sw": w_down_sw, "w_qb_sw": w_qb_sw}
    for nm in ("even_w_in", "pool_w", "lru_w_a", "lru_w_x", "even_w_out", "mla_w_down", "mla_w_qb",
               "mla_w_kvb", "mla_w_o", "mlp_w1", "mlp_w2"):
        shared[nm] = np.ascontiguousarray(np.asarray(inp[nm], f))
    return shared


LAUNCH_GROUPS = [(0, 1, 2, 3)]


def kernel(**inp):
    shared = _host_prep(inp)
    x = np.asarray(inp["x"], np.float32)
    pos = np.asarray(inp["positions"], np.int32)
    cur = [np.ascontiguousarray(x[b].T) for b in range(NCORES)]
    for grp in LAUNCH_GROUPS:
        if grp not in _NC_CACHE:
            _NC_CACHE[grp] = build(grp)
        nc = _NC_CACHE[grp]
        in_maps = []
        for b in range(NCORES):
            m = dict(shared)
            m["xT"] = cur[b]
            m["pos"] = np.ascontiguousarray(pos[b][None, :])
            in_maps.append(m)
        res = run_bass_kernel_spmd(nc, in_maps, core_ids=list(range(NCORES)))
        cur = [np.ascontiguousarray(res.results[b]["yT"]) for b in range(NCORES)]
    return np.stack([c.T for c in cur], axis=0).astype(np.float32)
```
